# Optimizing a Trainium2 kernel written in Bass

```python
import jax, jax.numpy as jnp
from jax import lax
import numpy as np

D_MODEL = 1024
BATCH = 4
SEQ = 4096
DEPTH = 2

GRID_W = 64
CTX_LEN = 256
HEAD_DIM = 64
NA_HEADS = 8
NA_WIDTH = NA_HEADS * HEAD_DIM
WIN_R = 8
WIN_C = 16
GQA_Q_HEADS = 8
GQA_KV_HEADS = 2
GQA_GROUP = GQA_Q_HEADS // GQA_KV_HEADS
GQA_WIDTH = GQA_Q_HEADS * HEAD_DIM
GQA_KV_WIDTH = GQA_KV_HEADS * HEAD_DIM
Q_BLOCK = 128
ROPE_THETA = 10000.0
ROT_AXIS = HEAD_DIM // 2
EPS = 1e-6
SPLIT_SIZES = (NA_WIDTH, NA_WIDTH, GQA_KV_WIDTH, GQA_KV_WIDTH,
               NA_WIDTH, GQA_WIDTH, NA_WIDTH, GQA_WIDTH, D_MODEL, D_MODEL)
KV_COLS = 2 * NA_WIDTH + 2 * GQA_KV_WIDTH
IN_COLS = sum(SPLIT_SIZES)

kernel_name = "hybrid_na_gqa_prefix_dit"


def _split(p, sizes):
    outs, off = [], 0
    for s in sizes:
        outs.append(p[..., off:off + s])
        off += s
    return outs


def _rmsnorm(x, g):
    xf = x.astype(jnp.float32)
    y = xf * lax.rsqrt(jnp.mean(xf * xf, axis=-1, keepdims=True) + EPS)
    return (y * g.astype(jnp.float32)).astype(x.dtype)


def _rope_axis(x, pos):
    half = ROT_AXIS // 2
    inv = 1.0 / (ROPE_THETA ** (jnp.arange(half, dtype=jnp.float32) / half))
    ang = pos.astype(jnp.float32)[:, None] * inv[None, :]
    cos = jnp.cos(ang)[None, :, None, :].astype(x.dtype)
    sin = jnp.sin(ang)[None, :, None, :].astype(x.dtype)
    x1, x2 = x[..., :half], x[..., half:]
    return jnp.concatenate([x1 * cos - x2 * sin, x2 * cos + x1 * sin], axis=-1)


def _rope_2d(x, pos_row, pos_col):
    return jnp.concatenate([_rope_axis(x[..., :ROT_AXIS], pos_row),
                            _rope_axis(x[..., ROT_AXIS:], pos_col)], axis=-1)


def _attend(q, k, v):
    s = jnp.einsum('bqkgd,btkd->bkgqt', q, k).astype(jnp.float32) * (HEAD_DIM ** -0.5)
    p = jax.nn.softmax(s, axis=-1).astype(v.dtype)
    return jnp.einsum('bkgqt,btkd->bqkgd', p, v)


def _na_latent(q, k, v, kc, vc, rpb, rows):
    B = q.shape[0]
    wr = min(WIN_R, rows)
    qg = q.reshape(B, rows, GRID_W, NA_HEADS, HEAD_DIM)
    kg = k.reshape(B, rows, GRID_W, NA_HEADS, HEAD_DIM)
    vg = v.reshape(B, rows, GRID_W, NA_HEADS, HEAD_DIM)
    cols = np.arange(GRID_W)
    cs = np.clip(cols - WIN_C // 2, 0, GRID_W - WIN_C)
    col_idx_np = (cs[:, None] + np.arange(WIN_C)[None, :]).astype(np.int32)
    col_off_np = (col_idx_np - cols[:, None] + (WIN_C - 1)).astype(np.int32)
    rws = np.arange(rows)
    rs = np.clip(rws - wr // 2, 0, rows - wr)
    row_idx_np = (rs[:, None] + np.arange(wr)[None, :]).astype(np.int32)
    row_off_np = (row_idx_np - rws[:, None] + (WIN_R - 1)).astype(np.int32)
    col_idx = jnp.asarray(col_idx_np)
    bias_all = rpb[:, row_off_np[:, :, None, None], col_off_np[None, None, :, :]]
    bias_all = jnp.transpose(bias_all, (1, 0, 3, 2, 4)).astype(jnp.float32)
    q_rows = jnp.transpose(qg, (1, 0, 2, 3, 4))
    scale = HEAD_DIM ** -0.5

    def row_block(args):
        qr, ridx, bias = args
        kb = jnp.take(kg, ridx, axis=1)
        vb = jnp.take(vg, ridx, axis=1)
        kw = jnp.take(kb, col_idx, axis=2)
        vw = jnp.take(vb, col_idx, axis=2)
        s_win = jnp.einsum('bchd,bicjhd->bhcij', qr, kw).astype(jnp.float32) * scale + bias[None]
        s_ctx = jnp.einsum('bchd,blhd->bhcl', qr, kc).astype(jnp.float32) * scale
        s = jnp.concatenate([s_win.reshape(B, NA_HEADS, GRID_W, wr * WIN_C), s_ctx], axis=-1)
        p = jax.nn.softmax(s, axis=-1).astype(v.dtype)
        p_win = p[..., :wr * WIN_C].reshape(B, NA_HEADS, GRID_W, wr, WIN_C)
        p_ctx = p[..., wr * WIN_C:]
        return (jnp.einsum('bhcij,bicjhd->bchd', p_win, vw)
                + jnp.einsum('bhcl,blhd->bchd', p_ctx, vc))

    out = lax.map(row_block, (q_rows, jnp.asarray(row_idx_np), bias_all))
    return jnp.transpose(out, (1, 0, 2, 3, 4)).reshape(B, rows * GRID_W, NA_WIDTH)


def _gqa_latent(q, k_all, v_all):
    B, S = q.shape[0], q.shape[1]
    nb = S // Q_BLOCK
    qb = q.reshape(B, nb, Q_BLOCK, GQA_KV_HEADS, GQA_GROUP, HEAD_DIM)
    qb = jnp.transpose(qb, (1, 0, 2, 3, 4, 5))
    out = lax.map(lambda qq: _attend(qq, k_all, v_all), qb)
    return jnp.transpose(out, (1, 0, 2, 3, 4, 5)).reshape(B, S, GQA_WIDTH)


def _merge(a_att, b_att, a_z, b_z, g_a, g_b, w_o_a, w_o_b, w_out):
    o_a = (a_att * jax.nn.silu(a_z)) @ w_o_a
    o_b = (b_att * jax.nn.silu(b_z)) @ w_o_b
    merged = jax.nn.sigmoid(g_a) * o_a + jax.nn.sigmoid(g_b) * o_b
    return merged @ w_out


def _layer(x, ctx, c, c_ctx, w_ada, b_ada, norm_g, w_in, q_norm_a, k_norm_a,
           q_norm_b, k_norm_b, rpb, w_o_a, w_o_b, w_out, pos_row, pos_col, update_ctx):
    B, S, _ = x.shape
    L = ctx.shape[1]
    rows = S // GRID_W
    mod_x = jax.nn.silu(c) @ w_ada + b_ada
    shift_x, scale_x, gate_x = jnp.split(mod_x[:, None, :], 3, axis=-1)
    mod_c = jax.nn.silu(c_ctx) @ w_ada + b_ada
    shift_c, scale_c, gate_c = jnp.split(mod_c, 3)
    hx = _rmsnorm(x, norm_g) * (1.0 + scale_x) + shift_x
    hc = _rmsnorm(ctx, norm_g) * (1.0 + scale_c) + shift_c

    px = hx @ w_in
    a_k, a_v, b_k, b_v, a_q, b_q, a_z, b_z, g_a, g_b = _split(px, SPLIT_SIZES)
    pc = hc @ (w_in if update_ctx else w_in[:, :KV_COLS])
    ca_k, ca_v, cb_k, cb_v = _split(pc[..., :KV_COLS], SPLIT_SIZES[:4])

    hs = lambda t, h: t.reshape(t.shape[0], t.shape[1], h, HEAD_DIM)
    qa = _rmsnorm(hs(a_q, NA_HEADS), q_norm_a)
    ka = _rmsnorm(hs(a_k, NA_HEADS), k_norm_a)
    va = hs(a_v, NA_HEADS)
    cka = _rmsnorm(hs(ca_k, NA_HEADS), k_norm_a)
    cva = hs(ca_v, NA_HEADS)
    a_att = _na_latent(qa, ka, va, cka, cva, rpb, rows)

    qb = _rope_2d(_rmsnorm(hs(b_q, GQA_Q_HEADS), q_norm_b), pos_row, pos_col)
    kb = _rope_2d(_rmsnorm(hs(b_k, GQA_KV_HEADS), k_norm_b), pos_row, pos_col)
    vb = hs(b_v, GQA_KV_HEADS)
    ckb = _rmsnorm(hs(cb_k, GQA_KV_HEADS), k_norm_b)
    cvb = hs(cb_v, GQA_KV_HEADS)
    k_all = jnp.concatenate([ckb, kb], axis=1)
    v_all = jnp.concatenate([cvb, vb], axis=1)
    b_att = _gqa_latent(qb, k_all, v_all)

    out_x = _merge(a_att, b_att, a_z, b_z, g_a, g_b, w_o_a, w_o_b, w_out)
    x_new = x + gate_x * out_x

    if update_ctx:
        _, _, _, _, c_aq, c_bq, c_az, c_bz, c_ga, c_gb = _split(pc, SPLIT_SIZES)
        cqa = _rmsnorm(hs(c_aq, NA_HEADS), q_norm_a)[:, :, :, None, :]
        c_a_att = _attend(cqa, cka, cva).reshape(B, L, NA_WIDTH)
        cqb = _rmsnorm(hs(c_bq, GQA_Q_HEADS), q_norm_b).reshape(
            B, L, GQA_KV_HEADS, GQA_GROUP, HEAD_DIM)
        c_b_att = _attend(cqb, ckb, cvb).reshape(B, L, GQA_WIDTH)
        out_c = _merge(c_a_att, c_b_att, c_az, c_bz, c_ga, c_gb, w_o_a, w_o_b, w_out)
        ctx = ctx + gate_c * out_c
    return x_new, ctx


def setup_inputs(seed: int = 0) -> dict:
    key = jax.random.key(seed)
    ks = jax.random.split(key, 16)
    f32 = jnp.float32
    nrm = lambda k, shape, s: jax.random.normal(k, shape, f32) * s
    return {
        "x": nrm(ks[0], (BATCH, SEQ, D_MODEL), 1.0),
        "c": nrm(ks[1], (BATCH, D_MODEL), 1.0),
        "ctx": nrm(ks[2], (BATCH, CTX_LEN, D_MODEL), 1.0),
        "c_ctx": nrm(ks[3], (D_MODEL,), 1.0),
        "w_ada": nrm(ks[4], (DEPTH, D_MODEL, 3 * D_MODEL), D_MODEL ** -0.5),
        "b_ada": nrm(ks[5], (DEPTH, 3 * D_MODEL), 0.02),
        "norm_g": 1.0 + nrm(ks[6], (DEPTH, D_MODEL), 0.05),
        "w_in": nrm(ks[7], (DEPTH, D_MODEL, IN_COLS), D_MODEL ** -0.5),
        "q_norm_a": 1.0 + nrm(ks[8], (DEPTH, HEAD_DIM), 0.05),
        "k_norm_a": 1.0 + nrm(ks[9], (DEPTH, HEAD_DIM), 0.05),
        "q_norm_b": 1.0 + nrm(ks[10], (DEPTH, HEAD_DIM), 0.05),
        "k_norm_b": 1.0 + nrm(ks[11], (DEPTH, HEAD_DIM), 0.05),
        "rpb": nrm(ks[12], (DEPTH, NA_HEADS, 2 * WIN_R - 1, 2 * WIN_C - 1), 0.1),
        "w_o_a": nrm(ks[13], (DEPTH, NA_WIDTH, D_MODEL), NA_WIDTH ** -0.5),
        "w_o_b": nrm(ks[14], (DEPTH, GQA_WIDTH, D_MODEL), GQA_WIDTH ** -0.5),
        "w_out": nrm(ks[15], (DEPTH, D_MODEL, D_MODEL), D_MODEL ** -0.5),
    }


def reference(x, c, ctx, c_ctx, w_ada, b_ada, norm_g, w_in, q_norm_a, k_norm_a,
              q_norm_b, k_norm_b, rpb, w_o_a, w_o_b, w_out):
    S = x.shape[1]
    t = jnp.arange(S, dtype=jnp.int32)
    pos_row = t // GRID_W
    pos_col = t % GRID_W
    for l in range(DEPTH):
        x, ctx = _layer(x, ctx, c, c_ctx, w_ada[l], b_ada[l], norm_g[l], w_in[l],
                        q_norm_a[l], k_norm_a[l], q_norm_b[l], k_norm_b[l], rpb[l],
                        w_o_a[l], w_o_b[l], w_out[l], pos_row, pos_col,
                        l < DEPTH - 1)
    return x
```

```python
import numpy as np
import ml_dtypes
from contextlib import ExitStack

import concourse.bass as bass
import concourse.mybir as mybir
from concourse.bass_utils import run_bass_kernel_spmd

F32 = mybir.dt.float32
BF16 = mybir.dt.bfloat16
AF = mybir.ActivationFunctionType
ALU = mybir.AluOpType

D = 1024
SEQ = 4096
CTX = 256
NB = 4
HD = 64
IN_COLS = 5376
EPS = 1e-6
NEG = -30000.0
NT_OWN = 16
NKC = 34

C_AK, C_AV, C_BK, C_BV, C_AQ, C_BQ, C_AZ, C_BZ, C_GA, C_GB = (
    0, 512, 1024, 1152, 1280, 1792, 2304, 2816, 3328, 4352)


def MM(out, lhsT, rhs, start=True, stop=True):
    return lambda e: e.matmul(out, lhsT=lhsT, rhs=rhs, start=start, stop=stop)


def TR(out, in_, ident):
    return lambda e: e.transpose(out, in_, ident)


def ACT(out, in_, func, bias=0.0, scale=1.0, accum_out=None):
    if accum_out is None:
        return lambda e: e.activation(out=out, in_=in_, func=func, bias=bias, scale=scale)
    return lambda e: e.activation(out=out, in_=in_, func=func, bias=bias, scale=scale,
                                  accum_out=accum_out)


def TS(out, in0, s1, s2, op0, op1=None):
    if op1 is None:
        return lambda e: e.tensor_scalar(out=out, in0=in0, scalar1=s1, scalar2=None, op0=op0)
    return lambda e: e.tensor_scalar(out=out, in0=in0, scalar1=s1, scalar2=s2, op0=op0, op1=op1)


def TT(out, in0, in1, op):
    return lambda e: e.tensor_tensor(out=out, in0=in0, in1=in1, op=op)


def STT(out, in0, scalar, in1, op0, op1):
    return lambda e: e.scalar_tensor_tensor(out=out, in0=in0, scalar=scalar, in1=in1,
                                            op0=op0, op1=op1)


def CP(out, in_):
    return lambda e: e.tensor_copy(out=out, in_=in_)


def RCP(out, in_):
    return lambda e: e.reciprocal(out=out, in_=in_)


def MEMSET(ap, v):
    return lambda e: e.memset(ap, v)


class Prog:
    ENG = ("pe", "act", "dve", "pool", "sp")

    def __init__(self, nc, stack):
        self.nc = nc
        self.stack = stack
        self.ops = {e: [] for e in self.ENG}
        self.state = {}
        self.pend = {e: ([], []) for e in self.ENG}
        self.waited = {e: {} for e in self.ENG}
        self.dsem = {}
        self.nsem = 0
        self.esem = {}
        self.ecnt = {}
        self.pe_sems = set()
        self.new_epoch()

    def _sem(self, name):
        self.nsem += 1
        return self.stack.enter_context(self.nc.semaphore(f"{name}_{self.nsem}"))

    def new_epoch(self):
        for e in ("pe", "act", "dve", "pool"):
            assert not self.pend[e][0] and not self.pend[e][1], f"pending on {e}"
            self.esem[e] = self._sem("e" + e)
            self.ecnt[e] = 0
            if e == "pe":
                self.pe_sems.add(id(self.esem[e]))

    def _st(self, k):
        s = self.state.get(k)
        if s is None:
            s = {"w": None, "r": {}}
            self.state[k] = s
        return s

    def _wait(self, eng, tok):
        if tok is None:
            return
        if tok[0] == "PEND":
            assert tok[1] == eng, f"{eng} waiting on a pending (unsignalled) write of {tok[1]}"
            return
        sem, val = tok
        sid = id(sem)
        if eng == "pe" and sid in self.pe_sems:
            return
        if self.waited[eng].get(sid, 0) >= val:
            return
        self.waited[eng][sid] = val
        self.ops[eng].append(lambda e: e.wait_ge(sem, val))

    def _pre(self, eng, r, w):
        for k in r:
            self._wait(eng, self._st(k)["w"])
        for k in w:
            s = self._st(k)
            self._wait(eng, s["w"])
            for t in s["r"].values():
                self._wait(eng, t)

    def _post(self, tok, r, w):
        sem, val = tok
        for k in w:
            s = self._st(k)
            s["w"] = tok
            s["r"] = {}
        for k in r:
            s = self._st(k)
            s["r"][id(sem)] = tok

    def do(self, eng, fn, r=(), w=(), sig=True):
        r = list(r)
        w = list(w)
        self._pre(eng, r, w)
        if not sig:
            self.ops[eng].append(fn)
            self.pend[eng][0].extend(r)
            self.pend[eng][1].extend(w)
            for k in w:
                self._st(k)["w"] = ("PEND", eng)
            return None
        sem = self.esem[eng]
        self.ecnt[eng] += 1
        val = self.ecnt[eng]
        self.ops[eng].append(lambda e: fn(e).then_inc(sem, 1))
        tok = (sem, val)
        pr, pw = self.pend[eng]
        self._post(tok, r + pr, w + pw)
        self.pend[eng] = ([], [])
        return tok

    def dma(self, q, out, in_, r=(), w=(), key=None):
        r = list(r)
        w = list(w)
        self._pre(q, r, w)
        if key is None:
            key = w[0] if w else r[0]
        if key not in self.dsem:
            self.dsem[key] = [self._sem("d"), 0]
        ent = self.dsem[key]
        ent[1] += 16
        sem, val = ent[0], ent[1]
        self.ops[q].append(lambda e: e.dma_start(out=out, in_=in_).then_inc(sem, 16))
        tok = (sem, val)
        self._post(tok, r, w)
        return tok

    def wait_all(self, eng, keys):
        for k in keys:
            self._wait(eng, self._st(k)["w"])

    def flush(self):
        ops = self.ops
        self.ops = {e: [] for e in self.ENG}
        with self.nc.Block() as block:
            @block.tensor
            def _(e):
                for f in ops["pe"]:
                    f(e)

            @block.scalar
            def _(e):
                for f in ops["act"]:
                    f(e)

            @block.vector
            def _(e):
                for f in ops["dve"]:
                    f(e)

            @block.gpsimd
            def _(e):
                for f in ops["pool"]:
                    f(e)

            @block.sync
            def _(e):
                for f in ops["sp"]:
                    f(e)
        self.new_epoch()


class LayerIO:
    pass


_CAST_RR = [0]


def CAST(P, dst, src, r, w):
    eng = ("pool", "dve", "act")[_CAST_RR[0] % 3]
    _CAST_RR[0] += 1
    if eng == "act":
        P.do("act", ACT(dst, src, AF.Copy), r=r, w=w)
    else:
        P.do(eng, CP(dst, src), r=r, w=w)


def declare_layer_inputs(nc, sfx):
    io = LayerIO()
    dt = nc.dram_tensor
    io.w_ada = dt("w_ada" + sfx, [D, 3 * D], F32, kind="ExternalInput").ap()
    io.b_ada_fm = dt("b_ada_fm" + sfx, [128, 24], F32, kind="ExternalInput").ap()
    io.b_gate = dt("b_gate" + sfx, [1, D], F32, kind="ExternalInput").ap()
    io.norm_g = dt("norm_g" + sfx, [128, 8], F32, kind="ExternalInput").ap()
    io.w_in = dt("w_in" + sfx, [D, IN_COLS], F32, kind="ExternalInput").ap()
    io.gains = dt("gains" + sfx, [128, 4], F32, kind="ExternalInput").ap()
    io.ebias = dt("ebias" + sfx, [3, 8, 128, 640], F32, kind="ExternalInput").ap()
    io.w_o_a = dt("w_o_a" + sfx, [512, D], F32, kind="ExternalInput").ap()
    io.w_o_b = dt("w_o_b" + sfx, [512, D], F32, kind="ExternalInput").ap()
    io.w_out = dt("w_out" + sfx, [D, D], F32, kind="ExternalInput").ap()
    return io


def emit_layer(P, nc, cst, io, x_rows, ctx_src, cvec, cos_t, sin_t, x_dst, ctx_dst,
               update_ctx, dbg=None, uid="", after_group=None):
    ident, blockones, rotT, ones64, ones_f, selT = cst
    def sb(name, shape, dtype):
        return nc.sbuf_tensor("s%s_%s" % (uid, name), shape, dtype)

    def ps(name, shape, dtype):
        return nc.psum_tensor("p%s_%s" % (uid, name), shape, dtype)

    own_groups = [("o%d" % g, g * 512, 4, False) for g in range(4)]
    ctx_group = ("c", 2048, 2, True)
    qgroups = own_groups + ([ctx_group] if update_ctx else [])

    with ExitStack() as LA:
        ent = LA.enter_context
        hxT = ent(sb("hxT", [128, 8, 2304], BF16))
        gatedA = ent(sb("gatedA", [128, 4, 2304], BF16))
        gatedB = ent(sb("gatedB", [128, 4, 2304], BF16))
        modv = ent(sb("modv", [128, 16, 2], F32))
        Amod = ent(sb("Amod", [128, 8, 2], F32))
        gate_bc = ent(sb("gate_bc", [128, 2, D], F32))
        gains = ent(sb("gains", [128, 4], F32))
        ng = ent(sb("ng", [128, 8], F32))

        with ExitStack() as S0:
            e0 = S0.enter_context
            cv = e0(sb("cv", [128, 8, 2], F32))
            sc = e0(sb("sc", [128, 8, 2], F32))
            scb = e0(sb("scb", [128, 2, 8, 128], F32))
            wst2 = e0(sb("wst", [128, 2, 8, 512], F32))
            bfm = e0(sb("bfm", [128, 24], F32))
            bgr = e0(sb("bgr", [1, D], F32))
            mps = e0(ps("mps", [128, 512], F32))
            gps = e0(ps("gps", [128, 2, 512], F32))

            P.dma("sp", cv[:], cvec, w=["cv"])
            P.dma("sp", bfm[:], io.b_ada_fm, w=["bfm"])
            P.dma("sp", bgr[:], io.b_gate, w=["bgr"])
            P.dma("sp", ng[:], io.norm_g, w=["ng"])
            P.dma("sp", gains[:], io.gains, w=["gains"])
            P.do("act", ACT(sc[:], cv[:], AF.Silu), r=["cv"], w=["sc"])
            P.do("dve", TS(gains[:, 0:1], gains[:, 0:1], 0.125, None, ALU.mult), r=["gains"], w=["gains"])
            P.do("dve", TS(gains[:, 2:3], gains[:, 2:3], 0.125, None, ALU.mult), r=["gains"], w=["gains"])
            for v in range(2):
                for k in range(8):
                    P.do("dve", TS(scb[:, v, k, :], ones_f[:], sc[:, k, v:v + 1], None, ALU.mult),
                         r=["sc"], w=["scb"])
            for blk in range(6):
                wb = blk % 2
                wst = wst2[:, wb]
                wkey = ("wst", wb)
                P.dma("sp" if wb == 0 else "act", wst, io.w_ada[:, blk * 512:(blk + 1) * 512].rearrange(
                    "(k p) c -> p k c", p=128), w=[wkey])
                if blk < 4:
                    for mm in range(4):
                        m = blk * 4 + mm
                        for k in range(8):
                            P.do("pe", MM(mps[:, 0:2], wst[:, k, mm * 128:(mm + 1) * 128], sc[:, k, :],
                                          start=(k == 0), stop=(k == 7)),
                                 r=[wkey, "sc"], w=["mps"], sig=(k == 7))
                        P.do("dve", TS(modv[:, m, :], mps[:, 0:2], bfm[:, m:m + 1], None, ALU.add),
                             r=["mps", "bfm"], w=["modv"])
                else:
                    hf = blk - 4
                    for v in range(2):
                        for k in range(8):
                            P.do("pe", MM(gps[:, v, :], scb[:, v, k, :], wst[:, k, :],
                                          start=(k == 0), stop=False),
                                 r=[wkey, "scb"], w=[("gps", v)], sig=False)
                        P.do("pe", MM(gps[:, v, :], ones_f[0:1, :], bgr[0:1, hf * 512:(hf + 1) * 512],
                                      start=False, stop=True), r=["bgr"], w=[("gps", v)])
                        P.do("dve", CP(gate_bc[:, v, hf * 512:(hf + 1) * 512], gps[:, v, :]),
                             r=[("gps", v)], w=["gate_bc"])
            for v in range(2):
                P.do("dve", STT(Amod[:, :, v], modv[:, 8:16, v], 1.0, ng[:], ALU.add, ALU.mult),
                     r=["modv", "ng"], w=["Amod"])
            P.flush()

        with ExitStack() as SB:
            eb_ = SB.enter_context
            KbT = eb_(sb("KbT", [128, 2, NKC * 128], BF16))
            Vb = eb_(sb("Vb", [128, NKC, 258], BF16))
            with ExitStack() as SC:
                ec_ = SC.enter_context
                KaT = ec_(sb("KaT", [128, 4, 18 * 128], BF16))
                Va = ec_(sb("Va", [128, 18, 512], BF16))
                KaTc = ec_(sb("KaTc", [128, 4, 256], BF16))
                Vac = ec_(sb("Vac", [128, 2, 512], BF16))

                with ExitStack() as S1:
                    e1 = S1.enter_context
                    Wak = e1(sb("Wak", [128, 8, 512], BF16))
                    Wav = e1(sb("Wav", [128, 8, 512], BF16))
                    Wbk = e1(sb("Wbk", [128, 8, 2, 128], BF16))
                    Wbv = e1(sb("Wbv", [128, 8, 128], BF16))
                    stg = e1(sb("stg", [128, 1, 8, 128], F32))
                    xs = e1(sb("xs", [128, 2, D], F32))
                    def xn_v(p):
                        return gatedB[:, :, p * D:(p + 1) * D]

                    def hxo_v(p):
                        return lambda k: gatedA[:, k // 2, p * D + (k % 2) * 512:p * D + (k % 2) * 512 + 512]
                    ssq = e1(sb("ssq", [128, 4], F32))
                    lnv = e1(sb("lnv", [128, 4], F32))
                    rstd = e1(sb("rstd", [128, 4], F32))
                    cosb = e1(sb("cosb", [128, 512], F32))
                    sinb = e1(sb("sinb", [128, 512], F32))
                    raw = e1(sb("raw", [128, 1, 512], F32))
                    sq = e1(sb("sq", [128, 1, 512], BF16))
                    lnr = e1(sb("lnr", [128, 1, 512], F32))
                    rsr = lnr
                    kn = e1(sb("kn", [128, 512], BF16))
                    P._pre("dve", [], ["gAscr", "gBscr"] + [("gA", i) for i in range(5)] + [("gB", i) for i in range(5)])
                    P._pre("act", [], ["gAscr", "gBscr"] + [("gA", i) for i in range(5)] + [("gB", i) for i in range(5)])
                    tp = e1(ps("tp", [128, 2, 1024], BF16))
                    pj = e1(ps("pj", [128, 4, 512], F32))
                    aux = e1(ps("aux", [128, 2, 512], F32))

                    def load_w(dst_fn, c0, nblk, tag):
                        for bi in range(nblk):
                            s = 0
                            load_w.cnt += 1
                            P.dma("sp", stg[:, s, :, :], io.w_in[:, c0 + bi * 128:c0 + (bi + 1) * 128]
                                  .rearrange("(k p) c -> p k c", p=128), w=[("stg", s)])
                            dst_fn(bi, s)
                    load_w.cnt = 0

                    def cast_to(dst, tag):
                        def f(bi, s):
                            CAST(P, dst[:, :, bi * 128:(bi + 1) * 128], stg[:, s, :, :], [("stg", s)], [tag])
                        return f
                    load_w(cast_to(Wak, "Wak"), C_AK, 4, "Wak")
                    load_w(cast_to(Wav, "Wav"), C_AV, 4, "Wav")

                    def cast_bk(bi, s):
                        for g in range(2):
                            for half in range(2):
                                CAST(P, Wbk[:, :, g, half * 64:(half + 1) * 64],
                                     stg[:, s, :, g * 64:(g + 1) * 64], [("stg", s)], ["Wbk"])
                    load_w(cast_bk, C_BK, 1, "Wbk")
                    load_w(cast_to(Wbv, "Wbv"), C_BV, 1, "Wbv")

                    tp_slot = [0]
                    P.do("pool", MEMSET(Vb[:], 0.0), w=["Vb"])
                    for oc_ in (0, 128, 256):
                        P.do("pool", MEMSET(Vb[:, :, oc_:oc_ + 1], 1.0), w=["Vb"])

                    def qknorm(src_ps, src_key, N, gain_ap, dst, dst_key, slot, rope=None):
                        aslot = slot
                        slot = 0
                        rs_ = raw[:, slot, :N]
                        P.do("dve", CP(rs_, src_ps), r=[src_key], w=[("raw", slot)])
                        P.do("pool", TT(sq[:, slot, :N], rs_, rs_, ALU.mult), r=[("raw", slot)], w=[("sq", slot)])
                        P.do("pe", MM(aux[:, aslot, :N], blockones[:], sq[:, slot, :N]),
                             r=[("sq", slot)], w=[("aux", aslot)])
                        P.do("act", ACT(lnr[:, slot, :N], aux[:, aslot, :N], AF.Ln, bias=EPS, scale=1.0 / HD),
                             r=[("aux", aslot)], w=[("rsr", slot)])
                        P.do("act", ACT(rsr[:, slot, :N], lnr[:, slot, :N], AF.Exp, scale=-0.5),
                             r=[("rsr", slot)], w=[("rsr", slot)])
                        if rope is None:
                            P.do("dve", STT(dst, rs_, gain_ap, rsr[:, slot, :N], ALU.mult, ALU.mult),
                                 r=[("raw", slot), ("rsr", slot), "gains"], w=[dst_key])
                        else:
                            P.do("dve", STT(kn[:, :N], rs_, gain_ap, rsr[:, slot, :N], ALU.mult, ALU.mult),
                                 r=[("raw", slot), ("rsr", slot), "gains"], w=["kn"])
                            P.do("pe", MM(aux[:, aslot, :N], rotT[:], kn[:, :N]), r=["kn"], w=[("aux", aslot)])
                            t1 = lnr[:, slot, :N]
                            t2 = raw[:, slot, :N]
                            P.do("dve", TT(t1, kn[:, :N], cosb[:, :N], ALU.mult), r=["kn", "cosb"], w=[("rsr", slot)])
                            P.do("dve", TT(t2, aux[:, aslot, :N], sinb[:, :N], ALU.mult),
                                 r=[("aux", aslot), "sinb"], w=[("raw", slot)])
                            P.do("dve", TT(dst, t1, t2, ALU.add), r=[("rsr", slot), ("raw", slot)], w=[dst_key])

                    groups = [("ctx", None, 2, 0, None)]
                    for g in range(4):
                        groups.append(("own", g * 512, 4, 2 + g * 4, g * 4))
                    for g in range(4):
                        groups.append(("oth", 2048 + g * 512, 4, 18 + g * 4, 16 if g == 0 else None))

                    pjc = [0]
                    xsc = [0]

                    def pjslot():
                        s = pjc[0] % 4
                        pjc[0] += 1
                        return s

                    def ginfo(gi):
                        (kind, r0, ntile, kc0, slot0) = groups[gi]
                        p = gi % 2
                        if kind == "own":
                            hdst = lambda k, c0=r0: hxT[:, k, c0:c0 + 512]
                            hkey = ("hxT", r0 // 512)
                        elif kind == "ctx":
                            hdst = lambda k: hxT[:, k, 2048:2304]
                            hkey = ("hxT", 4)
                        else:
                            hdst = hxo_v(p)
                            hkey = ("gAscr", p)
                        return kind, r0, ntile, kc0, slot0, p, hdst, hkey

                    def stageA(gi):
                        kind, r0, ntile, kc0, slot0, p, hdst, hkey = ginfo(gi)
                        xn = xn_v(p)
                        N = ntile * 128
                        v = 1 if kind == "ctx" else 0
                        for t in range(ntile):
                            xb = xsc[0] % 2
                            xsc[0] += 1
                            if kind == "ctx":
                                xsrc, xkeys = ctx_src[t * 128:(t + 1) * 128, :], []
                            else:
                                xsrc, xkeys = x_rows(r0 + t * 128, 128)
                            P.dma("sp", xs[:, xb, :], xsrc, r=xkeys, w=[("xs", xb)], key=("xs", xb))
                            P.do("act", ACT(xn[:, t, :], xs[:, xb, :], AF.Square, accum_out=ssq[:, t:t + 1]),
                                 r=[("xs", xb)], w=[("xn", p, t), "gBscr", ("ssq", t)])
                            P.do("act", ACT(lnv[:, t:t + 1], ssq[:, t:t + 1], AF.Ln, bias=EPS, scale=1.0 / D),
                                 r=[("ssq", t)], w=[("lnv", t)])
                            P.do("act", ACT(rstd[:, t:t + 1], lnv[:, t:t + 1], AF.Exp, scale=-0.5),
                                 r=[("lnv", t)], w=[("rstd", t)])
                            P.do("dve", TS(xn[:, t, :], xs[:, xb, :], rstd[:, t:t + 1], None, ALU.mult),
                                 r=[("xs", xb), ("rstd", t)], w=[("xn", p, t), "gBscr"])
                        for k in range(8):
                            sl = tp_slot[0] % 2
                            tp_slot[0] += 1
                            tps = tp[:, sl, 0:N]
                            for t in range(ntile):
                                P.do("pe", TR(tp[:, sl, t * 128:(t + 1) * 128],
                                              xn[:, t, k * 128:(k + 1) * 128], ident[:]),
                                     r=[("xn", p, t), "gBscr"], w=[("tp", sl)], sig=(t == ntile - 1))
                            P.do("dve", TS(hdst(k), tps, Amod[:, k, v:v + 1], modv[:, k, v:v + 1], ALU.mult, ALU.add),
                                 r=[("tp", sl), "Amod", "modv"], w=[hkey, "gAscr"])

                    def stageB(gi):
                        kind, r0, ntile, kc0, slot0, p, hdst, hkey = ginfo(gi)
                        N = ntile * 128
                        if kind != "ctx":
                            P.dma("sp", cosb[:], cos_t[:, r0:r0 + 512], w=["cosb"])
                            P.dma("sp", sinb[:], sin_t[:, r0:r0 + 512], w=["sinb"])

                        def hx(k):
                            return hdst(k)

                        for g in range(2):
                            s = pjslot()
                            for k in range(8):
                                P.do("pe", MM(pj[:, s, :N], Wbk[:, k, g, :], hx(k), start=(k == 0), stop=(k == 7)),
                                     r=["Wbk", hkey], w=[("pj", s)], sig=(k == 7))
                            qknorm(pj[:, s, :N], ("pj", s), N, gains[:, 3:4],
                                   KbT[:, g, kc0 * 128:kc0 * 128 + N], "KbT", g,
                                   rope=None if kind == "ctx" else True)
                        for t in range(ntile):
                            s = pjslot()
                            for k in range(8):
                                P.do("pe", MM(pj[:, s, 0:128], hx(k)[:, t * 128:(t + 1) * 128], Wbv[:, k, :],
                                              start=(k == 0), stop=(k == 7)),
                                     r=["Wbv", hkey], w=[("pj", s)], sig=(k == 7))
                            P.do("act", ACT(Vb[:, kc0 + t, 64:128], pj[:, s, 0:64], AF.Copy), r=[("pj", s)], w=["Vb"])
                            P.do("dve", CP(Vb[:, kc0 + t, 192:256], pj[:, s, 64:128]), r=[("pj", s)], w=["Vb"])
                        if kind == "ctx":
                            na_tiles = 2
                        elif slot0 is not None:
                            na_tiles = 4 if kind == "own" else 2
                        else:
                            na_tiles = 0
                        if na_tiles:
                            Nn = na_tiles * 128
                            for hp in range(4):
                                s = pjslot()
                                for k in range(8):
                                    P.do("pe", MM(pj[:, s, :Nn], Wak[:, k, hp * 128:(hp + 1) * 128], hx(k)[:, :Nn],
                                                  start=(k == 0), stop=(k == 7)),
                                         r=["Wak", hkey], w=[("pj", s)], sig=(k == 7))
                                if kind == "ctx":
                                    dst = KaTc[:, hp, 0:Nn]
                                    dkey = "KaTc"
                                else:
                                    dst = KaT[:, hp, slot0 * 128:slot0 * 128 + Nn]
                                    dkey = "KaT"
                                qknorm(pj[:, s, :Nn], ("pj", s), Nn, gains[:, 1:2], dst, dkey, hp % 2)
                            for t in range(na_tiles):
                                s = pjslot()
                                for k in range(8):
                                    P.do("pe", MM(pj[:, s, :], hx(k)[:, t * 128:(t + 1) * 128], Wav[:, k, :],
                                                  start=(k == 0), stop=(k == 7)),
                                         r=["Wav", hkey], w=[("pj", s)], sig=(k == 7))
                                if kind == "ctx":
                                    P.do("act", ACT(Vac[:, t, :], pj[:, s, :], AF.Copy), r=[("pj", s)], w=["Vac"])
                                else:
                                    P.do("act", ACT(Va[:, slot0 + t, :], pj[:, s, :], AF.Copy), r=[("pj", s)], w=["Va"])
                    stageA(0)
                    for gi in range(len(groups)):
                        if gi + 1 < len(groups):
                            stageA(gi + 1)
                        stageB(gi)
                    if dbg is not None:
                        dbg(P, "KbT", KbT[:], ["KbT"])
                        dbg(P, "Vb", Vb[:], ["Vb"])
                        dbg(P, "KaT", KaT[:], ["KaT"])
                        dbg(P, "Va", Va[:], ["Va"])
                        dbg(P, "hxT", hxT[:], [("hxT", i) for i in range(5)])
                    P.flush()

                with ExitStack() as S2:
                    e2 = S2.enter_context
                    Wq = e2(sb("Waq", [128, 8, 512], BF16))
                    Wz = e2(sb("Waz", [128, 8, 512], BF16))
                    stg = e2(sb("stg2", [128, 1, 8, 128], F32))
                    Eb2 = e2(sb("Eb", [128, 8, 640], BF16))
                    est = e2(sb("est", [128, 1, 640], F32))
                    raw = e2(sb("raw2", [128, 512], F32))
                    sq = e2(sb("sq2", [128, 512], BF16))
                    lnr = e2(sb("lnr2", [128, 512], F32))
                    rsr = lnr
                    Pt = e2(sb("Pt2", [128, 2, 1024], BF16))
                    ebuf = e2(sb("ebuf2", [128, 512], F32))
                    den = ebuf
                    rden = ebuf
                    tz = ebuf
                    P._pre("dve", [], ["gAscr", ("gAscr", 0), ("gAscr", 1)])
                    P._pre("act", [], ["gBscr"])

                    def Eb(cfg, h):
                        if cfg < 2:
                            return gatedB[:, 2 * cfg + h // 4, (h % 4) * 512:(h % 4 + 1) * 512]
                        return Eb2[:, h, :]
                    Sps = e2(ps("Sps", [128, 2, 1024], F32))
                    PV = e2(ps("PV", [128, 512], F32))
                    SM = e2(ps("SM", [128, 512], F32))
                    qps = e2(ps("qps", [128, 512], F32))
                    aps = e2(ps("aps", [128, 512], F32))

                    cnt = [0]

                    def load_cast(dst, c0, nblk, tag):
                        for bi in range(nblk):
                            s = 0
                            cnt[0] += 1
                            P.dma("sp", stg[:, s, :, :], io.w_in[:, c0 + bi * 128:c0 + (bi + 1) * 128]
                                  .rearrange("(k p) c -> p k c", p=128), w=[("stg", s)])
                            CAST(P, dst[:, :, bi * 128:(bi + 1) * 128], stg[:, s, :, :], [("stg", s)], [tag])
                    load_cast(Wq, C_AQ, 4, "Wq")
                    load_cast(Wz, C_AZ, 4, "Wz")
                    ec = 0
                    for c in range(3):
                        for h in range(8):
                            s = 0
                            ne = 512 if c < 2 else 640
                            P.dma("sp", est[:, s, :], io.ebias[c, h, :, :], w=[("est", s)])
                            P.do("act", ACT(Eb(c, h), est[:, s, 0:ne], AF.Exp), r=[("est", s)], w=[("Eb", c), "gBscr"])

                    units = [(grp, hp) for grp in qgroups for hp in range(4)]
                    qn2 = e2(sb("qn2b", [128, 2, 512], BF16))

                    def prep_steps(ui):
                        (gname, c0, ntile, is_ctx), hp = units[ui]
                        N = ntile * 128
                        gi = 4 if is_ctx else c0 // 512
                        hkey = ("hxT", gi)
                        qb = ui % 2

                        def s0():
                            for k in range(8):
                                P.do("pe", MM(qps[:, :N], Wq[:, k, hp * 128:(hp + 1) * 128], hxT[:, k, c0:c0 + N],
                                              start=(k == 0), stop=(k == 7)),
                                     r=["Wq", hkey], w=["qps"], sig=(k == 7))
                            P.do("dve", CP(raw[:, :N], qps[:, :N]), r=["qps"], w=["raw"])
                            P.do("pool", TT(sq[:, :N], raw[:, :N], raw[:, :N], ALU.mult), r=["raw"], w=["sq"])

                        def s1():
                            P.do("pe", MM(qps[:, :N], blockones[:], sq[:, :N]), r=["sq"], w=["qps"])
                            P.do("act", ACT(lnr[:, :N], qps[:, :N], AF.Ln, bias=EPS, scale=1.0 / HD), r=["qps"], w=["rsr"])
                            P.do("act", ACT(rsr[:, :N], lnr[:, :N], AF.Exp, scale=-0.5), r=["rsr"], w=["rsr"])
                            P.do("dve", STT(qn2[:, qb, :N], raw[:, :N], gains[:, 0:1], rsr[:, :N], ALU.mult, ALU.mult),
                                 r=["raw", "rsr", "gains"], w=[("qn", qb)])
                        return [s0, s1]

                    sbuf_i = [0]
                    for st_ in prep_steps(0):
                        st_()
                    for ui, ((gname, c0, ntile, is_ctx), hp) in enumerate(units):
                        N = ntile * 128
                        gi = 4 if is_ctx else c0 // 512
                        hkey = ("hxT", gi)
                        qb = ui % 2
                        nxt = prep_steps(ui + 1) if ui + 1 < len(units) else []
                        for k in range(8):
                            P.do("pe", MM(aps[:, :N], Wz[:, k, hp * 128:(hp + 1) * 128], hxT[:, k, c0:c0 + N],
                                          start=(k == 0), stop=(k == 7)),
                                 r=["Wz", hkey], w=["aps"], sig=(k == 7))
                        items = [(tq, par) for tq in range(ntile) for par in range(2)]

                        def item_info(it):
                            tq, par = it
                            if is_ctx:
                                return [("c", 0), ("c", 1)], None, 0
                            T = c0 // 128 + tq
                            s0_ = max(T - 2, 0)
                            nwin = 4 if T < 2 else 5
                            return ([("w", s0_ + i) for i in range(nwin)] + [("c", 0), ("c", 1)]), min(T, 2), nwin

                        def emit_S(it, bi_):
                            tq, par = it
                            chunks, cfg, nwin = item_info(it)
                            nch = len(chunks)
                            h = hp * 2 + par
                            hb = par * 64
                            for ci, (ck, cs) in enumerate(chunks):
                                lhs = (KaT[hb:hb + 64, hp, cs * 128:(cs + 1) * 128] if ck == "w"
                                       else KaTc[hb:hb + 64, hp, cs * 128:(cs + 1) * 128])
                                P.do("pe", MM(Sps[:, bi_, ci * 128:(ci + 1) * 128], lhs,
                                              qn2[hb:hb + 64, qb, tq * 128:(tq + 1) * 128]),
                                     r=["KaT", "KaTc", ("qn", qb)], w=[("S", bi_)], sig=(ci == nch - 1))
                            P.do("act", ACT(Pt[:, bi_, 0:nch * 128], Sps[:, bi_, 0:nch * 128], AF.Exp),
                                 r=[("S", bi_)], w=[("Pt", bi_)])
                            if not is_ctx:
                                P.do("dve", TT(Pt[:, bi_, 0:nwin * 128], Pt[:, bi_, 0:nwin * 128], Eb(cfg, h), ALU.mult),
                                     r=[("Pt", bi_), ("Eb", cfg), "gBscr"], w=[("Pt", bi_)])

                        def emit_PV(it, bi_):
                            tq, par = it
                            chunks, cfg, nwin = item_info(it)
                            nch = len(chunks)
                            h = hp * 2 + par
                            hb = par * 64
                            for ci, (ck, cs) in enumerate(chunks):
                                vv = (Va[:, cs, h * 64:(h + 1) * 64] if ck == "w" else Vac[:, cs, h * 64:(h + 1) * 64])
                                P.do("pe", MM(PV[hb:hb + 64, tq * 128:(tq + 1) * 128], vv,
                                              Pt[:, bi_, ci * 128:(ci + 1) * 128],
                                              start=(ci == 0), stop=(ci == nch - 1)),
                                     r=["Va", "Vac", ("Pt", bi_)], w=["PV"], sig=(ci == nch - 1))
                            for ci in range(nch):
                                P.do("pe", MM(SM[hb:hb + 64, tq * 128:(tq + 1) * 128], ones64[:],
                                              Pt[:, bi_, ci * 128:(ci + 1) * 128],
                                              start=(ci == 0), stop=(ci == nch - 1)),
                                     r=[("Pt", bi_)], w=["SM"], sig=(ci == nch - 1))

                        base = sbuf_i[0]
                        sbuf_i[0] += len(items)
                        emit_S(items[0], base % 2)
                        nstep = 0
                        for ii, it in enumerate(items):
                            if ii + 1 < len(items):
                                emit_S(items[ii + 1], (base + ii + 1) % 2)
                            emit_PV(it, (base + ii) % 2)
                            if nstep < len(nxt) and (ii % 3 == 1 or ii == len(items) - 1):
                                nxt[nstep]()
                                nstep += 1
                        while nstep < len(nxt):
                            nxt[nstep]()
                            nstep += 1
                        P.do("act", ACT(ebuf[:, :N], aps[:, :N], AF.Exp, scale=-1.0), r=["aps"], w=["ebuf"])
                        P.do("dve", STT(den[:, :N], ebuf[:, :N], 1.0, SM[:, :N], ALU.add, ALU.mult),
                             r=["ebuf", "SM"], w=["ebuf"])
                        P.do("dve", RCP(rden[:, :N], den[:, :N]), r=["ebuf"], w=["ebuf"])
                        P.do("dve", TT(tz[:, :N], aps[:, :N], rden[:, :N], ALU.mult), r=["aps", "ebuf"], w=["ebuf"])
                        P.do("dve", TT(gatedA[:, hp, c0:c0 + N], PV[:, :N], tz[:, :N], ALU.mult),
                             r=["PV", "ebuf"], w=[("gA", gi)])
                    if dbg is not None:
                        dbg(P, "gatedA", gatedA[:], [("gA", i) for i in range(5)])
                    P.flush()

            with ExitStack() as S3:
                e3 = S3.enter_context
                Wq = e3(sb("Wbq", [128, 8, 512], BF16))
                Wz = e3(sb("Wbz", [128, 8, 512], BF16))
                stg = e3(sb("stg3", [128, 2, 8, 128], F32))
                raw = e3(sb("raw3", [128, 512], F32))
                sq = e3(sb("sq3", [128, 512], BF16))
                lnr = e3(sb("lnr3", [128, 512], F32))
                rsr = lnr
                P._pre("dve", [], ["gBscr"])
                qn = e3(sb("qn3", [128, 512], BF16))
                qr = e3(sb("qr3", [128, 2, 512], BF16))
                cosb = e3(sb("cosb3", [128, 2, 512], F32))
                sinb = e3(sb("sinb3", [128, 2, 512], F32))
                t1 = e3(sb("t13", [128, 512], F32))
                t2 = e3(sb("t23", [128, 512], F32))
                Pt = e3(sb("Pt3", [128, 2, 1024], BF16))
                ebuf = e3(sb("ebuf3", [128, 512], F32))
                den = ebuf
                rden = ebuf
                tz = ebuf
                Sps = e3(ps("Sps3", [128, 2, 1024], F32))
                PVa = e3(ps("PV3", [128, 512], F32))
                PVb = e3(ps("SM3", [128, 512], F32))
                srow = e3(sb("srow3", [128, 512], F32))
                P.do("pool", MEMSET(srow[:], 0.0), w=["srow"])
                qps = e3(ps("qps3", [128, 512], F32))
                aps = e3(ps("aps3", [128, 512], F32))

                cnt = [0]

                def load_cast3(dst, c0, nblk, tag):
                    for bi in range(nblk):
                        s = cnt[0] % 2
                        cnt[0] += 1
                        P.dma("sp" if s == 0 else "act", stg[:, s, :, :], io.w_in[:, c0 + bi * 128:c0 + (bi + 1) * 128]
                              .rearrange("(k p) c -> p k c", p=128), w=[("stg", s)])
                        CAST(P, dst[:, :, bi * 128:(bi + 1) * 128], stg[:, s, :, :], [("stg", s)], [tag])
                load_cast3(Wq, C_BQ, 4, "Wq")
                load_cast3(Wz, C_BZ, 4, "Wz")

                units = [(grp, hp) for grp in qgroups for hp in range(4)]

                def prep_steps(ui):
                    (gname, c0, ntile, is_ctx), hp = units[ui]
                    N = ntile * 128
                    gi = 4 if is_ctx else c0 // 512
                    hkey = ("hxT", gi)
                    qb = ui % 2
                    cb_ = gi % 2
                    qrd = qr[:, qb, :N]

                    def s0():
                        if (not is_ctx) and hp == 0:
                            P.dma("sp", cosb[:, cb_, :], cos_t[:, c0:c0 + 512], w=[("cosb", cb_)])
                            P.dma("sp", sinb[:, cb_, :], sin_t[:, c0:c0 + 512], w=[("sinb", cb_)])
                        for k in range(8):
                            P.do("pe", MM(qps[:, :N], Wq[:, k, hp * 128:(hp + 1) * 128], hxT[:, k, c0:c0 + N],
                                          start=(k == 0), stop=(k == 7)),
                                 r=["Wq", hkey], w=["qps"], sig=(k == 7))
                        P.do("dve", CP(raw[:, :N], qps[:, :N]), r=["qps"], w=["raw"])
                        P.do("pool", TT(sq[:, :N], raw[:, :N], raw[:, :N], ALU.mult), r=["raw"], w=["sq"])

                    def s1():
                        P.do("pe", MM(qps[:, :N], blockones[:], sq[:, :N]), r=["sq"], w=["qps"])
                        P.do("act", ACT(lnr[:, :N], qps[:, :N], AF.Ln, bias=EPS, scale=1.0 / HD), r=["qps"], w=["rsr"])
                        P.do("act", ACT(rsr[:, :N], lnr[:, :N], AF.Exp, scale=-0.5), r=["rsr"], w=["rsr"])
                        if is_ctx:
                            P.do("dve", STT(qrd, raw[:, :N], gains[:, 2:3], rsr[:, :N], ALU.mult, ALU.mult),
                                 r=["raw", "rsr", "gains"], w=[("qr", qb)])
                        else:
                            P.do("dve", STT(qn[:, :N], raw[:, :N], gains[:, 2:3], rsr[:, :N], ALU.mult, ALU.mult),
                                 r=["raw", "rsr", "gains"], w=["qn"])

                    def s2():
                        if not is_ctx:
                            P.do("pe", MM(qps[:, :N], rotT[:], qn[:, :N]), r=["qn"], w=["qps"])
                            P.do("dve", TT(t1[:, :N], qn[:, :N], cosb[:, cb_, :N], ALU.mult),
                                 r=["qn", ("cosb", cb_)], w=["t1"])
                            P.do("dve", TT(t2[:, :N], qps[:, :N], sinb[:, cb_, :N], ALU.mult),
                                 r=["qps", ("sinb", cb_)], w=["t2"])
                            P.do("dve", TT(qrd, t1[:, :N], t2[:, :N], ALU.add), r=["t1", "t2"], w=[("qr", qb)])
                    return [s0, s1, s2]

                sbuf_i = [0]
                for st_ in prep_steps(0):
                    st_()
                for ui, ((gname, c0, ntile, is_ctx), hp) in enumerate(units):
                    N = ntile * 128
                    gi = 4 if is_ctx else c0 // 512
                    hkey = ("hxT", gi)
                    g = hp // 2
                    qb = ui % 2
                    nxt = prep_steps(ui + 1) if ui + 1 < len(units) else []
                    for k in range(8):
                        P.do("pe", MM(aps[:, :N], Wz[:, k, hp * 128:(hp + 1) * 128], hxT[:, k, c0:c0 + N],
                                      start=(k == 0), stop=(k == 7)),
                             r=["Wz", hkey], w=["aps"], sig=(k == 7))
                    chunks = [0, 1] if is_ctx else list(range(NKC))
                    nblk = len(chunks) // 2
                    items = [(par, bk) for par in range(2) for bk in range(nblk)]

                    def emit_S(it, bi_):
                        par, bk = it
                        hb = par * 64
                        for j in range(2):
                            ch = chunks[bk * 2 + j]
                            P.do("pe", MM(Sps[:, bi_, j * 512:j * 512 + N],
                                          KbT[hb:hb + 64, g, ch * 128:(ch + 1) * 128],
                                          qr[hb:hb + 64, qb, :N]),
                                 r=["KbT", ("qr", qb)], w=[("S", bi_)], sig=(j == 1))
                        if N == 512:
                            P.do("act", ACT(Pt[:, bi_, :], Sps[:, bi_, :], AF.Exp), r=[("S", bi_)], w=[("Pt", bi_)])
                        else:
                            for j in range(2):
                                P.do("act", ACT(Pt[:, bi_, j * 512:j * 512 + N], Sps[:, bi_, j * 512:j * 512 + N], AF.Exp),
                                     r=[("S", bi_)], w=[("Pt", bi_)])

                    def emit_PV(it, bi_):
                        par, bk = it
                        hb = par * 64
                        for j in range(2):
                            ch = chunks[bk * 2 + j]
                            first = (bk == 0 and j == 0)
                            last = (bk == nblk - 1 and j == 1)
                            if par == 0:
                                P.do("pe", MM(PVa[0:65, :N], Vb[:, ch, 64 + 128 * g:64 + 128 * g + 65],
                                              Pt[:, bi_, j * 512:j * 512 + N], start=first, stop=last),
                                     r=["Vb", ("Pt", bi_)], w=["PVa"], sig=(j == 1))
                            else:
                                P.do("pe", MM(PVb[:, :N], Vb[:, ch, 128 * g:128 * g + 128],
                                              Pt[:, bi_, j * 512:j * 512 + N], start=first, stop=last),
                                     r=["Vb", ("Pt", bi_)], w=["PVb"], sig=(j == 1))

                    base = sbuf_i[0]
                    sbuf_i[0] += len(items)
                    emit_S(items[0], base % 2)
                    nstep = 0
                    for ii, it in enumerate(items):
                        if ii + 1 < len(items):
                            emit_S(items[ii + 1], (base + ii + 1) % 2)
                        emit_PV(it, (base + ii) % 2)
                        if nstep < len(nxt) and (ii % 3 == 1 or ii == len(items) - 1):
                            nxt[nstep]()
                            nstep += 1
                    while nstep < len(nxt):
                        nxt[nstep]()
                        nstep += 1
                    P.do("act", ACT(ebuf[:, :N], aps[:, :N], AF.Exp, scale=-1.0), r=["aps"], w=["ebuf"])
                    P.do("dve", CP(srow[64:65, :N], PVa[64:65, :N]), r=["PVa"], w=["srow"])
                    P.do("dve", CP(srow[0:1, :N], PVb[0:1, :N]), r=["PVb"], w=["srow"])
                    P.do("pe", MM(qps[:, :N], selT[:], srow[:, :N]), r=["srow"], w=["qps"])
                    P.do("dve", STT(den[:, :N], ebuf[:, :N], 1.0, qps[:, :N], ALU.add, ALU.mult),
                         r=["ebuf", "qps"], w=["ebuf"])
                    P.do("dve", RCP(rden[:, :N], den[:, :N]), r=["ebuf"], w=["ebuf"])
                    P.do("dve", TT(tz[:, :N], aps[:, :N], rden[:, :N], ALU.mult), r=["aps", "ebuf"], w=["ebuf"])
                    P.do("dve", TT(gatedB[0:64, hp, c0:c0 + N], PVa[0:64, :N], tz[0:64, :N], ALU.mult),
                         r=["PVa", "ebuf"], w=[("gB", gi)])
                    P.do("dve", TT(gatedB[64:128, hp, c0:c0 + N], PVb[64:128, :N], tz[64:128, :N], ALU.mult),
                         r=["PVb", "ebuf"], w=[("gB", gi)])
                P.flush()

        with ExitStack() as S4:
            e4 = S4.enter_context
            Wga = e4(sb("Wga", [128, 8, D], BF16))
            Wgb = e4(sb("Wgb", [128, 8, D], BF16))
            Woa = e4(sb("Woa", [128, 4, D], BF16))
            Wob = e4(sb("Wob", [128, 4, D], BF16))
            Wo = e4(sb("Wo", [128, 8, D], BF16))
            stg = e4(sb("stg4", [128, 2, 8, 128], F32))
            sga = e4(sb("sga", [128, 512], F32))
            sgb = e4(sb("sgb", [128, 512], F32))
            ta = e4(sb("ta", [128, 512], F32))
            tb = e4(sb("tb", [128, 512], F32))
            mg = e4(sb("mg", [128, 8, 512], BF16))
            xres = e4(sb("xres", [128, 2, D], F32))
            xo = e4(sb("xo", [128, 2, D], F32))
            tmo = e4(sb("tmo", [128, D], F32))
            oa = e4(ps("oa", [128, 512], F32))
            ob = e4(ps("ob", [128, 512], F32))
            ga = e4(ps("ga", [128, 512], F32))
            gb = e4(ps("gb", [128, 512], F32))
            ops_ = e4(ps("ops", [128, 2, 2, 512], F32))

            cnt = [0]

            def load_cast4(dst, src, kch, c0, nblk, tag):
                for bi in range(nblk):
                    s = cnt[0] % 2
                    cnt[0] += 1
                    P.dma("sp" if s == 0 else "act", stg[:, s, 0:kch, :], src[:, c0 + bi * 128:c0 + (bi + 1) * 128]
                          .rearrange("(k p) c -> p k c", p=128), w=[("stg", s)])
                    CAST(P, dst[:, :, bi * 128:(bi + 1) * 128], stg[:, s, 0:kch, :], [("stg", s)], [tag])
            load_cast4(Wga, io.w_in, 8, C_GA, 8, "Wga")
            load_cast4(Wgb, io.w_in, 8, C_GB, 8, "Wgb")
            load_cast4(Woa, io.w_o_a, 4, 0, 8, "Woa")
            load_cast4(Wob, io.w_o_b, 4, 0, 8, "Wob")
            load_cast4(Wo, io.w_out, 8, 0, 8, "Wo")

            oc = [0]
            for (gname, c0, ntile, is_ctx) in qgroups:
                N = ntile * 128
                gi = 4 if is_ctx else c0 // 512
                hkey = ("hxT", gi)
                v = 1 if is_ctx else 0
                for c in range(8):
                    for k in range(8):
                        P.do("pe", MM(ga[:, :N], Wga[:, k, c * 128:(c + 1) * 128], hxT[:, k, c0:c0 + N],
                                      start=(k == 0), stop=(k == 7)),
                             r=["Wga", hkey], w=["ga"], sig=(k == 7))
                    for k in range(8):
                        P.do("pe", MM(gb[:, :N], Wgb[:, k, c * 128:(c + 1) * 128], hxT[:, k, c0:c0 + N],
                                      start=(k == 0), stop=(k == 7)),
                             r=["Wgb", hkey], w=["gb"], sig=(k == 7))
                    for hp in range(4):
                        P.do("pe", MM(oa[:, :N], Woa[:, hp, c * 128:(c + 1) * 128], gatedA[:, hp, c0:c0 + N],
                                      start=(hp == 0), stop=(hp == 3)),
                             r=["Woa", ("gA", gi)], w=["oa"], sig=(hp == 3))
                    for hp in range(4):
                        P.do("pe", MM(ob[:, :N], Wob[:, hp, c * 128:(c + 1) * 128], gatedB[:, hp, c0:c0 + N],
                                      start=(hp == 0), stop=(hp == 3)),
                             r=["Wob", ("gB", gi)], w=["ob"], sig=(hp == 3))
                    P.do("act", ACT(sga[:, :N], ga[:, :N], AF.Sigmoid), r=["ga"], w=["sga"])
                    P.do("act", ACT(sgb[:, :N], gb[:, :N], AF.Sigmoid), r=["gb"], w=["sgb"])
                    P.do("dve", TT(ta[:, :N], oa[:, :N], sga[:, :N], ALU.mult), r=["oa", "sga"], w=["ta"])
                    P.do("dve", TT(tb[:, :N], ob[:, :N], sgb[:, :N], ALU.mult), r=["ob", "sgb"], w=["tb"])
                    P.do("dve", TT(mg[:, c, :N], ta[:, :N], tb[:, :N], ALU.add), r=["ta", "tb"], w=["mg"])
                for t in range(ntile):
                    s = oc[0] % 2
                    oc[0] += 1
                    if is_ctx:
                        rsrc = ctx_src[t * 128:(t + 1) * 128, :]
                        dst = ctx_dst[t * 128:(t + 1) * 128, :]
                    else:
                        rsrc = x_rows(c0 + t * 128, 128)[0]
                        dst = x_dst[c0 + t * 128:c0 + (t + 1) * 128, :]
                    P.dma("sp", xres[:, s, :], rsrc, w=[("xres", s)])
                    for hf in range(2):
                        for c in range(8):
                            P.do("pe", MM(ops_[:, s, hf, :], mg[:, c, t * 128:(t + 1) * 128],
                                          Wo[:, c, hf * 512:(hf + 1) * 512], start=(c == 0), stop=(c == 7)),
                                 r=["mg", "Wo"], w=[("ops", s, hf)], sig=(c == 7))
                    for hf in range(2):
                        P.do("dve", TT(tmo[:, hf * 512:(hf + 1) * 512], ops_[:, s, hf, :],
                                       gate_bc[:, v, hf * 512:(hf + 1) * 512], ALU.mult),
                             r=[("ops", s, hf), "gate_bc"], w=["tmo"])
                    P.do("dve", TT(xo[:, s, :], tmo[:], xres[:, s, :], ALU.add), r=["tmo", ("xres", s)], w=[("xo", s)])
                    P.dma("pool", dst, xo[:, s, :], r=[("xo", s)], key=("xo", s))
                if after_group is not None and not is_ctx:
                    for s in range(2):
                        for t in list(P._st(("xo", s))["r"].values()):
                            P._wait("pool", t)
                    after_group(gi)
            for s in range(2):
                st = P._st(("xo", s))
                for t in list(st["r"].values()):
                    P._wait("pool", t)
            P.flush()


def build_program(mode):
    nc = bass.Bass("TRN2", target_bir_lowering=False)
    dt = nc.dram_tensor
    x_src = dt("x_prog", [SEQ, D], F32, kind="ExternalInput").ap()
    ctx_src = dt("ctx_in", [CTX, D], F32, kind="ExternalInput").ap()
    cvec = dt("cvec", [128, 8, 2], F32, kind="ExternalInput").ap()
    cos_t = dt("cos_t", [128, SEQ], F32, kind="ExternalInput").ap()
    sin_t = dt("sin_t", [128, SEQ], F32, kind="ExternalInput").ap()
    c_ident = dt("c_ident", [128, 128], BF16, kind="ExternalInput").ap()
    c_bones = dt("c_bones", [128, 128], BF16, kind="ExternalInput").ap()
    c_rotT = dt("c_rotT", [128, 128], BF16, kind="ExternalInput").ap()
    c_sel = dt("c_sel", [128, 128], F32, kind="ExternalInput").ap()
    fused = (mode == "fused")
    if fused:
        io0 = declare_layer_inputs(nc, "_0")
        io1 = declare_layer_inputs(nc, "_1")
        selm_in = dt("selm", [128, 2], F32, kind="ExternalInput").ap()
        xmid = dt("xmid", [2048, D], F32).ap()
        ctxmid = dt("ctxmid", [CTX, D], F32).ap()
        xgath = [dt("xgath%d" % i, [1024, D], F32).ap() for i in range(4)]
        xoth = dt("xoth", [2048, D], F32).ap()
    else:
        io = declare_layer_inputs(nc, "")
    x_dst = dt("xo_out", [2048, D], F32, kind="ExternalOutput").ap()
    ctx_dst = dt("ctxo_out", [CTX, D], F32, kind="ExternalOutput").ap() if mode == "layer0" else None

    with ExitStack() as stack:
        ent = stack.enter_context
        P = Prog(nc, stack)
        ident = ent(nc.sbuf_tensor("ident", [128, 128], BF16))
        blockones = ent(nc.sbuf_tensor("blockones", [128, 128], BF16))
        rotT = ent(nc.sbuf_tensor("rotT", [128, 128], BF16))
        ones64 = ent(nc.sbuf_tensor("ones64", [128, 64], BF16))
        ones_f = ent(nc.sbuf_tensor("ones_f", [128, 128], F32))
        selT = ent(nc.sbuf_tensor("selT", [128, 128], F32))
        P.dma("sp", selT[:], c_sel, w=["selT"])
        P.dma("sp", ident[:], c_ident, w=["ident"])
        P.dma("sp", blockones[:], c_bones, w=["blockones"])
        P.dma("sp", rotT[:], c_rotT, w=["rotT"])
        P.do("pool", MEMSET(ones64[:], 1.0), w=["ones64"])
        P.do("pool", MEMSET(ones_f[:], 1.0), w=["ones_f"])
        for e in ("pe", "act", "dve", "pool"):
            P._pre(e, ["ident", "blockones", "rotT", "ones64", "ones_f", "selT"], [])
        P.flush()
        cst = (ident, blockones, rotT, ones64, ones_f, selT)

        def rows_in(r0, n):
            return x_src[r0:r0 + n, :], []

        if not fused:
            emit_layer(P, nc, cst, io, rows_in, ctx_src, cvec, cos_t, sin_t, x_dst, ctx_dst,
                       mode == "layer0")
            return nc

        def cc(i):
            return lambda e: e.collective_compute(
                "AllGather", ALU.bypass, replica_groups=[[0, 1], [2, 3], [4, 5], [6, 7]],
                ins=[xmid[i * 512:(i + 1) * 512, :]], outs=[xgath[i]])

        def after_group(gi):
            P.do("pool", cc(gi), w=[("xgath", gi)])

        emit_layer(P, nc, cst, io0, rows_in, ctx_src, cvec, cos_t, sin_t, xmid, ctxmid, True, uid="a",
                   after_group=after_group)

        with ExitStack() as SX:
            ex = SX.enter_context
            ca = ex(nc.sbuf_tensor("x_ca", [128, 2, D], F32))
            cb = ex(nc.sbuf_tensor("x_cb", [128, 2, D], F32))
            oo = ex(nc.sbuf_tensor("x_oo", [128, 2, D], F32))
            selm = ex(nc.sbuf_tensor("x_selm", [128, 2], F32))
            P.dma("sp", selm[:], selm_in, w=["selm"])
            for U in range(16):
                s_ = U % 2
                pt = 15 - U
                ci, cj = pt // 4, pt % 4
                P.dma("sp", ca[:, s_, :], xgath[ci][cj * 128:(cj + 1) * 128, :], r=[("xgath", ci)],
                      w=[("ca", s_)], key=("ca", s_))
                P.dma("sp", cb[:, s_, :], xgath[ci][512 + cj * 128:512 + (cj + 1) * 128, :], r=[("xgath", ci)],
                      w=[("cb", s_)], key=("cb", s_))
                P.do("dve", TS(cb[:, s_, :], cb[:, s_, :], selm[:, 1:2], None, ALU.mult),
                     r=[("cb", s_), "selm"], w=[("cb", s_)])
                P.do("dve", STT(oo[:, s_, :], ca[:, s_, :], selm[:, 0:1], cb[:, s_, :], ALU.mult, ALU.add),
                     r=[("ca", s_), ("cb", s_), "selm"], w=[("oo", s_)])
                P.dma("pool", xoth[U * 128:(U + 1) * 128, :], oo[:, s_, :], r=[("oo", s_)], w=[("xoth", U)],
                      key=("oo", s_))
            for s_ in range(2):
                for t in list(P._st(("oo", s_))["r"].values()):
                    P._wait("pool", t)
            P.flush()

        def rows_mid(r0, n):
            if r0 < 2048:
                return xmid[r0:r0 + n, :], []
            U0 = (r0 - 2048) // 128
            return xoth[r0 - 2048:r0 - 2048 + n, :], [("xoth", U0 + i) for i in range(n // 128)]

        emit_layer(P, nc, cst, io1, rows_mid, ctxmid, cvec, cos_t, sin_t, x_dst, None, False, uid="b")
    return nc


def _tile_order(hf):
    return list(range(32)) if hf == 0 else list(range(31, -1, -1))


def _rope_tables(hf):
    gl = _tile_order(hf)
    t = np.concatenate([np.arange(g * 128, (g + 1) * 128) for g in gl]).astype(np.int32)
    pos_row = (t // 64).astype(np.float32)
    pos_col = (t % 64).astype(np.float32)
    half = 16
    inv = (1.0 / (np.float32(10000.0) ** (np.arange(half, dtype=np.float32) / np.float32(half)))).astype(np.float32)
    ar = pos_row[:, None] * inv[None, :]
    ac = pos_col[:, None] * inv[None, :]
    cos64 = np.concatenate([np.cos(ar), np.cos(ar), np.cos(ac), np.cos(ac)], axis=1).astype(np.float32)
    sin64 = np.concatenate([np.sin(ar), np.sin(ar), np.sin(ac), np.sin(ac)], axis=1).astype(np.float32)
    cos_t = np.ascontiguousarray(np.tile(cos64.T, (2, 1)))
    sin_t = np.ascontiguousarray(np.tile(sin64.T, (2, 1)))
    return cos_t, sin_t


def _consts():
    ident = np.eye(128, dtype=np.float32)
    bones = np.zeros((128, 128), np.float32)
    bones[:64, :64] = 1.0
    bones[64:, 64:] = 1.0
    R = np.zeros((64, 64), np.float32)
    for base in (0, 32):
        for i in range(16):
            R[base + i, base + i + 16] = -1.0
            R[base + 16 + i, base + i] = 1.0
    R2 = np.zeros((128, 128), np.float32)
    R2[:64, :64] = R
    R2[64:, 64:] = R
    bf = ml_dtypes.bfloat16
    return ident.astype(bf), bones.astype(bf), np.ascontiguousarray(R2.T).astype(bf)


def _sel_const():
    sel = np.zeros((128, 128), np.float32)
    sel[64, 0:64] = 1.0
    sel[0, 64:128] = 1.0
    return sel


def _ebias(rpb_l, hf):
    out = np.full((3, 8, 128, 5, 128), NEG, np.float32)
    kk = np.arange(128)
    a = kk // 64
    kc = kk % 64
    qq = np.arange(128)
    b = qq // 64
    qc = qq % 64
    cs = np.clip(qc - 8, 0, 48)
    colvalid = (kc[:, None] >= cs[None, :]) & (kc[:, None] < cs[None, :] + 16)
    co = kc[:, None] - qc[None, :] + 15
    for c in range(3):
        T = c
        s0 = max(T - 2, 0)
        j = T if hf == 0 else 31 - T
        qr = 2 * j + b
        rs = np.clip(qr - 4, 0, 56)
        for i in range(5):
            slot = s0 + i
            p = slot if hf == 0 else 31 - slot
            kr = 2 * p + a
            rowvalid = (kr[:, None] >= rs[None, :]) & (kr[:, None] < rs[None, :] + 8)
            ro = kr[:, None] - qr[None, :] + 7
            valid = rowvalid & colvalid
            roc = np.clip(ro, 0, 14)
            coc = np.clip(co, 0, 30)
            vals = rpb_l[:, roc, coc]
            out[c, :, :, i, :] = np.where(valid[None], vals, np.float32(NEG))
    return np.ascontiguousarray(out.reshape(3, 8, 128, 640))


def _layer_maps(l, hf, w_ada, b_ada, norm_g, w_in, q_norm_a, k_norm_a, q_norm_b, k_norm_b,
                rpb, w_o_a, w_o_b, w_out, sfx=""):
    f = np.float32
    gains = np.stack([np.tile(q_norm_a[l], 2), np.tile(k_norm_a[l], 2),
                      np.tile(q_norm_b[l], 2), np.tile(k_norm_b[l], 2)], axis=1).astype(f)
    return {
        "w_ada" + sfx: np.ascontiguousarray(w_ada[l]),
        "b_ada_fm" + sfx: np.ascontiguousarray(b_ada[l].reshape(24, 128).T),
        "b_gate" + sfx: np.ascontiguousarray(b_ada[l][None, 2048:3072]),
        "norm_g" + sfx: np.ascontiguousarray(norm_g[l].reshape(8, 128).T),
        "w_in" + sfx: np.ascontiguousarray(w_in[l]),
        "gains" + sfx: np.ascontiguousarray(gains),
        "ebias" + sfx: _ebias(rpb[l], hf),
        "w_o_a" + sfx: np.ascontiguousarray(w_o_a[l]),
        "w_o_b" + sfx: np.ascontiguousarray(w_o_b[l]),
        "w_out" + sfx: np.ascontiguousarray(w_out[l]),
    }


_PROG_CACHE = {}


def _get_prog(mode):
    if mode not in _PROG_CACHE:
        _PROG_CACHE[mode] = build_program(mode)
    return _PROG_CACHE[mode]


def _prog_order(xb, hf):
    t = xb.reshape(32, 128, D)
    if hf == 1:
        t = t[::-1]
    return np.ascontiguousarray(t.reshape(SEQ, D))


def _unprog_own(xo, hf):
    t = xo.reshape(16, 128, D)
    if hf == 1:
        t = t[::-1]
    return t.reshape(2048, D)


def make_in_maps(x, c, ctx, c_ctx, w_ada, b_ada, norm_g, w_in, q_norm_a, k_norm_a,
                 q_norm_b, k_norm_b, rpb, w_o_a, w_o_b, w_out, cores=range(8)):
    f = np.float32
    ident, bones, rotT = _consts()
    ropes = [_rope_tables(0), _rope_tables(1)]
    lm = {}
    for hf in range(2):
        d = {}
        for l in range(2):
            d.update(_layer_maps(l, hf, w_ada, b_ada, norm_g, w_in, q_norm_a, k_norm_a, q_norm_b,
                                 k_norm_b, rpb, w_o_a, w_o_b, w_out, sfx="_%d" % l))
        lm[hf] = d
    in_maps = []
    for core in cores:
        b, hf = core // 2, core % 2
        m = dict(lm[hf])
        cv = np.stack([c[b].reshape(8, 128).T, c_ctx.reshape(8, 128).T], axis=2)
        selm = np.zeros((128, 2), f)
        selm[:, 1 - hf] = 1.0
        m.update({
            "x_prog": _prog_order(x[b], hf),
            "ctx_in": np.ascontiguousarray(ctx[b]),
            "cvec": np.ascontiguousarray(cv.astype(f)),
            "cos_t": ropes[hf][0], "sin_t": ropes[hf][1],
            "c_ident": ident, "c_bones": bones, "c_rotT": rotT, "c_sel": _sel_const(),
            "selm": selm,
        })
        in_maps.append(m)
    return in_maps


def kernel(x, c, ctx, c_ctx, w_ada, b_ada, norm_g, w_in, q_norm_a, k_norm_a,
           q_norm_b, k_norm_b, rpb, w_o_a, w_o_b, w_out):
    f = np.float32
    arrs = [np.asarray(a, dtype=f) for a in (x, c, ctx, c_ctx, w_ada, b_ada, norm_g, w_in, q_norm_a,
                                              k_norm_a, q_norm_b, k_norm_b, rpb, w_o_a, w_o_b, w_out)]
    in_maps = make_in_maps(*arrs)
    nc = _get_prog("fused")
    res = run_bass_kernel_spmd(nc, in_maps, core_ids=list(range(8)))
    out = np.empty((NB, SEQ, D), f)
    for core in range(8):
        b, hf = core // 2, core % 2
        out[b, hf * 2048:(hf + 1) * 2048] = _unprog_own(np.asarray(res.results[core]["xo_out"]), hf)
    return out
```

```python
import numpy as np
import ml_dtypes
from contextlib import ExitStack

import concourse.bass as bass
import concourse.mybir as mybir
from concourse.bass_utils import run_bass_kernel_spmd

F32 = mybir.dt.float32
BF16 = mybir.dt.bfloat16
AF = mybir.ActivationFunctionType
ALU = mybir.AluOpType

D = 1024
SEQ = 4096
CTX = 256
NB = 4
HD = 64
IN_COLS = 5376
EPS = 1e-6
NEG = -30000.0
NT_OWN = 16
NKC = 34

C_AK, C_AV, C_BK, C_BV, C_AQ, C_BQ, C_AZ, C_BZ, C_GA, C_GB = (
    0, 512, 1024, 1152, 1280, 1792, 2304, 2816, 3328, 4352)


def MM(out, lhsT, rhs, start=True, stop=True):
    return lambda e: e.matmul(out, lhsT=lhsT, rhs=rhs, start=start, stop=stop)


def TR(out, in_, ident):
    return lambda e: e.transpose(out, in_, ident)


def ACT(out, in_, func, bias=0.0, scale=1.0, accum_out=None):
    if accum_out is None:
        return lambda e: e.activation(out=out, in_=in_, func=func, bias=bias, scale=scale)
    return lambda e: e.activation(out=out, in_=in_, func=func, bias=bias, scale=scale,
                                  accum_out=accum_out)


def TS(out, in0, s1, s2, op0, op1=None):
    if op1 is None:
        return lambda e: e.tensor_scalar(out=out, in0=in0, scalar1=s1, scalar2=None, op0=op0)
    return lambda e: e.tensor_scalar(out=out, in0=in0, scalar1=s1, scalar2=s2, op0=op0, op1=op1)


def TT(out, in0, in1, op):
    return lambda e: e.tensor_tensor(out=out, in0=in0, in1=in1, op=op)


def STT(out, in0, scalar, in1, op0, op1):
    return lambda e: e.scalar_tensor_tensor(out=out, in0=in0, scalar=scalar, in1=in1,
                                            op0=op0, op1=op1)


def CP(out, in_):
    return lambda e: e.tensor_copy(out=out, in_=in_)


def RCP(out, in_):
    return lambda e: e.reciprocal(out=out, in_=in_)


def MEMSET(ap, v):
    return lambda e: e.memset(ap, v)


class Prog:
    ENG = ("pe", "act", "dve", "pool", "sp")

    def __init__(self, nc, stack):
        self.nc = nc
        self.stack = stack
        self.ops = {e: [] for e in self.ENG}
        self.state = {}
        self.pend = {e: ([], []) for e in self.ENG}
        self.waited = {e: {} for e in self.ENG}
        self.dsem = {}
        self.nsem = 0
        self.esem = {}
        self.ecnt = {}
        self.pe_sems = set()
        self.new_epoch()

    def _sem(self, name):
        self.nsem += 1
        return self.stack.enter_context(self.nc.semaphore(f"{name}_{self.nsem}"))

    def new_epoch(self):
        for e in ("pe", "act", "dve", "pool"):
            assert not self.pend[e][0] and not self.pend[e][1], f"pending on {e}"
            self.esem[e] = self._sem("e" + e)
            self.ecnt[e] = 0
            if e == "pe":
                self.pe_sems.add(id(self.esem[e]))

    def _st(self, k):
        s = self.state.get(k)
        if s is None:
            s = {"w": None, "r": {}}
            self.state[k] = s
        return s

    def _wait(self, eng, tok):
        if tok is None:
            return
        if tok[0] == "PEND":
            assert tok[1] == eng, f"{eng} waiting on a pending (unsignalled) write of {tok[1]}"
            return
        sem, val = tok
        sid = id(sem)
        if eng == "pe" and sid in self.pe_sems:
            return
        if self.waited[eng].get(sid, 0) >= val:
            return
        self.waited[eng][sid] = val
        self.ops[eng].append(lambda e: e.wait_ge(sem, val))

    def _pre(self, eng, r, w):
        for k in r:
            self._wait(eng, self._st(k)["w"])
        for k in w:
            s = self._st(k)
            self._wait(eng, s["w"])
            for t in s["r"].values():
                self._wait(eng, t)

    def _post(self, tok, r, w):
        sem, val = tok
        for k in w:
            s = self._st(k)
            s["w"] = tok
            s["r"] = {}
        for k in r:
            s = self._st(k)
            s["r"][id(sem)] = tok

    def do(self, eng, fn, r=(), w=(), sig=True):
        r = list(r)
        w = list(w)
        self._pre(eng, r, w)
        if not sig:
            self.ops[eng].append(fn)
            self.pend[eng][0].extend(r)
            self.pend[eng][1].extend(w)
            for k in w:
                self._st(k)["w"] = ("PEND", eng)
            return None
        sem = self.esem[eng]
        self.ecnt[eng] += 1
        val = self.ecnt[eng]
        self.ops[eng].append(lambda e: fn(e).then_inc(sem, 1))
        tok = (sem, val)
        pr, pw = self.pend[eng]
        self._post(tok, r + pr, w + pw)
        self.pend[eng] = ([], [])
        return tok

    def dma(self, q, out, in_, r=(), w=(), key=None):
        r = list(r)
        w = list(w)
        self._pre(q, r, w)
        if key is None:
            key = w[0] if w else r[0]
        if key not in self.dsem:
            self.dsem[key] = [self._sem("d"), 0]
        ent = self.dsem[key]
        ent[1] += 16
        sem, val = ent[0], ent[1]
        self.ops[q].append(lambda e: e.dma_start(out=out, in_=in_).then_inc(sem, 16))
        tok = (sem, val)
        self._post(tok, r, w)
        return tok

    def wait_all(self, eng, keys):
        for k in keys:
            self._wait(eng, self._st(k)["w"])

    def flush(self):
        ops = self.ops
        self.ops = {e: [] for e in self.ENG}
        with self.nc.Block() as block:
            @block.tensor
            def _(e):
                for f in ops["pe"]:
                    f(e)

            @block.scalar
            def _(e):
                for f in ops["act"]:
                    f(e)

            @block.vector
            def _(e):
                for f in ops["dve"]:
                    f(e)

            @block.gpsimd
            def _(e):
                for f in ops["pool"]:
                    f(e)

            @block.sync
            def _(e):
                for f in ops["sp"]:
                    f(e)
        self.new_epoch()


class LayerIO:
    pass


_CAST_RR = [0]


def CAST(P, dst, src, r, w):
    eng = ("pool", "dve", "act")[_CAST_RR[0] % 3]
    _CAST_RR[0] += 1
    if eng == "act":
        P.do("act", ACT(dst, src, AF.Copy), r=r, w=w)
    else:
        P.do(eng, CP(dst, src), r=r, w=w)


def declare_layer_inputs(nc, sfx):
    io = LayerIO()
    dt = nc.dram_tensor
    io.w_ada = dt("w_ada" + sfx, [D, 3 * D], F32, kind="ExternalInput").ap()
    io.b_ada_fm = dt("b_ada_fm" + sfx, [128, 24], F32, kind="ExternalInput").ap()
    io.b_gate = dt("b_gate" + sfx, [1, D], F32, kind="ExternalInput").ap()
    io.norm_g = dt("norm_g" + sfx, [128, 8], F32, kind="ExternalInput").ap()
    io.w_in = dt("w_in" + sfx, [D, IN_COLS], F32, kind="ExternalInput").ap()
    io.gains = dt("gains" + sfx, [128, 4], F32, kind="ExternalInput").ap()
    io.ebias = dt("ebias" + sfx, [3, 8, 128, 640], F32, kind="ExternalInput").ap()
    io.w_o_a = dt("w_o_a" + sfx, [512, D], F32, kind="ExternalInput").ap()
    io.w_o_b = dt("w_o_b" + sfx, [512, D], F32, kind="ExternalInput").ap()
    io.w_out = dt("w_out" + sfx, [D, D], F32, kind="ExternalInput").ap()
    return io


def emit_layer(P, nc, cst, io, x_rows, ctx_src, cvec, cos_t, sin_t, x_dst, ctx_dst,
               update_ctx, dbg=None, uid="", after_group=None):
    ident, blockones, rotT, ones64, ones_f, selT = cst
    def sb(name, shape, dtype):
        return nc.sbuf_tensor("s%s_%s" % (uid, name), shape, dtype)

    def ps(name, shape, dtype):
        return nc.psum_tensor("p%s_%s" % (uid, name), shape, dtype)

    own_groups = [("o%d" % g, g * 512, 4, False) for g in range(4)]
    ctx_group = ("c", 2048, 2, True)
    qgroups = own_groups + ([ctx_group] if update_ctx else [])

    with ExitStack() as LA:
        ent = LA.enter_context
        hxT = ent(sb("hxT", [128, 8, 2304], BF16))
        gatedA = ent(sb("gatedA", [128, 4, 2304], BF16))
        gatedB = ent(sb("gatedB", [128, 4, 2304], BF16))
        modv = ent(sb("modv", [128, 16, 2], F32))
        Amod = ent(sb("Amod", [128, 8, 2], F32))
        gate_bc = ent(sb("gate_bc", [128, 2, D], F32))
        gains = ent(sb("gains", [128, 4], F32))
        ng = ent(sb("ng", [128, 8], F32))

        with ExitStack() as S0:
            e0 = S0.enter_context
            cv = e0(sb("cv", [128, 8, 2], F32))
            sc = e0(sb("sc", [128, 8, 2], F32))
            scb = e0(sb("scb", [128, 2, 8, 128], F32))
            wst2 = e0(sb("wst", [128, 2, 8, 512], F32))
            bfm = e0(sb("bfm", [128, 24], F32))
            bgr = e0(sb("bgr", [1, D], F32))
            mps = e0(ps("mps", [128, 512], F32))
            gps = e0(ps("gps", [128, 2, 512], F32))

            P.dma("sp", cv[:], cvec, w=["cv"])
            P.dma("sp", bfm[:], io.b_ada_fm, w=["bfm"])
            P.dma("sp", bgr[:], io.b_gate, w=["bgr"])
            P.dma("sp", ng[:], io.norm_g, w=["ng"])
            P.dma("sp", gains[:], io.gains, w=["gains"])
            P.do("act", ACT(sc[:], cv[:], AF.Silu), r=["cv"], w=["sc"])
            P.do("dve", TS(gains[:, 0:1], gains[:, 0:1], 0.125, None, ALU.mult), r=["gains"], w=["gains"])
            P.do("dve", TS(gains[:, 2:3], gains[:, 2:3], 0.125, None, ALU.mult), r=["gains"], w=["gains"])
            for v in range(2):
                for k in range(8):
                    P.do("dve", TS(scb[:, v, k, :], ones_f[:], sc[:, k, v:v + 1], None, ALU.mult),
                         r=["sc"], w=["scb"])
            for blk in range(6):
                wb = blk % 2
                wst = wst2[:, wb]
                wkey = ("wst", wb)
                P.dma("sp" if wb == 0 else "act", wst, io.w_ada[:, blk * 512:(blk + 1) * 512].rearrange(
                    "(k p) c -> p k c", p=128), w=[wkey])
                if blk < 4:
                    for mm in range(4):
                        m = blk * 4 + mm
                        for k in range(8):
                            P.do("pe", MM(mps[:, 0:2], wst[:, k, mm * 128:(mm + 1) * 128], sc[:, k, :],
                                          start=(k == 0), stop=(k == 7)),
                                 r=[wkey, "sc"], w=["mps"], sig=(k == 7))
                        P.do("dve", TS(modv[:, m, :], mps[:, 0:2], bfm[:, m:m + 1], None, ALU.add),
                             r=["mps", "bfm"], w=["modv"])
                else:
                    hf = blk - 4
                    for v in range(2):
                        for k in range(8):
                            P.do("pe", MM(gps[:, v, :], scb[:, v, k, :], wst[:, k, :],
                                          start=(k == 0), stop=False),
                                 r=[wkey, "scb"], w=[("gps", v)], sig=False)
                        P.do("pe", MM(gps[:, v, :], ones_f[0:1, :], bgr[0:1, hf * 512:(hf + 1) * 512],
                                      start=False, stop=True), r=["bgr"], w=[("gps", v)])
                        P.do("dve", CP(gate_bc[:, v, hf * 512:(hf + 1) * 512], gps[:, v, :]),
                             r=[("gps", v)], w=["gate_bc"])
            for v in range(2):
                P.do("dve", STT(Amod[:, :, v], modv[:, 8:16, v], 1.0, ng[:], ALU.add, ALU.mult),
                     r=["modv", "ng"], w=["Amod"])
            P.flush()

        with ExitStack() as SB:
            eb_ = SB.enter_context
            KbT = eb_(sb("KbT", [128, 2, NKC * 128], BF16))
            Vb = eb_(sb("Vb", [128, NKC, 258], BF16))
            with ExitStack() as SC:
                ec_ = SC.enter_context
                KaT = ec_(sb("KaT", [128, 4, 18 * 128], BF16))
                Va = ec_(sb("Va", [128, 18, 512], BF16))
                KaTc = ec_(sb("KaTc", [128, 4, 256], BF16))
                Vac = ec_(sb("Vac", [128, 2, 512], BF16))

                with ExitStack() as S1:
                    e1 = S1.enter_context
                    Wak = e1(sb("Wak", [128, 8, 512], BF16))
                    Wav = e1(sb("Wav", [128, 8, 512], BF16))
                    Wbk = e1(sb("Wbk", [128, 8, 2, 128], BF16))
                    Wbv = e1(sb("Wbv", [128, 8, 128], BF16))
                    stg = e1(sb("stg", [128, 1, 8, 128], F32))
                    xs = e1(sb("xs", [128, 2, D], F32))
                    def xn_v(p):
                        return gatedB[:, :, p * D:(p + 1) * D]

                    def hxo_v(p):
                        return lambda k: gatedA[:, k // 2, p * D + (k % 2) * 512:p * D + (k % 2) * 512 + 512]
                    ssq = e1(sb("ssq", [128, 4], F32))
                    lnv = e1(sb("lnv", [128, 4], F32))
                    rstd = e1(sb("rstd", [128, 4], F32))
                    cosb = e1(sb("cosb", [128, 512], F32))
                    sinb = e1(sb("sinb", [128, 512], F32))
                    raw = e1(sb("raw", [128, 1, 512], F32))
                    sq = e1(sb("sq", [128, 1, 512], BF16))
                    lnr = e1(sb("lnr", [128, 1, 512], F32))
                    rsr = lnr
                    kn = e1(sb("kn", [128, 512], BF16))
                    P._pre("dve", [], ["gAscr", "gBscr"] + [("gA", i) for i in range(5)] + [("gB", i) for i in range(5)])
                    P._pre("act", [], ["gAscr", "gBscr"] + [("gA", i) for i in range(5)] + [("gB", i) for i in range(5)])
                    tp = e1(ps("tp", [128, 2, 1024], BF16))
                    pj = e1(ps("pj", [128, 4, 512], F32))
                    aux = e1(ps("aux", [128, 2, 512], F32))

                    def load_w(dst_fn, c0, nblk, tag):
                        for bi in range(nblk):
                            s = 0
                            load_w.cnt += 1
                            P.dma("sp", stg[:, s, :, :], io.w_in[:, c0 + bi * 128:c0 + (bi + 1) * 128]
                                  .rearrange("(k p) c -> p k c", p=128), w=[("stg", s)])
                            dst_fn(bi, s)
                    load_w.cnt = 0

                    def cast_to(dst, tag):
                        def f(bi, s):
                            CAST(P, dst[:, :, bi * 128:(bi + 1) * 128], stg[:, s, :, :], [("stg", s)], [tag])
                        return f
                    load_w(cast_to(Wak, "Wak"), C_AK, 4, "Wak")
                    load_w(cast_to(Wav, "Wav"), C_AV, 4, "Wav")

                    def cast_bk(bi, s):
                        for g in range(2):
                            for half in range(2):
                                CAST(P, Wbk[:, :, g, half * 64:(half + 1) * 64],
                                     stg[:, s, :, g * 64:(g + 1) * 64], [("stg", s)], ["Wbk"])
                    load_w(cast_bk, C_BK, 1, "Wbk")
                    load_w(cast_to(Wbv, "Wbv"), C_BV, 1, "Wbv")

                    tp_slot = [0]
                    P.do("pool", MEMSET(Vb[:], 0.0), w=["Vb"])
                    for oc_ in (0, 128, 256):
                        P.do("pool", MEMSET(Vb[:, :, oc_:oc_ + 1], 1.0), w=["Vb"])

                    def qknorm(src_ps, src_key, N, gain_ap, dst, dst_key, slot, rope=None):
                        aslot = slot
                        slot = 0
                        rs_ = raw[:, slot, :N]
                        P.do("dve", CP(rs_, src_ps), r=[src_key], w=[("raw", slot)])
                        P.do("pool", TT(sq[:, slot, :N], rs_, rs_, ALU.mult), r=[("raw", slot)], w=[("sq", slot)])
                        P.do("pe", MM(aux[:, aslot, :N], blockones[:], sq[:, slot, :N]),
                             r=[("sq", slot)], w=[("aux", aslot)])
                        P.do("act", ACT(lnr[:, slot, :N], aux[:, aslot, :N], AF.Ln, bias=EPS, scale=1.0 / HD),
                             r=[("aux", aslot)], w=[("rsr", slot)])
                        P.do("act", ACT(rsr[:, slot, :N], lnr[:, slot, :N], AF.Exp, scale=-0.5),
                             r=[("rsr", slot)], w=[("rsr", slot)])
                        if rope is None:
                            P.do("dve", STT(dst, rs_, gain_ap, rsr[:, slot, :N], ALU.mult, ALU.mult),
                                 r=[("raw", slot), ("rsr", slot), "gains"], w=[dst_key])
                        else:
                            P.do("dve", STT(kn[:, :N], rs_, gain_ap, rsr[:, slot, :N], ALU.mult, ALU.mult),
                                 r=[("raw", slot), ("rsr", slot), "gains"], w=["kn"])
                            P.do("pe", MM(aux[:, aslot, :N], rotT[:], kn[:, :N]), r=["kn"], w=[("aux", aslot)])
                            t1 = lnr[:, slot, :N]
                            t2 = raw[:, slot, :N]
                            P.do("dve", TT(t1, kn[:, :N], cosb[:, :N], ALU.mult), r=["kn", "cosb"], w=[("rsr", slot)])
                            P.do("dve", TT(t2, aux[:, aslot, :N], sinb[:, :N], ALU.mult),
                                 r=[("aux", aslot), "sinb"], w=[("raw", slot)])
                            P.do("dve", TT(dst, t1, t2, ALU.add), r=[("rsr", slot), ("raw", slot)], w=[dst_key])

                    groups = [("ctx", None, 2, 0, None)]
                    for g in range(4):
                        groups.append(("own", g * 512, 4, 2 + g * 4, g * 4))
                    for g in range(4):
                        groups.append(("oth", 2048 + g * 512, 4, 18 + g * 4, 16 if g == 0 else None))

                    pjc = [0]
                    xsc = [0]

                    def pjslot():
                        s = pjc[0] % 4
                        pjc[0] += 1
                        return s

                    def ginfo(gi):
                        (kind, r0, ntile, kc0, slot0) = groups[gi]
                        p = gi % 2
                        if kind == "own":
                            hdst = lambda k, c0=r0: hxT[:, k, c0:c0 + 512]
                            hkey = ("hxT", r0 // 512)
                        elif kind == "ctx":
                            hdst = lambda k: hxT[:, k, 2048:2304]
                            hkey = ("hxT", 4)
                        else:
                            hdst = hxo_v(p)
                            hkey = ("gAscr", p)
                        return kind, r0, ntile, kc0, slot0, p, hdst, hkey

                    def stageA(gi):
                        kind, r0, ntile, kc0, slot0, p, hdst, hkey = ginfo(gi)
                        xn = xn_v(p)
                        N = ntile * 128
                        v = 1 if kind == "ctx" else 0
                        for t in range(ntile):
                            xb = xsc[0] % 2
                            xsc[0] += 1
                            if kind == "ctx":
                                xsrc, xkeys = ctx_src[t * 128:(t + 1) * 128, :], []
                            else:
                                xsrc, xkeys = x_rows(r0 + t * 128, 128)
                            P.dma("sp", xs[:, xb, :], xsrc, r=xkeys, w=[("xs", xb)], key=("xs", xb))
                            P.do("act", ACT(xn[:, t, :], xs[:, xb, :], AF.Square, accum_out=ssq[:, t:t + 1]),
                                 r=[("xs", xb)], w=[("xn", p, t), "gBscr", ("ssq", t)])
                            P.do("act", ACT(lnv[:, t:t + 1], ssq[:, t:t + 1], AF.Ln, bias=EPS, scale=1.0 / D),
                                 r=[("ssq", t)], w=[("lnv", t)])
                            P.do("act", ACT(rstd[:, t:t + 1], lnv[:, t:t + 1], AF.Exp, scale=-0.5),
                                 r=[("lnv", t)], w=[("rstd", t)])
                            P.do("dve", TS(xn[:, t, :], xs[:, xb, :], rstd[:, t:t + 1], None, ALU.mult),
                                 r=[("xs", xb), ("rstd", t)], w=[("xn", p, t), "gBscr"])
                        for k in range(8):
                            sl = tp_slot[0] % 2
                            tp_slot[0] += 1
                            tps = tp[:, sl, 0:N]
                            for t in range(ntile):
                                P.do("pe", TR(tp[:, sl, t * 128:(t + 1) * 128],
                                              xn[:, t, k * 128:(k + 1) * 128], ident[:]),
                                     r=[("xn", p, t), "gBscr"], w=[("tp", sl)], sig=(t == ntile - 1))
                            P.do("dve", TS(hdst(k), tps, Amod[:, k, v:v + 1], modv[:, k, v:v + 1], ALU.mult, ALU.add),
                                 r=[("tp", sl), "Amod", "modv"], w=[hkey, "gAscr"])

                    def stageB(gi):
                        kind, r0, ntile, kc0, slot0, p, hdst, hkey = ginfo(gi)
                        N = ntile * 128
                        if kind != "ctx":
                            P.dma("sp", cosb[:], cos_t[:, r0:r0 + 512], w=["cosb"])
                            P.dma("sp", sinb[:], sin_t[:, r0:r0 + 512], w=["sinb"])

                        def hx(k):
                            return hdst(k)

                        for g in range(2):
                            s = pjslot()
                            for k in range(8):
                                P.do("pe", MM(pj[:, s, :N], Wbk[:, k, g, :], hx(k), start=(k == 0), stop=(k == 7)),
                                     r=["Wbk", hkey], w=[("pj", s)], sig=(k == 7))
                            qknorm(pj[:, s, :N], ("pj", s), N, gains[:, 3:4],
                                   KbT[:, g, kc0 * 128:kc0 * 128 + N], "KbT", g,
                                   rope=None if kind == "ctx" else True)
                        for t in range(ntile):
                            s = pjslot()
                            for k in range(8):
                                P.do("pe", MM(pj[:, s, 0:128], hx(k)[:, t * 128:(t + 1) * 128], Wbv[:, k, :],
                                              start=(k == 0), stop=(k == 7)),
                                     r=["Wbv", hkey], w=[("pj", s)], sig=(k == 7))
                            P.do("act", ACT(Vb[:, kc0 + t, 64:128], pj[:, s, 0:64], AF.Copy), r=[("pj", s)], w=["Vb"])
                            P.do("dve", CP(Vb[:, kc0 + t, 192:256], pj[:, s, 64:128]), r=[("pj", s)], w=["Vb"])
                        if kind == "ctx":
                            na_tiles = 2
                        elif slot0 is not None:
                            na_tiles = 4 if kind == "own" else 2
                        else:
                            na_tiles = 0
                        if na_tiles:
                            Nn = na_tiles * 128
                            for hp in range(4):
                                s = pjslot()
                                for k in range(8):
                                    P.do("pe", MM(pj[:, s, :Nn], Wak[:, k, hp * 128:(hp + 1) * 128], hx(k)[:, :Nn],
                                                  start=(k == 0), stop=(k == 7)),
                                         r=["Wak", hkey], w=[("pj", s)], sig=(k == 7))
                                if kind == "ctx":
                                    dst = KaTc[:, hp, 0:Nn]
                                    dkey = "KaTc"
                                else:
                                    dst = KaT[:, hp, slot0 * 128:slot0 * 128 + Nn]
                                    dkey = "KaT"
                                qknorm(pj[:, s, :Nn], ("pj", s), Nn, gains[:, 1:2], dst, dkey, hp % 2)
                            for t in range(na_tiles):
                                s = pjslot()
                                for k in range(8):
                                    P.do("pe", MM(pj[:, s, :], hx(k)[:, t * 128:(t + 1) * 128], Wav[:, k, :],
                                                  start=(k == 0), stop=(k == 7)),
                                         r=["Wav", hkey], w=[("pj", s)], sig=(k == 7))
                                if kind == "ctx":
                                    P.do("act", ACT(Vac[:, t, :], pj[:, s, :], AF.Copy), r=[("pj", s)], w=["Vac"])
                                else:
                                    P.do("act", ACT(Va[:, slot0 + t, :], pj[:, s, :], AF.Copy), r=[("pj", s)], w=["Va"])
                    stageA(0)
                    for gi in range(len(groups)):
                        if gi + 1 < len(groups):
                            stageA(gi + 1)
                        stageB(gi)
                    if dbg is not None:
                        dbg(P, "KbT", KbT[:], ["KbT"])
                        dbg(P, "Vb", Vb[:], ["Vb"])
                        dbg(P, "KaT", KaT[:], ["KaT"])
                        dbg(P, "Va", Va[:], ["Va"])
                        dbg(P, "hxT", hxT[:], [("hxT", i) for i in range(5)])
                    P.flush()

                with ExitStack() as S2:
                    e2 = S2.enter_context
                    Wq = e2(sb("Waq", [128, 8, 512], BF16))
                    Wz = e2(sb("Waz", [128, 8, 512], BF16))
                    stg = e2(sb("stg2", [128, 1, 8, 128], F32))
                    Eb2 = e2(sb("Eb", [128, 8, 640], BF16))
                    est = e2(sb("est", [128, 1, 640], F32))
                    raw = e2(sb("raw2", [128, 512], F32))
                    sq = e2(sb("sq2", [128, 512], BF16))
                    lnr = e2(sb("lnr2", [128, 512], F32))
                    rsr = lnr
                    Pt = e2(sb("Pt2", [128, 2, 1024], BF16))
                    ebuf = e2(sb("ebuf2", [128, 512], F32))
                    den = ebuf
                    rden = ebuf
                    tz = ebuf
                    P._pre("dve", [], ["gAscr", ("gAscr", 0), ("gAscr", 1)])
                    P._pre("act", [], ["gBscr"])

                    def Eb(cfg, h):
                        if cfg < 2:
                            return gatedB[:, 2 * cfg + h // 4, (h % 4) * 512:(h % 4 + 1) * 512]
                        return Eb2[:, h, :]
                    Sps = e2(ps("Sps", [128, 2, 1024], F32))
                    PV = e2(ps("PV", [128, 512], F32))
                    SM = e2(ps("SM", [128, 512], F32))
                    qps = e2(ps("qps", [128, 512], F32))
                    aps = e2(ps("aps", [128, 512], F32))

                    cnt = [0]

                    def load_cast(dst, c0, nblk, tag):
                        for bi in range(nblk):
                            s = 0
                            cnt[0] += 1
                            P.dma("sp", stg[:, s, :, :], io.w_in[:, c0 + bi * 128:c0 + (bi + 1) * 128]
                                  .rearrange("(k p) c -> p k c", p=128), w=[("stg", s)])
                            CAST(P, dst[:, :, bi * 128:(bi + 1) * 128], stg[:, s, :, :], [("stg", s)], [tag])
                    load_cast(Wq, C_AQ, 4, "Wq")
                    load_cast(Wz, C_AZ, 4, "Wz")
                    ec = 0
                    for c in range(3):
                        for h in range(8):
                            s = 0
                            ne = 512 if c < 2 else 640
                            P.dma("sp", est[:, s, :], io.ebias[c, h, :, :], w=[("est", s)])
                            P.do("act", ACT(Eb(c, h), est[:, s, 0:ne], AF.Exp), r=[("est", s)], w=[("Eb", c), "gBscr"])

                    units = [(grp, hp) for grp in qgroups for hp in range(4)]
                    qn2 = e2(sb("qn2b", [128, 2, 512], BF16))

                    def prep_steps(ui):
                        (gname, c0, ntile, is_ctx), hp = units[ui]
                        N = ntile * 128
                        gi = 4 if is_ctx else c0 // 512
                        hkey = ("hxT", gi)
                        qb = ui % 2

                        def s0():
                            for k in range(8):
                                P.do("pe", MM(qps[:, :N], Wq[:, k, hp * 128:(hp + 1) * 128], hxT[:, k, c0:c0 + N],
                                              start=(k == 0), stop=(k == 7)),
                                     r=["Wq", hkey], w=["qps"], sig=(k == 7))
                            P.do("dve", CP(raw[:, :N], qps[:, :N]), r=["qps"], w=["raw"])
                            P.do("pool", TT(sq[:, :N], raw[:, :N], raw[:, :N], ALU.mult), r=["raw"], w=["sq"])

                        def s1():
                            P.do("pe", MM(qps[:, :N], blockones[:], sq[:, :N]), r=["sq"], w=["qps"])
                            P.do("act", ACT(lnr[:, :N], qps[:, :N], AF.Ln, bias=EPS, scale=1.0 / HD), r=["qps"], w=["rsr"])
                            P.do("act", ACT(rsr[:, :N], lnr[:, :N], AF.Exp, scale=-0.5), r=["rsr"], w=["rsr"])
                            P.do("dve", STT(qn2[:, qb, :N], raw[:, :N], gains[:, 0:1], rsr[:, :N], ALU.mult, ALU.mult),
                                 r=["raw", "rsr", "gains"], w=[("qn", qb)])
                        return [s0, s1]

                    sbuf_i = [0]
                    for st_ in prep_steps(0):
                        st_()
                    for ui, ((gname, c0, ntile, is_ctx), hp) in enumerate(units):
                        N = ntile * 128
                        gi = 4 if is_ctx else c0 // 512
                        hkey = ("hxT", gi)
                        qb = ui % 2
                        nxt = prep_steps(ui + 1) if ui + 1 < len(units) else []
                        for k in range(8):
                            P.do("pe", MM(aps[:, :N], Wz[:, k, hp * 128:(hp + 1) * 128], hxT[:, k, c0:c0 + N],
                                          start=(k == 0), stop=(k == 7)),
                                 r=["Wz", hkey], w=["aps"], sig=(k == 7))
                        items = [(tq, par) for tq in range(ntile) for par in range(2)]

                        def item_info(it):
                            tq, par = it
                            if is_ctx:
                                return [("c", 0), ("c", 1)], None, 0
                            T = c0 // 128 + tq
                            s0_ = max(T - 2, 0)
                            nwin = 4 if T < 2 else 5
                            return ([("w", s0_ + i) for i in range(nwin)] + [("c", 0), ("c", 1)]), min(T, 2), nwin

                        def emit_S(it, bi_):
                            tq, par = it
                            chunks, cfg, nwin = item_info(it)
                            nch = len(chunks)
                            h = hp * 2 + par
                            hb = par * 64
                            for ci, (ck, cs) in enumerate(chunks):
                                lhs = (KaT[hb:hb + 64, hp, cs * 128:(cs + 1) * 128] if ck == "w"
                                       else KaTc[hb:hb + 64, hp, cs * 128:(cs + 1) * 128])
                                P.do("pe", MM(Sps[:, bi_, ci * 128:(ci + 1) * 128], lhs,
                                              qn2[hb:hb + 64, qb, tq * 128:(tq + 1) * 128]),
                                     r=["KaT", "KaTc", ("qn", qb)], w=[("S", bi_)], sig=(ci == nch - 1))
                            P.do("act", ACT(Pt[:, bi_, 0:nch * 128], Sps[:, bi_, 0:nch * 128], AF.Exp),
                                 r=[("S", bi_)], w=[("Pt", bi_)])
                            if not is_ctx:
                                P.do("dve", TT(Pt[:, bi_, 0:nwin * 128], Pt[:, bi_, 0:nwin * 128], Eb(cfg, h), ALU.mult),
                                     r=[("Pt", bi_), ("Eb", cfg), "gBscr"], w=[("Pt", bi_)])

                        def emit_PV(it, bi_):
                            tq, par = it
                            chunks, cfg, nwin = item_info(it)
                            nch = len(chunks)
                            h = hp * 2 + par
                            hb = par * 64
                            for ci, (ck, cs) in enumerate(chunks):
                                vv = (Va[:, cs, h * 64:(h + 1) * 64] if ck == "w" else Vac[:, cs, h * 64:(h + 1) * 64])
                                P.do("pe", MM(PV[hb:hb + 64, tq * 128:(tq + 1) * 128], vv,
                                              Pt[:, bi_, ci * 128:(ci + 1) * 128],
                                              start=(ci == 0), stop=(ci == nch - 1)),
                                     r=["Va", "Vac", ("Pt", bi_)], w=["PV"], sig=(ci == nch - 1))
                            for ci in range(nch):
                                P.do("pe", MM(SM[hb:hb + 64, tq * 128:(tq + 1) * 128], ones64[:],
                                              Pt[:, bi_, ci * 128:(ci + 1) * 128],
                                              start=(ci == 0), stop=(ci == nch - 1)),
                                     r=[("Pt", bi_)], w=["SM"], sig=(ci == nch - 1))

                        base = sbuf_i[0]
                        sbuf_i[0] += len(items)
                        emit_S(items[0], base % 2)
                        nstep = 0
                        for ii, it in enumerate(items):
                            if ii + 1 < len(items):
                                emit_S(items[ii + 1], (base + ii + 1) % 2)
                            emit_PV(it, (base + ii) % 2)
                            if nstep < len(nxt) and (ii % 3 == 1 or ii == len(items) - 1):
                                nxt[nstep]()
                                nstep += 1
                        while nstep < len(nxt):
                            nxt[nstep]()
                            nstep += 1
                        P.do("act", ACT(ebuf[:, :N], aps[:, :N], AF.Exp, scale=-1.0), r=["aps"], w=["ebuf"])
                        P.do("dve", STT(den[:, :N], ebuf[:, :N], 1.0, SM[:, :N], ALU.add, ALU.mult),
                             r=["ebuf", "SM"], w=["ebuf"])
                        P.do("dve", RCP(rden[:, :N], den[:, :N]), r=["ebuf"], w=["ebuf"])
                        P.do("dve", TT(tz[:, :N], aps[:, :N], rden[:, :N], ALU.mult), r=["aps", "ebuf"], w=["ebuf"])
                        P.do("dve", TT(gatedA[:, hp, c0:c0 + N], PV[:, :N], tz[:, :N], ALU.mult),
                             r=["PV", "ebuf"], w=[("gA", gi)])
                    if dbg is not None:
                        dbg(P, "gatedA", gatedA[:], [("gA", i) for i in range(5)])
                    P.flush()

            with ExitStack() as S3:
                e3 = S3.enter_context
                Wq = e3(sb("Wbq", [128, 8, 512], BF16))
                Wz = e3(sb("Wbz", [128, 8, 512], BF16))
                stg = e3(sb("stg3", [128, 2, 8, 128], F32))
                raw = e3(sb("raw3", [128, 512], F32))
                sq = e3(sb("sq3", [128, 512], BF16))
                lnr = e3(sb("lnr3", [128, 512], F32))
                rsr = lnr
                P._pre("dve", [], ["gBscr"])
                qn = e3(sb("qn3", [128, 512], BF16))
                qr = e3(sb("qr3", [128, 2, 512], BF16))
                cosb = e3(sb("cosb3", [128, 2, 512], F32))
                sinb = e3(sb("sinb3", [128, 2, 512], F32))
                t1 = e3(sb("t13", [128, 512], F32))
                t2 = e3(sb("t23", [128, 512], F32))
                Pt = e3(sb("Pt3", [128, 2, 1024], BF16))
                ebuf = e3(sb("ebuf3", [128, 512], F32))
                den = ebuf
                rden = ebuf
                tz = ebuf
                Sps = e3(ps("Sps3", [128, 2, 1024], F32))
                PVa = e3(ps("PV3", [128, 512], F32))
                PVb = e3(ps("SM3", [128, 512], F32))
                srow = e3(sb("srow3", [128, 512], F32))
                P.do("pool", MEMSET(srow[:], 0.0), w=["srow"])
                qps = e3(ps("qps3", [128, 512], F32))
                aps = e3(ps("aps3", [128, 512], F32))

                cnt = [0]

                def load_cast3(dst, c0, nblk, tag):
                    for bi in range(nblk):
                        s = cnt[0] % 2
                        cnt[0] += 1
                        P.dma("sp" if s == 0 else "act", stg[:, s, :, :], io.w_in[:, c0 + bi * 128:c0 + (bi + 1) * 128]
                              .rearrange("(k p) c -> p k c", p=128), w=[("stg", s)])
                        CAST(P, dst[:, :, bi * 128:(bi + 1) * 128], stg[:, s, :, :], [("stg", s)], [tag])
                load_cast3(Wq, C_BQ, 4, "Wq")
                load_cast3(Wz, C_BZ, 4, "Wz")

                units = [(grp, hp) for grp in qgroups for hp in range(4)]

                def prep_steps(ui):
                    (gname, c0, ntile, is_ctx), hp = units[ui]
                    N = ntile * 128
                    gi = 4 if is_ctx else c0 // 512
                    hkey = ("hxT", gi)
                    qb = ui % 2
                    cb_ = gi % 2
                    qrd = qr[:, qb, :N]

                    def s0():
                        if (not is_ctx) and hp == 0:
                            P.dma("sp", cosb[:, cb_, :], cos_t[:, c0:c0 + 512], w=[("cosb", cb_)])
                            P.dma("sp", sinb[:, cb_, :], sin_t[:, c0:c0 + 512], w=[("sinb", cb_)])
                        for k in range(8):
                            P.do("pe", MM(qps[:, :N], Wq[:, k, hp * 128:(hp + 1) * 128], hxT[:, k, c0:c0 + N],
                                          start=(k == 0), stop=(k == 7)),
                                 r=["Wq", hkey], w=["qps"], sig=(k == 7))
                        P.do("dve", CP(raw[:, :N], qps[:, :N]), r=["qps"], w=["raw"])
                        P.do("pool", TT(sq[:, :N], raw[:, :N], raw[:, :N], ALU.mult), r=["raw"], w=["sq"])

                    def s1():
                        P.do("pe", MM(qps[:, :N], blockones[:], sq[:, :N]), r=["sq"], w=["qps"])
                        P.do("act", ACT(lnr[:, :N], qps[:, :N], AF.Ln, bias=EPS, scale=1.0 / HD), r=["qps"], w=["rsr"])
                        P.do("act", ACT(rsr[:, :N], lnr[:, :N], AF.Exp, scale=-0.5), r=["rsr"], w=["rsr"])
                        if is_ctx:
                            P.do("dve", STT(qrd, raw[:, :N], gains[:, 2:3], rsr[:, :N], ALU.mult, ALU.mult),
                                 r=["raw", "rsr", "gains"], w=[("qr", qb)])
                        else:
                            P.do("dve", STT(qn[:, :N], raw[:, :N], gains[:, 2:3], rsr[:, :N], ALU.mult, ALU.mult),
                                 r=["raw", "rsr", "gains"], w=["qn"])

                    def s2():
                        if not is_ctx:
                            P.do("pe", MM(qps[:, :N], rotT[:], qn[:, :N]), r=["qn"], w=["qps"])
                            P.do("dve", TT(t1[:, :N], qn[:, :N], cosb[:, cb_, :N], ALU.mult),
                                 r=["qn", ("cosb", cb_)], w=["t1"])
                            P.do("dve", TT(t2[:, :N], qps[:, :N], sinb[:, cb_, :N], ALU.mult),
                                 r=["qps", ("sinb", cb_)], w=["t2"])
                            P.do("dve", TT(qrd, t1[:, :N], t2[:, :N], ALU.add), r=["t1", "t2"], w=[("qr", qb)])
                    return [s0, s1, s2]

                sbuf_i = [0]
                for st_ in prep_steps(0):
                    st_()
                for ui, ((gname, c0, ntile, is_ctx), hp) in enumerate(units):
                    N = ntile * 128
                    gi = 4 if is_ctx else c0 // 512
                    hkey = ("hxT", gi)
                    g = hp // 2
                    qb = ui % 2
                    nxt = prep_steps(ui + 1) if ui + 1 < len(units) else []
                    for k in range(8):
                        P.do("pe", MM(aps[:, :N], Wz[:, k, hp * 128:(hp + 1) * 128], hxT[:, k, c0:c0 + N],
                                      start=(k == 0), stop=(k == 7)),
                             r=["Wz", hkey], w=["aps"], sig=(k == 7))
                    chunks = [0, 1] if is_ctx else list(range(NKC))
                    nblk = len(chunks) // 2
                    items = [(par, bk) for par in range(2) for bk in range(nblk)]

                    def emit_S(it, bi_):
                        par, bk = it
                        hb = par * 64
                        for j in range(2):
                            ch = chunks[bk * 2 + j]
                            P.do("pe", MM(Sps[:, bi_, j * 512:j * 512 + N],
                                          KbT[hb:hb + 64, g, ch * 128:(ch + 1) * 128],
                                          qr[hb:hb + 64, qb, :N]),
                                 r=["KbT", ("qr", qb)], w=[("S", bi_)], sig=(j == 1))
                        if N == 512:
                            P.do("act", ACT(Pt[:, bi_, :], Sps[:, bi_, :], AF.Exp), r=[("S", bi_)], w=[("Pt", bi_)])
                        else:
                            for j in range(2):
                                P.do("act", ACT(Pt[:, bi_, j * 512:j * 512 + N], Sps[:, bi_, j * 512:j * 512 + N], AF.Exp),
                                     r=[("S", bi_)], w=[("Pt", bi_)])

                    def emit_PV(it, bi_):
                        par, bk = it
                        hb = par * 64
                        for j in range(2):
                            ch = chunks[bk * 2 + j]
                            first = (bk == 0 and j == 0)
                            last = (bk == nblk - 1 and j == 1)
                            if par == 0:
                                P.do("pe", MM(PVa[0:65, :N], Vb[:, ch, 64 + 128 * g:64 + 128 * g + 65],
                                              Pt[:, bi_, j * 512:j * 512 + N], start=first, stop=last),
                                     r=["Vb", ("Pt", bi_)], w=["PVa"], sig=(j == 1))
                            else:
                                P.do("pe", MM(PVb[:, :N], Vb[:, ch, 128 * g:128 * g + 128],
                                              Pt[:, bi_, j * 512:j * 512 + N], start=first, stop=last),
                                     r=["Vb", ("Pt", bi_)], w=["PVb"], sig=(j == 1))

                    base = sbuf_i[0]
                    sbuf_i[0] += len(items)
                    emit_S(items[0], base % 2)
                    nstep = 0
                    for ii, it in enumerate(items):
                        if ii + 1 < len(items):
                            emit_S(items[ii + 1], (base + ii + 1) % 2)
                        emit_PV(it, (base + ii) % 2)
                        if nstep < len(nxt) and (ii % 3 == 1 or ii == len(items) - 1):
                            nxt[nstep]()
                            nstep += 1
                    while nstep < len(nxt):
                        nxt[nstep]()
                        nstep += 1
                    P.do("act", ACT(ebuf[:, :N], aps[:, :N], AF.Exp, scale=-1.0), r=["aps"], w=["ebuf"])
                    P.do("dve", CP(srow[64:65, :N], PVa[64:65, :N]), r=["PVa"], w=["srow"])
                    P.do("dve", CP(srow[0:1, :N], PVb[0:1, :N]), r=["PVb"], w=["srow"])
                    P.do("pe", MM(qps[:, :N], selT[:], srow[:, :N]), r=["srow"], w=["qps"])
                    P.do("dve", STT(den[:, :N], ebuf[:, :N], 1.0, qps[:, :N], ALU.add, ALU.mult),
                         r=["ebuf", "qps"], w=["ebuf"])
                    P.do("dve", RCP(rden[:, :N], den[:, :N]), r=["ebuf"], w=["ebuf"])
                    P.do("dve", TT(tz[:, :N], aps[:, :N], rden[:, :N], ALU.mult), r=["aps", "ebuf"], w=["ebuf"])
                    P.do("dve", TT(gatedB[0:64, hp, c0:c0 + N], PVa[0:64, :N], tz[0:64, :N], ALU.mult),
                         r=["PVa", "ebuf"], w=[("gB", gi)])
                    P.do("dve", TT(gatedB[64:128, hp, c0:c0 + N], PVb[64:128, :N], tz[64:128, :N], ALU.mult),
                         r=["PVb", "ebuf"], w=[("gB", gi)])
                P.flush()

        with ExitStack() as S4:
            e4 = S4.enter_context
            Wga = e4(sb("Wga", [128, 8, D], BF16))
            Wgb = e4(sb("Wgb", [128, 8, D], BF16))
            Woa = e4(sb("Woa", [128, 4, D], BF16))
            Wob = e4(sb("Wob", [128, 4, D], BF16))
            Wo = e4(sb("Wo", [128, 8, D], BF16))
            stg = e4(sb("stg4", [128, 2, 8, 128], F32))
            sga = e4(sb("sga", [128, 512], F32))
            sgb = e4(sb("sgb", [128, 512], F32))
            ta = e4(sb("ta", [128, 512], F32))
            tb = e4(sb("tb", [128, 512], F32))
            mg = e4(sb("mg", [128, 8, 512], BF16))
            xres = e4(sb("xres", [128, 2, D], F32))
            xo = e4(sb("xo", [128, 2, D], F32))
            tmo = e4(sb("tmo", [128, D], F32))
            oa = e4(ps("oa", [128, 512], F32))
            ob = e4(ps("ob", [128, 512], F32))
            ga = e4(ps("ga", [128, 512], F32))
            gb = e4(ps("gb", [128, 512], F32))
            ops_ = e4(ps("ops", [128, 2, 2, 512], F32))

            cnt = [0]

            def load_cast4(dst, src, kch, c0, nblk, tag):
                for bi in range(nblk):
                    s = cnt[0] % 2
                    cnt[0] += 1
                    P.dma("sp" if s == 0 else "act", stg[:, s, 0:kch, :], src[:, c0 + bi * 128:c0 + (bi + 1) * 128]
                          .rearrange("(k p) c -> p k c", p=128), w=[("stg", s)])
                    CAST(P, dst[:, :, bi * 128:(bi + 1) * 128], stg[:, s, 0:kch, :], [("stg", s)], [tag])
            load_cast4(Wga, io.w_in, 8, C_GA, 8, "Wga")
            load_cast4(Wgb, io.w_in, 8, C_GB, 8, "Wgb")
            load_cast4(Woa, io.w_o_a, 4, 0, 8, "Woa")
            load_cast4(Wob, io.w_o_b, 4, 0, 8, "Wob")
            load_cast4(Wo, io.w_out, 8, 0, 8, "Wo")

            oc = [0]
            for (gname, c0, ntile, is_ctx) in qgroups:
                N = ntile * 128
                gi = 4 if is_ctx else c0 // 512
                hkey = ("hxT", gi)
                v = 1 if is_ctx else 0
                for c in range(8):
                    for k in range(8):
                        P.do("pe", MM(ga[:, :N], Wga[:, k, c * 128:(c + 1) * 128], hxT[:, k, c0:c0 + N],
                                      start=(k == 0), stop=(k == 7)),
                             r=["Wga", hkey], w=["ga"], sig=(k == 7))
                    for k in range(8):
                        P.do("pe", MM(gb[:, :N], Wgb[:, k, c * 128:(c + 1) * 128], hxT[:, k, c0:c0 + N],
                                      start=(k == 0), stop=(k == 7)),
                             r=["Wgb", hkey], w=["gb"], sig=(k == 7))
                    for hp in range(4):
                        P.do("pe", MM(oa[:, :N], Woa[:, hp, c * 128:(c + 1) * 128], gatedA[:, hp, c0:c0 + N],
                                      start=(hp == 0), stop=(hp == 3)),
                             r=["Woa", ("gA", gi)], w=["oa"], sig=(hp == 3))
                    for hp in range(4):
                        P.do("pe", MM(ob[:, :N], Wob[:, hp, c * 128:(c + 1) * 128], gatedB[:, hp, c0:c0 + N],
                                      start=(hp == 0), stop=(hp == 3)),
                             r=["Wob", ("gB", gi)], w=["ob"], sig=(hp == 3))
                    P.do("act", ACT(sga[:, :N], ga[:, :N], AF.Sigmoid), r=["ga"], w=["sga"])
                    P.do("act", ACT(sgb[:, :N], gb[:, :N], AF.Sigmoid), r=["gb"], w=["sgb"])
                    P.do("dve", TT(ta[:, :N], oa[:, :N], sga[:, :N], ALU.mult), r=["oa", "sga"], w=["ta"])
                    P.do("dve", TT(tb[:, :N], ob[:, :N], sgb[:, :N], ALU.mult), r=["ob", "sgb"], w=["tb"])
                    P.do("dve", TT(mg[:, c, :N], ta[:, :N], tb[:, :N], ALU.add), r=["ta", "tb"], w=["mg"])
                for t in range(ntile):
                    s = oc[0] % 2
                    oc[0] += 1
                    if is_ctx:
                        rsrc = ctx_src[t * 128:(t + 1) * 128, :]
                        dst = ctx_dst[t * 128:(t + 1) * 128, :]
                    else:
                        rsrc = x_rows(c0 + t * 128, 128)[0]
                        dst = x_dst[c0 + t * 128:c0 + (t + 1) * 128, :]
                    P.dma("sp", xres[:, s, :], rsrc, w=[("xres", s)])
                    for hf in range(2):
                        for c in range(8):
                            P.do("pe", MM(ops_[:, s, hf, :], mg[:, c, t * 128:(t + 1) * 128],
                                          Wo[:, c, hf * 512:(hf + 1) * 512], start=(c == 0), stop=(c == 7)),
                                 r=["mg", "Wo"], w=[("ops", s, hf)], sig=(c == 7))
                    for hf in range(2):
                        P.do("dve", TT(tmo[:, hf * 512:(hf + 1) * 512], ops_[:, s, hf, :],
                                       gate_bc[:, v, hf * 512:(hf + 1) * 512], ALU.mult),
                             r=[("ops", s, hf), "gate_bc"], w=["tmo"])
                    P.do("dve", TT(xo[:, s, :], tmo[:], xres[:, s, :], ALU.add), r=["tmo", ("xres", s)], w=[("xo", s)])
                    P.dma("pool", dst, xo[:, s, :], r=[("xo", s)], key=("xo", s))
                if after_group is not None and not is_ctx:
                    for s in range(2):
                        for t in list(P._st(("xo", s))["r"].values()):
                            P._wait("pool", t)
                    after_group(gi)
            for s in range(2):
                st = P._st(("xo", s))
                for t in list(st["r"].values()):
                    P._wait("pool", t)
            P.flush()


def build_program(mode):
    nc = bass.Bass("TRN2", target_bir_lowering=False)
    dt = nc.dram_tensor
    x_src = dt("x_prog", [SEQ, D], F32, kind="ExternalInput").ap()
    ctx_src = dt("ctx_in", [CTX, D], F32, kind="ExternalInput").ap()
    cvec = dt("cvec", [128, 8, 2], F32, kind="ExternalInput").ap()
    cos_t = dt("cos_t", [128, SEQ], F32, kind="ExternalInput").ap()
    sin_t = dt("sin_t", [128, SEQ], F32, kind="ExternalInput").ap()
    c_ident = dt("c_ident", [128, 128], BF16, kind="ExternalInput").ap()
    c_bones = dt("c_bones", [128, 128], BF16, kind="ExternalInput").ap()
    c_rotT = dt("c_rotT", [128, 128], BF16, kind="ExternalInput").ap()
    c_sel = dt("c_sel", [128, 128], F32, kind="ExternalInput").ap()
    fused = (mode == "fused")
    if fused:
        io0 = declare_layer_inputs(nc, "_0")
        io1 = declare_layer_inputs(nc, "_1")
        selm_in = dt("selm", [128, 2], F32, kind="ExternalInput").ap()
        xmid = dt("xmid", [2048, D], F32).ap()
        ctxmid = dt("ctxmid", [CTX, D], F32).ap()
        xgath = [dt("xgath%d" % i, [1024, D], F32).ap() for i in range(4)]
        xoth = dt("xoth", [2048, D], F32).ap()
    else:
        io = declare_layer_inputs(nc, "")
    x_dst = dt("xo_out", [2048, D], F32, kind="ExternalOutput").ap()
    ctx_dst = dt("ctxo_out", [CTX, D], F32, kind="ExternalOutput").ap() if mode == "layer0" else None

    with ExitStack() as stack:
        ent = stack.enter_context
        P = Prog(nc, stack)
        ident = ent(nc.sbuf_tensor("ident", [128, 128], BF16))
        blockones = ent(nc.sbuf_tensor("blockones", [128, 128], BF16))
        rotT = ent(nc.sbuf_tensor("rotT", [128, 128], BF16))
        ones64 = ent(nc.sbuf_tensor("ones64", [128, 64], BF16))
        ones_f = ent(nc.sbuf_tensor("ones_f", [128, 128], F32))
        selT = ent(nc.sbuf_tensor("selT", [128, 128], F32))
        P.dma("sp", selT[:], c_sel, w=["selT"])
        P.dma("sp", ident[:], c_ident, w=["ident"])
        P.dma("sp", blockones[:], c_bones, w=["blockones"])
        P.dma("sp", rotT[:], c_rotT, w=["rotT"])
        P.do("pool", MEMSET(ones64[:], 1.0), w=["ones64"])
        P.do("pool", MEMSET(ones_f[:], 1.0), w=["ones_f"])
        for e in ("pe", "act", "dve", "pool"):
            P._pre(e, ["ident", "blockones", "rotT", "ones64", "ones_f", "selT"], [])
        P.flush()
        cst = (ident, blockones, rotT, ones64, ones_f, selT)

        def rows_in(r0, n):
            return x_src[r0:r0 + n, :], []

        if not fused:
            emit_layer(P, nc, cst, io, rows_in, ctx_src, cvec, cos_t, sin_t, x_dst, ctx_dst,
                       mode == "layer0")
            return nc

        def cc(i):
            return lambda e: e.collective_compute(
                "AllGather", ALU.bypass, replica_groups=[[0, 1], [2, 3], [4, 5], [6, 7]],
                ins=[xmid[i * 512:(i + 1) * 512, :]], outs=[xgath[i]])

        def after_group(gi):
            P.do("pool", cc(gi), w=[("xgath", gi)])

        emit_layer(P, nc, cst, io0, rows_in, ctx_src, cvec, cos_t, sin_t, xmid, ctxmid, True, uid="a",
                   after_group=after_group)

        with ExitStack() as SX:
            ex = SX.enter_context
            ca = ex(nc.sbuf_tensor("x_ca", [128, 2, D], F32))
            cb = ex(nc.sbuf_tensor("x_cb", [128, 2, D], F32))
            oo = ex(nc.sbuf_tensor("x_oo", [128, 2, D], F32))
            selm = ex(nc.sbuf_tensor("x_selm", [128, 2], F32))
            P.dma("sp", selm[:], selm_in, w=["selm"])
            for U in range(15, -1, -1):
                s_ = U % 2
                pt = 15 - U
                ci, cj = pt // 4, pt % 4
                P.dma("sp", ca[:, s_, :], xgath[ci][cj * 128:(cj + 1) * 128, :], r=[("xgath", ci)],
                      w=[("ca", s_)], key=("ca", s_))
                P.dma("act", cb[:, s_, :], xgath[ci][512 + cj * 128:512 + (cj + 1) * 128, :], r=[("xgath", ci)],
                      w=[("cb", s_)], key=("cb", s_))
                P.do("dve", TS(cb[:, s_, :], cb[:, s_, :], selm[:, 1:2], None, ALU.mult),
                     r=[("cb", s_), "selm"], w=[("cb", s_)])
                P.do("dve", STT(oo[:, s_, :], ca[:, s_, :], selm[:, 0:1], cb[:, s_, :], ALU.mult, ALU.add),
                     r=[("ca", s_), ("cb", s_), "selm"], w=[("oo", s_)])
                P.dma("pool", xoth[U * 128:(U + 1) * 128, :], oo[:, s_, :], r=[("oo", s_)], w=[("xoth", U)],
                      key=("oo", s_))
            for s_ in range(2):
                for t in list(P._st(("oo", s_))["r"].values()):
                    P._wait("pool", t)
            P.flush()

        def rows_mid(r0, n):
            if r0 < 2048:
                return xmid[r0:r0 + n, :], []
            U0 = (r0 - 2048) // 128
            return xoth[r0 - 2048:r0 - 2048 + n, :], [("xoth", U0 + i) for i in range(n // 128)]

        emit_layer(P, nc, cst, io1, rows_mid, ctxmid, cvec, cos_t, sin_t, x_dst, None, False, uid="b")
    return nc


def _tile_order(hf):
    return list(range(32)) if hf == 0 else list(range(31, -1, -1))


def _rope_tables(hf):
    gl = _tile_order(hf)
    t = np.concatenate([np.arange(g * 128, (g + 1) * 128) for g in gl]).astype(np.int32)
    pos_row = (t // 64).astype(np.float32)
    pos_col = (t % 64).astype(np.float32)
    half = 16
    inv = (1.0 / (np.float32(10000.0) ** (np.arange(half, dtype=np.float32) / np.float32(half)))).astype(np.float32)
    ar = pos_row[:, None] * inv[None, :]
    ac = pos_col[:, None] * inv[None, :]
    cos64 = np.concatenate([np.cos(ar), np.cos(ar), np.cos(ac), np.cos(ac)], axis=1).astype(np.float32)
    sin64 = np.concatenate([np.sin(ar), np.sin(ar), np.sin(ac), np.sin(ac)], axis=1).astype(np.float32)
    cos_t = np.ascontiguousarray(np.tile(cos64.T, (2, 1)))
    sin_t = np.ascontiguousarray(np.tile(sin64.T, (2, 1)))
    return cos_t, sin_t


def _consts():
    ident = np.eye(128, dtype=np.float32)
    bones = np.zeros((128, 128), np.float32)
    bones[:64, :64] = 1.0
    bones[64:, 64:] = 1.0
    R = np.zeros((64, 64), np.float32)
    for base in (0, 32):
        for i in range(16):
            R[base + i, base + i + 16] = -1.0
            R[base + 16 + i, base + i] = 1.0
    R2 = np.zeros((128, 128), np.float32)
    R2[:64, :64] = R
    R2[64:, 64:] = R
    bf = ml_dtypes.bfloat16
    return ident.astype(bf), bones.astype(bf), np.ascontiguousarray(R2.T).astype(bf)


def _sel_const():
    sel = np.zeros((128, 128), np.float32)
    sel[64, 0:64] = 1.0
    sel[0, 64:128] = 1.0
    return sel


def _ebias(rpb_l, hf):
    out = np.full((3, 8, 128, 5, 128), NEG, np.float32)
    kk = np.arange(128)
    a = kk // 64
    kc = kk % 64
    qq = np.arange(128)
    b = qq // 64
    qc = qq % 64
    cs = np.clip(qc - 8, 0, 48)
    colvalid = (kc[:, None] >= cs[None, :]) & (kc[:, None] < cs[None, :] + 16)
    co = kc[:, None] - qc[None, :] + 15
    for c in range(3):
        T = c
        s0 = max(T - 2, 0)
        j = T if hf == 0 else 31 - T
        qr = 2 * j + b
        rs = np.clip(qr - 4, 0, 56)
        for i in range(5):
            slot = s0 + i
            p = slot if hf == 0 else 31 - slot
            kr = 2 * p + a
            rowvalid = (kr[:, None] >= rs[None, :]) & (kr[:, None] < rs[None, :] + 8)
            ro = kr[:, None] - qr[None, :] + 7
            valid = rowvalid & colvalid
            roc = np.clip(ro, 0, 14)
            coc = np.clip(co, 0, 30)
            vals = rpb_l[:, roc, coc]
            out[c, :, :, i, :] = np.where(valid[None], vals, np.float32(NEG))
    return np.ascontiguousarray(out.reshape(3, 8, 128, 640))


def _layer_maps(l, hf, w_ada, b_ada, norm_g, w_in, q_norm_a, k_norm_a, q_norm_b, k_norm_b,
                rpb, w_o_a, w_o_b, w_out, sfx=""):
    f = np.float32
    gains = np.stack([np.tile(q_norm_a[l], 2), np.tile(k_norm_a[l], 2),
                      np.tile(q_norm_b[l], 2), np.tile(k_norm_b[l], 2)], axis=1).astype(f)
    return {
        "w_ada" + sfx: np.ascontiguousarray(w_ada[l]),
        "b_ada_fm" + sfx: np.ascontiguousarray(b_ada[l].reshape(24, 128).T),
        "b_gate" + sfx: np.ascontiguousarray(b_ada[l][None, 2048:3072]),
        "norm_g" + sfx: np.ascontiguousarray(norm_g[l].reshape(8, 128).T),
        "w_in" + sfx: np.ascontiguousarray(w_in[l]),
        "gains" + sfx: np.ascontiguousarray(gains),
        "ebias" + sfx: _ebias(rpb[l], hf),
        "w_o_a" + sfx: np.ascontiguousarray(w_o_a[l]),
        "w_o_b" + sfx: np.ascontiguousarray(w_o_b[l]),
        "w_out" + sfx: np.ascontiguousarray(w_out[l]),
    }


_PROG_CACHE = {}


def _get_prog(mode):
    if mode not in _PROG_CACHE:
        _PROG_CACHE[mode] = build_program(mode)
    return _PROG_CACHE[mode]


def _prog_order(xb, hf):
    t = xb.reshape(32, 128, D)
    if hf == 1:
        t = t[::-1]
    return np.ascontiguousarray(t.reshape(SEQ, D))


def _unprog_own(xo, hf):
    t = xo.reshape(16, 128, D)
    if hf == 1:
        t = t[::-1]
    return t.reshape(2048, D)


def make_in_maps(x, c, ctx, c_ctx, w_ada, b_ada, norm_g, w_in, q_norm_a, k_norm_a,
                 q_norm_b, k_norm_b, rpb, w_o_a, w_o_b, w_out, cores=range(8)):
    f = np.float32
    ident, bones, rotT = _consts()
    ropes = [_rope_tables(0), _rope_tables(1)]
    lm = {}
    for hf in range(2):
        d = {}
        for l in range(2):
            d.update(_layer_maps(l, hf, w_ada, b_ada, norm_g, w_in, q_norm_a, k_norm_a, q_norm_b,
                                 k_norm_b, rpb, w_o_a, w_o_b, w_out, sfx="_%d" % l))
        lm[hf] = d
    in_maps = []
    for core in cores:
        b, hf = core // 2, core % 2
        m = dict(lm[hf])
        cv = np.stack([c[b].reshape(8, 128).T, c_ctx.reshape(8, 128).T], axis=2)
        selm = np.zeros((128, 2), f)
        selm[:, 1 - hf] = 1.0
        m.update({
            "x_prog": _prog_order(x[b], hf),
            "ctx_in": np.ascontiguousarray(ctx[b]),
            "cvec": np.ascontiguousarray(cv.astype(f)),
            "cos_t": ropes[hf][0], "sin_t": ropes[hf][1],
            "c_ident": ident, "c_bones": bones, "c_rotT": rotT, "c_sel": _sel_const(),
            "selm": selm,
        })
        in_maps.append(m)
    return in_maps


def kernel(x, c, ctx, c_ctx, w_ada, b_ada, norm_g, w_in, q_norm_a, k_norm_a,
           q_norm_b, k_norm_b, rpb, w_o_a, w_o_b, w_out):
    f = np.float32
    arrs = [np.asarray(a, dtype=f) for a in (x, c, ctx, c_ctx, w_ada, b_ada, norm_g, w_in, q_norm_a,
                                              k_norm_a, q_norm_b, k_norm_b, rpb, w_o_a, w_o_b, w_out)]
    in_maps = make_in_maps(*arrs)
    nc = _get_prog("fused")
    res = run_bass_kernel_spmd(nc, in_maps, core_ids=list(range(8)))
    out = np.empty((NB, SEQ, D), f)
    for core in range(8):
        b, hf = core // 2, core % 2
        out[b, hf * 2048:(hf + 1) * 2048] = _unprog_own(np.asarray(res.results[core]["xo_out"]), hf)
    return out
```

```python
import numpy as np
import ml_dtypes
from contextlib import ExitStack

import concourse.bass as bass
import concourse.mybir as mybir
from concourse.bass_utils import run_bass_kernel_spmd

F32 = mybir.dt.float32
BF16 = mybir.dt.bfloat16
AF = mybir.ActivationFunctionType
ALU = mybir.AluOpType

D = 1024
SEQ = 4096
CTX = 256
NB = 4
HD = 64
IN_COLS = 5376
EPS = 1e-6
NEG = -30000.0
NT_OWN = 16
NKC = 34

C_AK, C_AV, C_BK, C_BV, C_AQ, C_BQ, C_AZ, C_BZ, C_GA, C_GB = (
    0, 512, 1024, 1152, 1280, 1792, 2304, 2816, 3328, 4352)


def MM(out, lhsT, rhs, start=True, stop=True):
    return lambda e: e.matmul(out, lhsT=lhsT, rhs=rhs, start=start, stop=stop)


def TR(out, in_, ident):
    return lambda e: e.transpose(out, in_, ident)


def ACT(out, in_, func, bias=0.0, scale=1.0, accum_out=None):
    if accum_out is None:
        return lambda e: e.activation(out=out, in_=in_, func=func, bias=bias, scale=scale)
    return lambda e: e.activation(out=out, in_=in_, func=func, bias=bias, scale=scale,
                                  accum_out=accum_out)


def TS(out, in0, s1, s2, op0, op1=None):
    if op1 is None:
        return lambda e: e.tensor_scalar(out=out, in0=in0, scalar1=s1, scalar2=None, op0=op0)
    return lambda e: e.tensor_scalar(out=out, in0=in0, scalar1=s1, scalar2=s2, op0=op0, op1=op1)


def TT(out, in0, in1, op):
    return lambda e: e.tensor_tensor(out=out, in0=in0, in1=in1, op=op)


def STT(out, in0, scalar, in1, op0, op1):
    return lambda e: e.scalar_tensor_tensor(out=out, in0=in0, scalar=scalar, in1=in1,
                                            op0=op0, op1=op1)


def CP(out, in_):
    return lambda e: e.tensor_copy(out=out, in_=in_)


def RCP(out, in_):
    return lambda e: e.reciprocal(out=out, in_=in_)


def MEMSET(ap, v):
    return lambda e: e.memset(ap, v)


class Prog:
    ENG = ("pe", "act", "dve", "pool", "sp")

    def __init__(self, nc, stack):
        self.nc = nc
        self.stack = stack
        self.ops = {e: [] for e in self.ENG}
        self.state = {}
        self.pend = {e: ([], []) for e in self.ENG}
        self.waited = {e: {} for e in self.ENG}
        self.dsem = {}
        self.nsem = 0
        self.esem = {}
        self.ecnt = {}
        self.pe_sems = set()
        self.new_epoch()

    def _sem(self, name):
        self.nsem += 1
        return self.stack.enter_context(self.nc.semaphore(f"{name}_{self.nsem}"))

    def new_epoch(self):
        for e in ("pe", "act", "dve", "pool"):
            assert not self.pend[e][0] and not self.pend[e][1], f"pending on {e}"
            self.esem[e] = self._sem("e" + e)
            self.ecnt[e] = 0
            if e == "pe":
                self.pe_sems.add(id(self.esem[e]))

    def _st(self, k):
        s = self.state.get(k)
        if s is None:
            s = {"w": None, "r": {}}
            self.state[k] = s
        return s

    def _wait(self, eng, tok):
        if tok is None:
            return
        if tok[0] == "PEND":
            assert tok[1] == eng, f"{eng} waiting on a pending (unsignalled) write of {tok[1]}"
            return
        sem, val = tok
        sid = id(sem)
        if eng == "pe" and sid in self.pe_sems:
            return
        if self.waited[eng].get(sid, 0) >= val:
            return
        self.waited[eng][sid] = val
        self.ops[eng].append(lambda e: e.wait_ge(sem, val))

    def _pre(self, eng, r, w):
        for k in r:
            self._wait(eng, self._st(k)["w"])
        for k in w:
            s = self._st(k)
            self._wait(eng, s["w"])
            for t in s["r"].values():
                self._wait(eng, t)

    def _post(self, tok, r, w):
        sem, val = tok
        for k in w:
            s = self._st(k)
            s["w"] = tok
            s["r"] = {}
        for k in r:
            s = self._st(k)
            s["r"][id(sem)] = tok

    def do(self, eng, fn, r=(), w=(), sig=True):
        r = list(r)
        w = list(w)
        self._pre(eng, r, w)
        if not sig:
            self.ops[eng].append(fn)
            self.pend[eng][0].extend(r)
            self.pend[eng][1].extend(w)
            for k in w:
                self._st(k)["w"] = ("PEND", eng)
            return None
        sem = self.esem[eng]
        self.ecnt[eng] += 1
        val = self.ecnt[eng]
        self.ops[eng].append(lambda e: fn(e).then_inc(sem, 1))
        tok = (sem, val)
        pr, pw = self.pend[eng]
        self._post(tok, r + pr, w + pw)
        self.pend[eng] = ([], [])
        return tok

    def dma(self, q, out, in_, r=(), w=(), key=None):
        r = list(r)
        w = list(w)
        self._pre(q, r, w)
        if key is None:
            key = w[0] if w else r[0]
        if key not in self.dsem:
            self.dsem[key] = [self._sem("d"), 0]
        ent = self.dsem[key]
        ent[1] += 16
        sem, val = ent[0], ent[1]
        self.ops[q].append(lambda e: e.dma_start(out=out, in_=in_).then_inc(sem, 16))
        tok = (sem, val)
        self._post(tok, r, w)
        return tok

    def wait_all(self, eng, keys):
        for k in keys:
            self._wait(eng, self._st(k)["w"])

    def flush(self):
        ops = self.ops
        self.ops = {e: [] for e in self.ENG}
        with self.nc.Block() as block:
            @block.tensor
            def _(e):
                for f in ops["pe"]:
                    f(e)

            @block.scalar
            def _(e):
                for f in ops["act"]:
                    f(e)

            @block.vector
            def _(e):
                for f in ops["dve"]:
                    f(e)

            @block.gpsimd
            def _(e):
                for f in ops["pool"]:
                    f(e)

            @block.sync
            def _(e):
                for f in ops["sp"]:
                    f(e)
        self.new_epoch()


class LayerIO:
    pass


_CAST_RR = [0]


def CAST(P, dst, src, r, w):
    eng = ("pool", "dve", "act")[_CAST_RR[0] % 3]
    _CAST_RR[0] += 1
    if eng == "act":
        P.do("act", ACT(dst, src, AF.Copy), r=r, w=w)
    else:
        P.do(eng, CP(dst, src), r=r, w=w)


def declare_layer_inputs(nc, sfx):
    io = LayerIO()
    dt = nc.dram_tensor
    io.w_ada = dt("w_ada" + sfx, [D, 3 * D], F32, kind="ExternalInput").ap()
    io.b_ada_fm = dt("b_ada_fm" + sfx, [128, 24], F32, kind="ExternalInput").ap()
    io.b_gate = dt("b_gate" + sfx, [1, D], F32, kind="ExternalInput").ap()
    io.norm_g = dt("norm_g" + sfx, [128, 8], F32, kind="ExternalInput").ap()
    io.w_in = dt("w_in" + sfx, [D, IN_COLS], F32, kind="ExternalInput").ap()
    io.gains = dt("gains" + sfx, [128, 4], F32, kind="ExternalInput").ap()
    io.ebias = dt("ebias" + sfx, [3, 8, 128, 640], F32, kind="ExternalInput").ap()
    io.w_o_a = dt("w_o_a" + sfx, [512, D], F32, kind="ExternalInput").ap()
    io.w_o_b = dt("w_o_b" + sfx, [512, D], F32, kind="ExternalInput").ap()
    io.w_out = dt("w_out" + sfx, [D, D], F32, kind="ExternalInput").ap()
    return io


def emit_layer(P, nc, cst, io, x_rows, ctx_src, cvec, cos_t, sin_t, x_dst, ctx_dst,
               update_ctx, dbg=None, uid="", after_group=None):
    ident, blockones, rotT, ones64, ones_f, selT = cst
    def sb(name, shape, dtype):
        return nc.sbuf_tensor("s%s_%s" % (uid, name), shape, dtype)

    def ps(name, shape, dtype):
        return nc.psum_tensor("p%s_%s" % (uid, name), shape, dtype)

    own_groups = [("o%d" % g, g * 512, 4, False) for g in range(4)]
    ctx_group = ("c", 2048, 2, True)
    qgroups = own_groups + ([ctx_group] if update_ctx else [])

    with ExitStack() as LA:
        ent = LA.enter_context
        hxT = ent(sb("hxT", [128, 8, 2304], BF16))
        gatedA = ent(sb("gatedA", [128, 4, 2304], BF16))
        gatedB = ent(sb("gatedB", [128, 4, 2304], BF16))
        modv = ent(sb("modv", [128, 16, 2], F32))
        Amod = ent(sb("Amod", [128, 8, 2], F32))
        gate_bc = ent(sb("gate_bc", [128, 2, D], F32))
        gains = ent(sb("gains", [128, 4], F32))
        ng = ent(sb("ng", [128, 8], F32))

        with ExitStack() as S0:
            e0 = S0.enter_context
            cv = e0(sb("cv", [128, 8, 2], F32))
            sc = e0(sb("sc", [128, 8, 2], F32))
            scb = e0(sb("scb", [128, 2, 8, 128], F32))
            wst2 = e0(sb("wst", [128, 2, 8, 512], F32))
            bfm = e0(sb("bfm", [128, 24], F32))
            bgr = e0(sb("bgr", [1, D], F32))
            mps = e0(ps("mps", [128, 512], F32))
            gps = e0(ps("gps", [128, 2, 512], F32))

            P.dma("sp", cv[:], cvec, w=["cv"])
            P.dma("sp", bfm[:], io.b_ada_fm, w=["bfm"])
            P.dma("sp", bgr[:], io.b_gate, w=["bgr"])
            P.dma("sp", ng[:], io.norm_g, w=["ng"])
            P.dma("sp", gains[:], io.gains, w=["gains"])
            P.do("act", ACT(sc[:], cv[:], AF.Silu), r=["cv"], w=["sc"])
            P.do("dve", TS(gains[:, 0:1], gains[:, 0:1], 0.125, None, ALU.mult), r=["gains"], w=["gains"])
            P.do("dve", TS(gains[:, 2:3], gains[:, 2:3], 0.125, None, ALU.mult), r=["gains"], w=["gains"])
            for v in range(2):
                for k in range(8):
                    P.do("dve", TS(scb[:, v, k, :], ones_f[:], sc[:, k, v:v + 1], None, ALU.mult),
                         r=["sc"], w=["scb"])
            for blk in range(6):
                wb = blk % 2
                wst = wst2[:, wb]
                wkey = ("wst", wb)
                P.dma("sp" if wb == 0 else "act", wst, io.w_ada[:, blk * 512:(blk + 1) * 512].rearrange(
                    "(k p) c -> p k c", p=128), w=[wkey])
                if blk < 4:
                    for mm in range(4):
                        m = blk * 4 + mm
                        for k in range(8):
                            P.do("pe", MM(mps[:, 0:2], wst[:, k, mm * 128:(mm + 1) * 128], sc[:, k, :],
                                          start=(k == 0), stop=(k == 7)),
                                 r=[wkey, "sc"], w=["mps"], sig=(k == 7))
                        P.do("dve", TS(modv[:, m, :], mps[:, 0:2], bfm[:, m:m + 1], None, ALU.add),
                             r=["mps", "bfm"], w=["modv"])
                else:
                    hf = blk - 4
                    for v in range(2):
                        for k in range(8):
                            P.do("pe", MM(gps[:, v, :], scb[:, v, k, :], wst[:, k, :],
                                          start=(k == 0), stop=False),
                                 r=[wkey, "scb"], w=[("gps", v)], sig=False)
                        P.do("pe", MM(gps[:, v, :], ones_f[0:1, :], bgr[0:1, hf * 512:(hf + 1) * 512],
                                      start=False, stop=True), r=["bgr"], w=[("gps", v)])
                        P.do("dve", CP(gate_bc[:, v, hf * 512:(hf + 1) * 512], gps[:, v, :]),
                             r=[("gps", v)], w=["gate_bc"])
            for v in range(2):
                P.do("dve", STT(Amod[:, :, v], modv[:, 8:16, v], 1.0, ng[:], ALU.add, ALU.mult),
                     r=["modv", "ng"], w=["Amod"])
            P.flush()

        with ExitStack() as SB:
            eb_ = SB.enter_context
            KbT = eb_(sb("KbT", [128, 2, NKC * 128], BF16))
            Vb = eb_(sb("Vb", [128, NKC, 258], BF16))
            with ExitStack() as SC:
                ec_ = SC.enter_context
                KaT = ec_(sb("KaT", [128, 4, 18 * 128], BF16))
                Va = ec_(sb("Va", [128, 18, 512], BF16))
                KaTc = ec_(sb("KaTc", [128, 4, 256], BF16))
                Vac = ec_(sb("Vac", [128, 2, 512], BF16))

                with ExitStack() as S1:
                    e1 = S1.enter_context
                    Wak = e1(sb("Wak", [128, 8, 512], BF16))
                    Wav = e1(sb("Wav", [128, 8, 512], BF16))
                    Wbk = e1(sb("Wbk", [128, 8, 2, 128], BF16))
                    Wbv = e1(sb("Wbv", [128, 8, 128], BF16))
                    stg = e1(sb("stg", [128, 1, 8, 128], F32))
                    xs = e1(sb("xs", [128, 2, D], F32))
                    def xn_v(p):
                        return gatedB[:, :, p * D:(p + 1) * D]

                    def hxo_v(p):
                        return lambda k: gatedA[:, k // 2, p * D + (k % 2) * 512:p * D + (k % 2) * 512 + 512]
                    ssq = e1(sb("ssq", [128, 4], F32))
                    lnv = e1(sb("lnv", [128, 4], F32))
                    rstd = e1(sb("rstd", [128, 4], F32))
                    cosb = e1(sb("cosb", [128, 512], F32))
                    sinb = e1(sb("sinb", [128, 512], F32))
                    raw = e1(sb("raw", [128, 1, 512], F32))
                    sq = e1(sb("sq", [128, 1, 512], BF16))
                    lnr = e1(sb("lnr", [128, 1, 512], F32))
                    rsr = lnr
                    kn = e1(sb("kn", [128, 512], BF16))
                    P._pre("dve", [], ["gAscr", "gBscr"] + [("gA", i) for i in range(5)] + [("gB", i) for i in range(5)])
                    P._pre("act", [], ["gAscr", "gBscr"] + [("gA", i) for i in range(5)] + [("gB", i) for i in range(5)])
                    tp = e1(ps("tp", [128, 2, 1024], BF16))
                    pj = e1(ps("pj", [128, 4, 512], F32))
                    aux = e1(ps("aux", [128, 2, 512], F32))

                    def load_w(dst_fn, c0, nblk, tag):
                        for bi in range(nblk):
                            s = 0
                            load_w.cnt += 1
                            P.dma("sp", stg[:, s, :, :], io.w_in[:, c0 + bi * 128:c0 + (bi + 1) * 128]
                                  .rearrange("(k p) c -> p k c", p=128), w=[("stg", s)])
                            dst_fn(bi, s)
                    load_w.cnt = 0

                    def cast_to(dst, tag):
                        def f(bi, s):
                            CAST(P, dst[:, :, bi * 128:(bi + 1) * 128], stg[:, s, :, :], [("stg", s)], [tag])
                        return f
                    load_w(cast_to(Wak, "Wak"), C_AK, 4, "Wak")
                    load_w(cast_to(Wav, "Wav"), C_AV, 4, "Wav")

                    def cast_bk(bi, s):
                        for g in range(2):
                            for half in range(2):
                                CAST(P, Wbk[:, :, g, half * 64:(half + 1) * 64],
                                     stg[:, s, :, g * 64:(g + 1) * 64], [("stg", s)], ["Wbk"])
                    load_w(cast_bk, C_BK, 1, "Wbk")
                    load_w(cast_to(Wbv, "Wbv"), C_BV, 1, "Wbv")

                    tp_slot = [0]
                    P.do("pool", MEMSET(Vb[:], 0.0), w=["Vb"])
                    for oc_ in (0, 128, 256):
                        P.do("pool", MEMSET(Vb[:, :, oc_:oc_ + 1], 1.0), w=["Vb"])

                    def qknorm(src_ps, src_key, N, gain_ap, dst, dst_key, slot, rope=None):
                        aslot = slot
                        slot = 0
                        rs_ = raw[:, slot, :N]
                        P.do("dve", CP(rs_, src_ps), r=[src_key], w=[("raw", slot)])
                        P.do("pool", TT(sq[:, slot, :N], rs_, rs_, ALU.mult), r=[("raw", slot)], w=[("sq", slot)])
                        P.do("pe", MM(aux[:, aslot, :N], blockones[:], sq[:, slot, :N]),
                             r=[("sq", slot)], w=[("aux", aslot)])
                        P.do("act", ACT(lnr[:, slot, :N], aux[:, aslot, :N], AF.Ln, bias=EPS, scale=1.0 / HD),
                             r=[("aux", aslot)], w=[("rsr", slot)])
                        P.do("act", ACT(rsr[:, slot, :N], lnr[:, slot, :N], AF.Exp, scale=-0.5),
                             r=[("rsr", slot)], w=[("rsr", slot)])
                        if rope is None:
                            P.do("dve", STT(dst, rs_, gain_ap, rsr[:, slot, :N], ALU.mult, ALU.mult),
                                 r=[("raw", slot), ("rsr", slot), "gains"], w=[dst_key])
                        else:
                            P.do("dve", STT(kn[:, :N], rs_, gain_ap, rsr[:, slot, :N], ALU.mult, ALU.mult),
                                 r=[("raw", slot), ("rsr", slot), "gains"], w=["kn"])
                            P.do("pe", MM(aux[:, aslot, :N], rotT[:], kn[:, :N]), r=["kn"], w=[("aux", aslot)])
                            t1 = lnr[:, slot, :N]
                            t2 = raw[:, slot, :N]
                            P.do("dve", TT(t1, kn[:, :N], cosb[:, :N], ALU.mult), r=["kn", "cosb"], w=[("rsr", slot)])
                            P.do("dve", TT(t2, aux[:, aslot, :N], sinb[:, :N], ALU.mult),
                                 r=[("aux", aslot), "sinb"], w=[("raw", slot)])
                            P.do("dve", TT(dst, t1, t2, ALU.add), r=[("rsr", slot), ("raw", slot)], w=[dst_key])

                    groups = [("ctx", None, 2, 0, None)]
                    for g in range(4):
                        groups.append(("own", g * 512, 4, 2 + g * 4, g * 4))
                    for g in range(4):
                        groups.append(("oth", 2048 + g * 512, 4, 18 + g * 4, 16 if g == 0 else None))

                    pjc = [0]
                    xsc = [0]

                    def pjslot():
                        s = pjc[0] % 4
                        pjc[0] += 1
                        return s

                    def ginfo(gi):
                        (kind, r0, ntile, kc0, slot0) = groups[gi]
                        p = gi % 2
                        if kind == "own":
                            hdst = lambda k, c0=r0: hxT[:, k, c0:c0 + 512]
                            hkey = ("hxT", r0 // 512)
                        elif kind == "ctx":
                            hdst = lambda k: hxT[:, k, 2048:2304]
                            hkey = ("hxT", 4)
                        else:
                            hdst = hxo_v(p)
                            hkey = ("gAscr", p)
                        return kind, r0, ntile, kc0, slot0, p, hdst, hkey

                    def stageA(gi):
                        kind, r0, ntile, kc0, slot0, p, hdst, hkey = ginfo(gi)
                        xn = xn_v(p)
                        N = ntile * 128
                        v = 1 if kind == "ctx" else 0
                        for t in range(ntile):
                            xb = xsc[0] % 2
                            xsc[0] += 1
                            if kind == "ctx":
                                xsrc, xkeys = ctx_src[t * 128:(t + 1) * 128, :], []
                            else:
                                xsrc, xkeys = x_rows(r0 + t * 128, 128)
                            P.dma("sp", xs[:, xb, :], xsrc, r=xkeys, w=[("xs", xb)], key=("xs", xb))
                            P.do("act", ACT(xn[:, t, :], xs[:, xb, :], AF.Square, accum_out=ssq[:, t:t + 1]),
                                 r=[("xs", xb)], w=[("xn", p, t), "gBscr", ("ssq", t)])
                            P.do("act", ACT(lnv[:, t:t + 1], ssq[:, t:t + 1], AF.Ln, bias=EPS, scale=1.0 / D),
                                 r=[("ssq", t)], w=[("lnv", t)])
                            P.do("act", ACT(rstd[:, t:t + 1], lnv[:, t:t + 1], AF.Exp, scale=-0.5),
                                 r=[("lnv", t)], w=[("rstd", t)])
                            P.do("dve", TS(xn[:, t, :], xs[:, xb, :], rstd[:, t:t + 1], None, ALU.mult),
                                 r=[("xs", xb), ("rstd", t)], w=[("xn", p, t), "gBscr"])
                        for k in range(8):
                            sl = tp_slot[0] % 2
                            tp_slot[0] += 1
                            tps = tp[:, sl, 0:N]
                            for t in range(ntile):
                                P.do("pe", TR(tp[:, sl, t * 128:(t + 1) * 128],
                                              xn[:, t, k * 128:(k + 1) * 128], ident[:]),
                                     r=[("xn", p, t), "gBscr"], w=[("tp", sl)], sig=(t == ntile - 1))
                            P.do("dve", TS(hdst(k), tps, Amod[:, k, v:v + 1], modv[:, k, v:v + 1], ALU.mult, ALU.add),
                                 r=[("tp", sl), "Amod", "modv"], w=[hkey, "gAscr"])

                    def stageB(gi):
                        kind, r0, ntile, kc0, slot0, p, hdst, hkey = ginfo(gi)
                        N = ntile * 128
                        if kind != "ctx":
                            P.dma("sp", cosb[:], cos_t[:, r0:r0 + 512], w=["cosb"])
                            P.dma("sp", sinb[:], sin_t[:, r0:r0 + 512], w=["sinb"])

                        def hx(k):
                            return hdst(k)

                        for g in range(2):
                            s = pjslot()
                            for k in range(8):
                                P.do("pe", MM(pj[:, s, :N], Wbk[:, k, g, :], hx(k), start=(k == 0), stop=(k == 7)),
                                     r=["Wbk", hkey], w=[("pj", s)], sig=(k == 7))
                            qknorm(pj[:, s, :N], ("pj", s), N, gains[:, 3:4],
                                   KbT[:, g, kc0 * 128:kc0 * 128 + N], "KbT", g,
                                   rope=None if kind == "ctx" else True)
                        for t in range(ntile):
                            s = pjslot()
                            for k in range(8):
                                P.do("pe", MM(pj[:, s, 0:128], hx(k)[:, t * 128:(t + 1) * 128], Wbv[:, k, :],
                                              start=(k == 0), stop=(k == 7)),
                                     r=["Wbv", hkey], w=[("pj", s)], sig=(k == 7))
                            P.do("act", ACT(Vb[:, kc0 + t, 64:128], pj[:, s, 0:64], AF.Copy), r=[("pj", s)], w=["Vb"])
                            P.do("dve", CP(Vb[:, kc0 + t, 192:256], pj[:, s, 64:128]), r=[("pj", s)], w=["Vb"])
                        if kind == "ctx":
                            na_tiles = 2
                        elif slot0 is not None:
                            na_tiles = 4 if kind == "own" else 2
                        else:
                            na_tiles = 0
                        if na_tiles:
                            Nn = na_tiles * 128
                            for hp in range(4):
                                s = pjslot()
                                for k in range(8):
                                    P.do("pe", MM(pj[:, s, :Nn], Wak[:, k, hp * 128:(hp + 1) * 128], hx(k)[:, :Nn],
                                                  start=(k == 0), stop=(k == 7)),
                                         r=["Wak", hkey], w=[("pj", s)], sig=(k == 7))
                                if kind == "ctx":
                                    dst = KaTc[:, hp, 0:Nn]
                                    dkey = "KaTc"
                                else:
                                    dst = KaT[:, hp, slot0 * 128:slot0 * 128 + Nn]
                                    dkey = "KaT"
                                qknorm(pj[:, s, :Nn], ("pj", s), Nn, gains[:, 1:2], dst, dkey, hp % 2)
                            for t in range(na_tiles):
                                s = pjslot()
                                for k in range(8):
                                    P.do("pe", MM(pj[:, s, :], hx(k)[:, t * 128:(t + 1) * 128], Wav[:, k, :],
                                                  start=(k == 0), stop=(k == 7)),
                                         r=["Wav", hkey], w=[("pj", s)], sig=(k == 7))
                                if kind == "ctx":
                                    P.do("act", ACT(Vac[:, t, :], pj[:, s, :], AF.Copy), r=[("pj", s)], w=["Vac"])
                                else:
                                    P.do("act", ACT(Va[:, slot0 + t, :], pj[:, s, :], AF.Copy), r=[("pj", s)], w=["Va"])
                    stageA(0)
                    for gi in range(len(groups)):
                        if gi + 1 < len(groups):
                            stageA(gi + 1)
                        stageB(gi)
                    if dbg is not None:
                        dbg(P, "KbT", KbT[:], ["KbT"])
                        dbg(P, "Vb", Vb[:], ["Vb"])
                        dbg(P, "KaT", KaT[:], ["KaT"])
                        dbg(P, "Va", Va[:], ["Va"])
                        dbg(P, "hxT", hxT[:], [("hxT", i) for i in range(5)])
                    P.flush()

                with ExitStack() as S2:
                    e2 = S2.enter_context
                    Wq = e2(sb("Waq", [128, 8, 512], BF16))
                    Wz = e2(sb("Waz", [128, 8, 512], BF16))
                    stg = e2(sb("stg2", [128, 1, 8, 128], F32))
                    Eb2 = e2(sb("Eb", [128, 8, 640], BF16))
                    est = e2(sb("est", [128, 1, 640], F32))
                    raw = e2(sb("raw2", [128, 512], F32))
                    sq = e2(sb("sq2", [128, 512], BF16))
                    lnr = e2(sb("lnr2", [128, 512], F32))
                    rsr = lnr
                    Pt = e2(sb("Pt2", [128, 2, 1024], BF16))
                    ebuf = e2(sb("ebuf2", [128, 512], F32))
                    den = ebuf
                    rden = ebuf
                    tz = ebuf
                    P._pre("dve", [], ["gAscr", ("gAscr", 0), ("gAscr", 1)])
                    P._pre("act", [], ["gBscr"])

                    def Eb(cfg, h):
                        if cfg < 2:
                            return gatedB[:, 2 * cfg + h // 4, (h % 4) * 512:(h % 4 + 1) * 512]
                        return Eb2[:, h, :]
                    Sps = e2(ps("Sps", [128, 2, 1024], F32))
                    PV = e2(ps("PV", [128, 512], F32))
                    SM = e2(ps("SM", [128, 512], F32))
                    qps = e2(ps("qps", [128, 512], F32))
                    aps = e2(ps("aps", [128, 512], F32))

                    cnt = [0]

                    def load_cast(dst, c0, nblk, tag):
                        for bi in range(nblk):
                            s = 0
                            cnt[0] += 1
                            P.dma("sp", stg[:, s, :, :], io.w_in[:, c0 + bi * 128:c0 + (bi + 1) * 128]
                                  .rearrange("(k p) c -> p k c", p=128), w=[("stg", s)])
                            CAST(P, dst[:, :, bi * 128:(bi + 1) * 128], stg[:, s, :, :], [("stg", s)], [tag])
                    load_cast(Wq, C_AQ, 4, "Wq")
                    load_cast(Wz, C_AZ, 4, "Wz")
                    ec = 0
                    for c in range(3):
                        for h in range(8):
                            s = 0
                            ne = 512 if c < 2 else 640
                            P.dma("sp", est[:, s, :], io.ebias[c, h, :, :], w=[("est", s)])
                            P.do("act", ACT(Eb(c, h), est[:, s, 0:ne], AF.Exp), r=[("est", s)], w=[("Eb", c), "gBscr"])

                    units = [(grp, hp) for grp in qgroups for hp in range(4)]
                    qn2 = e2(sb("qn2b", [128, 2, 512], BF16))

                    def prep_steps(ui):
                        (gname, c0, ntile, is_ctx), hp = units[ui]
                        N = ntile * 128
                        gi = 4 if is_ctx else c0 // 512
                        hkey = ("hxT", gi)
                        qb = ui % 2

                        def s0():
                            for k in range(8):
                                P.do("pe", MM(qps[:, :N], Wq[:, k, hp * 128:(hp + 1) * 128], hxT[:, k, c0:c0 + N],
                                              start=(k == 0), stop=(k == 7)),
                                     r=["Wq", hkey], w=["qps"], sig=(k == 7))
                            P.do("dve", CP(raw[:, :N], qps[:, :N]), r=["qps"], w=["raw"])
                            P.do("pool", TT(sq[:, :N], raw[:, :N], raw[:, :N], ALU.mult), r=["raw"], w=["sq"])

                        def s1():
                            P.do("pe", MM(qps[:, :N], blockones[:], sq[:, :N]), r=["sq"], w=["qps"])
                            P.do("act", ACT(lnr[:, :N], qps[:, :N], AF.Ln, bias=EPS, scale=1.0 / HD), r=["qps"], w=["rsr"])
                            P.do("act", ACT(rsr[:, :N], lnr[:, :N], AF.Exp, scale=-0.5), r=["rsr"], w=["rsr"])
                            P.do("dve", STT(qn2[:, qb, :N], raw[:, :N], gains[:, 0:1], rsr[:, :N], ALU.mult, ALU.mult),
                                 r=["raw", "rsr", "gains"], w=[("qn", qb)])
                        return [s0, s1]

                    sbuf_i = [0]
                    for st_ in prep_steps(0):
                        st_()
                    for ui, ((gname, c0, ntile, is_ctx), hp) in enumerate(units):
                        N = ntile * 128
                        gi = 4 if is_ctx else c0 // 512
                        hkey = ("hxT", gi)
                        qb = ui % 2
                        nxt = prep_steps(ui + 1) if ui + 1 < len(units) else []
                        def emit_z(N=N, hp=hp, c0=c0, hkey=hkey):
                            for k in range(8):
                                P.do("pe", MM(aps[:, :N], Wz[:, k, hp * 128:(hp + 1) * 128], hxT[:, k, c0:c0 + N],
                                              start=(k == 0), stop=(k == 7)),
                                     r=["Wz", hkey], w=["aps"], sig=(k == 7))
                        items = [(tq, par) for tq in range(ntile) for par in range(2)]

                        def item_info(it):
                            tq, par = it
                            if is_ctx:
                                return [("c", 0), ("c", 1)], None, 0
                            T = c0 // 128 + tq
                            s0_ = max(T - 2, 0)
                            nwin = 4 if T < 2 else 5
                            return ([("w", s0_ + i) for i in range(nwin)] + [("c", 0), ("c", 1)]), min(T, 2), nwin

                        def emit_S(it, bi_):
                            tq, par = it
                            chunks, cfg, nwin = item_info(it)
                            nch = len(chunks)
                            h = hp * 2 + par
                            hb = par * 64
                            for ci, (ck, cs) in enumerate(chunks):
                                lhs = (KaT[hb:hb + 64, hp, cs * 128:(cs + 1) * 128] if ck == "w"
                                       else KaTc[hb:hb + 64, hp, cs * 128:(cs + 1) * 128])
                                P.do("pe", MM(Sps[:, bi_, ci * 128:(ci + 1) * 128], lhs,
                                              qn2[hb:hb + 64, qb, tq * 128:(tq + 1) * 128]),
                                     r=["KaT", "KaTc", ("qn", qb)], w=[("S", bi_)], sig=(ci == nch - 1))
                            P.do("act", ACT(Pt[:, bi_, 0:nch * 128], Sps[:, bi_, 0:nch * 128], AF.Exp),
                                 r=[("S", bi_)], w=[("Pt", bi_)])
                            if not is_ctx:
                                P.do("dve", TT(Pt[:, bi_, 0:nwin * 128], Pt[:, bi_, 0:nwin * 128], Eb(cfg, h), ALU.mult),
                                     r=[("Pt", bi_), ("Eb", cfg), "gBscr"], w=[("Pt", bi_)])

                        def emit_PV(it, bi_):
                            tq, par = it
                            chunks, cfg, nwin = item_info(it)
                            nch = len(chunks)
                            h = hp * 2 + par
                            hb = par * 64
                            for ci, (ck, cs) in enumerate(chunks):
                                vv = (Va[:, cs, h * 64:(h + 1) * 64] if ck == "w" else Vac[:, cs, h * 64:(h + 1) * 64])
                                P.do("pe", MM(PV[hb:hb + 64, tq * 128:(tq + 1) * 128], vv,
                                              Pt[:, bi_, ci * 128:(ci + 1) * 128],
                                              start=(ci == 0), stop=(ci == nch - 1)),
                                     r=["Va", "Vac", ("Pt", bi_)], w=["PV"], sig=(ci == nch - 1))
                            for ci in range(nch):
                                P.do("pe", MM(SM[hb:hb + 64, tq * 128:(tq + 1) * 128], ones64[:],
                                              Pt[:, bi_, ci * 128:(ci + 1) * 128],
                                              start=(ci == 0), stop=(ci == nch - 1)),
                                     r=[("Pt", bi_)], w=["SM"], sig=(ci == nch - 1))

                        base = sbuf_i[0]
                        sbuf_i[0] += len(items)
                        emit_S(items[0], base % 2)
                        nstep = 0
                        for ii, it in enumerate(items):
                            if ii + 1 < len(items):
                                emit_S(items[ii + 1], (base + ii + 1) % 2)
                            emit_PV(it, (base + ii) % 2)
                            if ii == min(len(items) - 1, 3):
                                emit_z()
                            if nstep < len(nxt) and (ii % 3 == 1 or ii == len(items) - 1):
                                nxt[nstep]()
                                nstep += 1
                        while nstep < len(nxt):
                            nxt[nstep]()
                            nstep += 1
                        P.do("act", ACT(ebuf[:, :N], aps[:, :N], AF.Exp, scale=-1.0), r=["aps"], w=["ebuf"])
                        P.do("dve", STT(den[:, :N], ebuf[:, :N], 1.0, SM[:, :N], ALU.add, ALU.mult),
                             r=["ebuf", "SM"], w=["ebuf"])
                        P.do("dve", RCP(rden[:, :N], den[:, :N]), r=["ebuf"], w=["ebuf"])
                        P.do("dve", TT(tz[:, :N], aps[:, :N], rden[:, :N], ALU.mult), r=["aps", "ebuf"], w=["ebuf"])
                        P.do("dve", TT(gatedA[:, hp, c0:c0 + N], PV[:, :N], tz[:, :N], ALU.mult),
                             r=["PV", "ebuf"], w=[("gA", gi)])
                    if dbg is not None:
                        dbg(P, "gatedA", gatedA[:], [("gA", i) for i in range(5)])
                    P.flush()

            with ExitStack() as S3:
                e3 = S3.enter_context
                Wq = e3(sb("Wbq", [128, 8, 512], BF16))
                Wz = e3(sb("Wbz", [128, 8, 512], BF16))
                stg = e3(sb("stg3", [128, 2, 8, 128], F32))
                raw = e3(sb("raw3", [128, 512], F32))
                sq = e3(sb("sq3", [128, 512], BF16))
                lnr = e3(sb("lnr3", [128, 512], F32))
                rsr = lnr
                P._pre("dve", [], ["gBscr"])
                qn = e3(sb("qn3", [128, 512], BF16))
                qr = e3(sb("qr3", [128, 2, 512], BF16))
                cosb = e3(sb("cosb3", [128, 2, 512], F32))
                sinb = e3(sb("sinb3", [128, 2, 512], F32))
                t1 = e3(sb("t13", [128, 512], F32))
                t2 = e3(sb("t23", [128, 512], F32))
                Pt = e3(sb("Pt3", [128, 2, 1024], BF16))
                ebuf = e3(sb("ebuf3", [128, 512], F32))
                den = ebuf
                rden = ebuf
                tz = ebuf
                Sps = e3(ps("Sps3", [128, 2, 1024], F32))
                PVa = e3(ps("PV3", [128, 512], F32))
                PVb = e3(ps("SM3", [128, 512], F32))
                srow = e3(sb("srow3", [128, 512], F32))
                P.do("pool", MEMSET(srow[:], 0.0), w=["srow"])
                qps = e3(ps("qps3", [128, 512], F32))
                aps = e3(ps("aps3", [128, 512], F32))

                cnt = [0]

                def load_cast3(dst, c0, nblk, tag):
                    for bi in range(nblk):
                        s = cnt[0] % 2
                        cnt[0] += 1
                        P.dma("sp" if s == 0 else "act", stg[:, s, :, :], io.w_in[:, c0 + bi * 128:c0 + (bi + 1) * 128]
                              .rearrange("(k p) c -> p k c", p=128), w=[("stg", s)])
                        CAST(P, dst[:, :, bi * 128:(bi + 1) * 128], stg[:, s, :, :], [("stg", s)], [tag])
                load_cast3(Wq, C_BQ, 4, "Wq")
                load_cast3(Wz, C_BZ, 4, "Wz")

                units = [(grp, hp) for grp in qgroups for hp in range(4)]

                def prep_steps(ui):
                    (gname, c0, ntile, is_ctx), hp = units[ui]
                    N = ntile * 128
                    gi = 4 if is_ctx else c0 // 512
                    hkey = ("hxT", gi)
                    qb = ui % 2
                    cb_ = gi % 2
                    qrd = qr[:, qb, :N]

                    def s0():
                        if (not is_ctx) and hp == 0:
                            P.dma("sp", cosb[:, cb_, :], cos_t[:, c0:c0 + 512], w=[("cosb", cb_)])
                            P.dma("sp", sinb[:, cb_, :], sin_t[:, c0:c0 + 512], w=[("sinb", cb_)])
                        for k in range(8):
                            P.do("pe", MM(qps[:, :N], Wq[:, k, hp * 128:(hp + 1) * 128], hxT[:, k, c0:c0 + N],
                                          start=(k == 0), stop=(k == 7)),
                                 r=["Wq", hkey], w=["qps"], sig=(k == 7))
                        P.do("dve", CP(raw[:, :N], qps[:, :N]), r=["qps"], w=["raw"])
                        P.do("pool", TT(sq[:, :N], raw[:, :N], raw[:, :N], ALU.mult), r=["raw"], w=["sq"])

                    def s1():
                        P.do("pe", MM(qps[:, :N], blockones[:], sq[:, :N]), r=["sq"], w=["qps"])
                        P.do("act", ACT(lnr[:, :N], qps[:, :N], AF.Ln, bias=EPS, scale=1.0 / HD), r=["qps"], w=["rsr"])
                        P.do("act", ACT(rsr[:, :N], lnr[:, :N], AF.Exp, scale=-0.5), r=["rsr"], w=["rsr"])
                        if is_ctx:
                            P.do("dve", STT(qrd, raw[:, :N], gains[:, 2:3], rsr[:, :N], ALU.mult, ALU.mult),
                                 r=["raw", "rsr", "gains"], w=[("qr", qb)])
                        else:
                            P.do("dve", STT(qn[:, :N], raw[:, :N], gains[:, 2:3], rsr[:, :N], ALU.mult, ALU.mult),
                                 r=["raw", "rsr", "gains"], w=["qn"])

                    def s2():
                        if not is_ctx:
                            P.do("pe", MM(qps[:, :N], rotT[:], qn[:, :N]), r=["qn"], w=["qps"])
                            P.do("dve", TT(t1[:, :N], qn[:, :N], cosb[:, cb_, :N], ALU.mult),
                                 r=["qn", ("cosb", cb_)], w=["t1"])
                            P.do("dve", TT(t2[:, :N], qps[:, :N], sinb[:, cb_, :N], ALU.mult),
                                 r=["qps", ("sinb", cb_)], w=["t2"])
                            P.do("dve", TT(qrd, t1[:, :N], t2[:, :N], ALU.add), r=["t1", "t2"], w=[("qr", qb)])
                    return [s0, s1, s2]

                sbuf_i = [0]
                for st_ in prep_steps(0):
                    st_()
                for ui, ((gname, c0, ntile, is_ctx), hp) in enumerate(units):
                    N = ntile * 128
                    gi = 4 if is_ctx else c0 // 512
                    hkey = ("hxT", gi)
                    g = hp // 2
                    qb = ui % 2
                    nxt = prep_steps(ui + 1) if ui + 1 < len(units) else []
                    def emit_z(N=N, hp=hp, c0=c0, hkey=hkey):
                        for k in range(8):
                            P.do("pe", MM(aps[:, :N], Wz[:, k, hp * 128:(hp + 1) * 128], hxT[:, k, c0:c0 + N],
                                          start=(k == 0), stop=(k == 7)),
                                 r=["Wz", hkey], w=["aps"], sig=(k == 7))
                    chunks = [0, 1] if is_ctx else list(range(NKC))
                    nblk = len(chunks) // 2
                    items = [(par, bk) for par in range(2) for bk in range(nblk)]

                    def emit_S(it, bi_):
                        par, bk = it
                        hb = par * 64
                        for j in range(2):
                            ch = chunks[bk * 2 + j]
                            P.do("pe", MM(Sps[:, bi_, j * 512:j * 512 + N],
                                          KbT[hb:hb + 64, g, ch * 128:(ch + 1) * 128],
                                          qr[hb:hb + 64, qb, :N]),
                                 r=["KbT", ("qr", qb)], w=[("S", bi_)], sig=(j == 1))
                        if N == 512:
                            P.do("act", ACT(Pt[:, bi_, :], Sps[:, bi_, :], AF.Exp), r=[("S", bi_)], w=[("Pt", bi_)])
                        else:
                            for j in range(2):
                                P.do("act", ACT(Pt[:, bi_, j * 512:j * 512 + N], Sps[:, bi_, j * 512:j * 512 + N], AF.Exp),
                                     r=[("S", bi_)], w=[("Pt", bi_)])

                    def emit_PV(it, bi_):
                        par, bk = it
                        hb = par * 64
                        for j in range(2):
                            ch = chunks[bk * 2 + j]
                            first = (bk == 0 and j == 0)
                            last = (bk == nblk - 1 and j == 1)
                            if par == 0:
                                P.do("pe", MM(PVa[0:65, :N], Vb[:, ch, 64 + 128 * g:64 + 128 * g + 65],
                                              Pt[:, bi_, j * 512:j * 512 + N], start=first, stop=last),
                                     r=["Vb", ("Pt", bi_)], w=["PVa"], sig=(j == 1))
                            else:
                                P.do("pe", MM(PVb[:, :N], Vb[:, ch, 128 * g:128 * g + 128],
                                              Pt[:, bi_, j * 512:j * 512 + N], start=first, stop=last),
                                     r=["Vb", ("Pt", bi_)], w=["PVb"], sig=(j == 1))

                    base = sbuf_i[0]
                    sbuf_i[0] += len(items)
                    emit_S(items[0], base % 2)
                    nstep = 0
                    for ii, it in enumerate(items):
                        if ii + 1 < len(items):
                            emit_S(items[ii + 1], (base + ii + 1) % 2)
                        emit_PV(it, (base + ii) % 2)
                        if ii == min(len(items) - 1, 12):
                            emit_z()
                        if nstep < len(nxt) and (ii % 3 == 1 or ii == len(items) - 1):
                            nxt[nstep]()
                            nstep += 1
                    while nstep < len(nxt):
                        nxt[nstep]()
                        nstep += 1
                    P.do("act", ACT(ebuf[:, :N], aps[:, :N], AF.Exp, scale=-1.0), r=["aps"], w=["ebuf"])
                    P.do("dve", CP(srow[64:65, :N], PVa[64:65, :N]), r=["PVa"], w=["srow"])
                    P.do("dve", CP(srow[0:1, :N], PVb[0:1, :N]), r=["PVb"], w=["srow"])
                    P.do("pe", MM(qps[:, :N], selT[:], srow[:, :N]), r=["srow"], w=["qps"])
                    P.do("dve", STT(den[:, :N], ebuf[:, :N], 1.0, qps[:, :N], ALU.add, ALU.mult),
                         r=["ebuf", "qps"], w=["ebuf"])
                    P.do("dve", RCP(rden[:, :N], den[:, :N]), r=["ebuf"], w=["ebuf"])
                    P.do("dve", TT(tz[:, :N], aps[:, :N], rden[:, :N], ALU.mult), r=["aps", "ebuf"], w=["ebuf"])
                    P.do("dve", TT(gatedB[0:64, hp, c0:c0 + N], PVa[0:64, :N], tz[0:64, :N], ALU.mult),
                         r=["PVa", "ebuf"], w=[("gB", gi)])
                    P.do("dve", TT(gatedB[64:128, hp, c0:c0 + N], PVb[64:128, :N], tz[64:128, :N], ALU.mult),
                         r=["PVb", "ebuf"], w=[("gB", gi)])
                P.flush()

        with ExitStack() as S4:
            e4 = S4.enter_context
            Wga = e4(sb("Wga", [128, 8, D], BF16))
            Wgb = e4(sb("Wgb", [128, 8, D], BF16))
            Woa = e4(sb("Woa", [128, 4, D], BF16))
            Wob = e4(sb("Wob", [128, 4, D], BF16))
            Wo = e4(sb("Wo", [128, 8, D], BF16))
            stg = e4(sb("stg4", [128, 2, 8, 128], F32))
            sga = e4(sb("sga", [128, 512], F32))
            sgb = e4(sb("sgb", [128, 512], F32))
            ta = e4(sb("ta", [128, 512], F32))
            tb = e4(sb("tb", [128, 512], F32))
            mg = e4(sb("mg", [128, 8, 512], BF16))
            xres = e4(sb("xres", [128, 2, D], F32))
            xo = e4(sb("xo", [128, 2, D], F32))
            tmo = e4(sb("tmo", [128, D], F32))
            oa = e4(ps("oa", [128, 512], F32))
            ob = e4(ps("ob", [128, 512], F32))
            ga = e4(ps("ga", [128, 512], F32))
            gb = e4(ps("gb", [128, 512], F32))
            ops_ = e4(ps("ops", [128, 2, 2, 512], F32))

            cnt = [0]

            def load_cast4(dst, src, kch, c0, nblk, tag):
                for bi in range(nblk):
                    s = cnt[0] % 2
                    cnt[0] += 1
                    P.dma("sp" if s == 0 else "act", stg[:, s, 0:kch, :], src[:, c0 + bi * 128:c0 + (bi + 1) * 128]
                          .rearrange("(k p) c -> p k c", p=128), w=[("stg", s)])
                    CAST(P, dst[:, :, bi * 128:(bi + 1) * 128], stg[:, s, 0:kch, :], [("stg", s)], [tag])
            load_cast4(Wga, io.w_in, 8, C_GA, 8, "Wga")
            load_cast4(Wgb, io.w_in, 8, C_GB, 8, "Wgb")
            load_cast4(Woa, io.w_o_a, 4, 0, 8, "Woa")
            load_cast4(Wob, io.w_o_b, 4, 0, 8, "Wob")
            load_cast4(Wo, io.w_out, 8, 0, 8, "Wo")

            oc = [0]
            for (gname, c0, ntile, is_ctx) in qgroups:
                N = ntile * 128
                gi = 4 if is_ctx else c0 // 512
                hkey = ("hxT", gi)
                v = 1 if is_ctx else 0
                for c in range(8):
                    for k in range(8):
                        P.do("pe", MM(ga[:, :N], Wga[:, k, c * 128:(c + 1) * 128], hxT[:, k, c0:c0 + N],
                                      start=(k == 0), stop=(k == 7)),
                             r=["Wga", hkey], w=["ga"], sig=(k == 7))
                    for k in range(8):
                        P.do("pe", MM(gb[:, :N], Wgb[:, k, c * 128:(c + 1) * 128], hxT[:, k, c0:c0 + N],
                                      start=(k == 0), stop=(k == 7)),
                             r=["Wgb", hkey], w=["gb"], sig=(k == 7))
                    for hp in range(4):
                        P.do("pe", MM(oa[:, :N], Woa[:, hp, c * 128:(c + 1) * 128], gatedA[:, hp, c0:c0 + N],
                                      start=(hp == 0), stop=(hp == 3)),
                             r=["Woa", ("gA", gi)], w=["oa"], sig=(hp == 3))
                    for hp in range(4):
                        P.do("pe", MM(ob[:, :N], Wob[:, hp, c * 128:(c + 1) * 128], gatedB[:, hp, c0:c0 + N],
                                      start=(hp == 0), stop=(hp == 3)),
                             r=["Wob", ("gB", gi)], w=["ob"], sig=(hp == 3))
                    P.do("act", ACT(sga[:, :N], ga[:, :N], AF.Sigmoid), r=["ga"], w=["sga"])
                    P.do("act", ACT(sgb[:, :N], gb[:, :N], AF.Sigmoid), r=["gb"], w=["sgb"])
                    P.do("dve", TT(ta[:, :N], oa[:, :N], sga[:, :N], ALU.mult), r=["oa", "sga"], w=["ta"])
                    P.do("dve", TT(tb[:, :N], ob[:, :N], sgb[:, :N], ALU.mult), r=["ob", "sgb"], w=["tb"])
                    P.do("dve", TT(mg[:, c, :N], ta[:, :N], tb[:, :N], ALU.add), r=["ta", "tb"], w=["mg"])
                for t in range(ntile):
                    s = oc[0] % 2
                    oc[0] += 1
                    if is_ctx:
                        rsrc = ctx_src[t * 128:(t + 1) * 128, :]
                        dst = ctx_dst[t * 128:(t + 1) * 128, :]
                    else:
                        rsrc = x_rows(c0 + t * 128, 128)[0]
                        dst = x_dst[c0 + t * 128:c0 + (t + 1) * 128, :]
                    P.dma("sp", xres[:, s, :], rsrc, w=[("xres", s)])
                    for hf in range(2):
                        for c in range(8):
                            P.do("pe", MM(ops_[:, s, hf, :], mg[:, c, t * 128:(t + 1) * 128],
                                          Wo[:, c, hf * 512:(hf + 1) * 512], start=(c == 0), stop=(c == 7)),
                                 r=["mg", "Wo"], w=[("ops", s, hf)], sig=(c == 7))
                    for hf in range(2):
                        P.do("dve", TT(tmo[:, hf * 512:(hf + 1) * 512], ops_[:, s, hf, :],
                                       gate_bc[:, v, hf * 512:(hf + 1) * 512], ALU.mult),
                             r=[("ops", s, hf), "gate_bc"], w=["tmo"])
                    P.do("dve", TT(xo[:, s, :], tmo[:], xres[:, s, :], ALU.add), r=["tmo", ("xres", s)], w=[("xo", s)])
                    P.dma("pool", dst, xo[:, s, :], r=[("xo", s)], key=("xo", s))
                if after_group is not None and not is_ctx:
                    for s in range(2):
                        for t in list(P._st(("xo", s))["r"].values()):
                            P._wait("pool", t)
                    after_group(gi)
            for s in range(2):
                st = P._st(("xo", s))
                for t in list(st["r"].values()):
                    P._wait("pool", t)
            P.flush()


def build_program(mode):
    nc = bass.Bass("TRN2", target_bir_lowering=False)
    dt = nc.dram_tensor
    x_src = dt("x_prog", [SEQ, D], F32, kind="ExternalInput").ap()
    ctx_src = dt("ctx_in", [CTX, D], F32, kind="ExternalInput").ap()
    cvec = dt("cvec", [128, 8, 2], F32, kind="ExternalInput").ap()
    cos_t = dt("cos_t", [128, SEQ], F32, kind="ExternalInput").ap()
    sin_t = dt("sin_t", [128, SEQ], F32, kind="ExternalInput").ap()
    c_ident = dt("c_ident", [128, 128], BF16, kind="ExternalInput").ap()
    c_bones = dt("c_bones", [128, 128], BF16, kind="ExternalInput").ap()
    c_rotT = dt("c_rotT", [128, 128], BF16, kind="ExternalInput").ap()
    c_sel = dt("c_sel", [128, 128], F32, kind="ExternalInput").ap()
    fused = (mode == "fused")
    if fused:
        io0 = declare_layer_inputs(nc, "_0")
        io1 = declare_layer_inputs(nc, "_1")
        selm_in = dt("selm", [128, 2], F32, kind="ExternalInput").ap()
        xmid = dt("xmid", [2048, D], F32).ap()
        ctxmid = dt("ctxmid", [CTX, D], F32).ap()
        xgath = [dt("xgath%d" % i, [1024, D], F32).ap() for i in range(4)]
        xoth = dt("xoth", [2048, D], F32).ap()
    else:
        io = declare_layer_inputs(nc, "")
    x_dst = dt("xo_out", [2048, D], F32, kind="ExternalOutput").ap()
    ctx_dst = dt("ctxo_out", [CTX, D], F32, kind="ExternalOutput").ap() if mode == "layer0" else None

    with ExitStack() as stack:
        ent = stack.enter_context
        P = Prog(nc, stack)
        ident = ent(nc.sbuf_tensor("ident", [128, 128], BF16))
        blockones = ent(nc.sbuf_tensor("blockones", [128, 128], BF16))
        rotT = ent(nc.sbuf_tensor("rotT", [128, 128], BF16))
        ones64 = ent(nc.sbuf_tensor("ones64", [128, 64], BF16))
        ones_f = ent(nc.sbuf_tensor("ones_f", [128, 128], F32))
        selT = ent(nc.sbuf_tensor("selT", [128, 128], F32))
        P.dma("sp", selT[:], c_sel, w=["selT"])
        P.dma("sp", ident[:], c_ident, w=["ident"])
        P.dma("sp", blockones[:], c_bones, w=["blockones"])
        P.dma("sp", rotT[:], c_rotT, w=["rotT"])
        P.do("pool", MEMSET(ones64[:], 1.0), w=["ones64"])
        P.do("pool", MEMSET(ones_f[:], 1.0), w=["ones_f"])
        for e in ("pe", "act", "dve", "pool"):
            P._pre(e, ["ident", "blockones", "rotT", "ones64", "ones_f", "selT"], [])
        P.flush()
        cst = (ident, blockones, rotT, ones64, ones_f, selT)

        def rows_in(r0, n):
            return x_src[r0:r0 + n, :], []

        if not fused:
            emit_layer(P, nc, cst, io, rows_in, ctx_src, cvec, cos_t, sin_t, x_dst, ctx_dst,
                       mode == "layer0")
            return nc

        def cc(i):
            return lambda e: e.collective_compute(
                "AllGather", ALU.bypass, replica_groups=[[0, 1], [2, 3], [4, 5], [6, 7]],
                ins=[xmid[i * 512:(i + 1) * 512, :]], outs=[xgath[i]])

        def after_group(gi):
            P.do("pool", cc(gi), w=[("xgath", gi)])

        emit_layer(P, nc, cst, io0, rows_in, ctx_src, cvec, cos_t, sin_t, xmid, ctxmid, True, uid="a",
                   after_group=after_group)

        with ExitStack() as SX:
            ex = SX.enter_context
            ca = ex(nc.sbuf_tensor("x_ca", [128, 2, D], F32))
            cb = ex(nc.sbuf_tensor("x_cb", [128, 2, D], F32))
            oo = ex(nc.sbuf_tensor("x_oo", [128, 2, D], F32))
            selm = ex(nc.sbuf_tensor("x_selm", [128, 2], F32))
            P.dma("sp", selm[:], selm_in, w=["selm"])
            for U in range(16):
                s_ = U % 2
                pt = 15 - U
                ci, cj = pt // 4, pt % 4
                P.dma("sp", ca[:, s_, :], xgath[ci][cj * 128:(cj + 1) * 128, :], r=[("xgath", ci)],
                      w=[("ca", s_)], key=("ca", s_))
                P.dma("sp", cb[:, s_, :], xgath[ci][512 + cj * 128:512 + (cj + 1) * 128, :], r=[("xgath", ci)],
                      w=[("cb", s_)], key=("cb", s_))
                P.do("dve", TS(cb[:, s_, :], cb[:, s_, :], selm[:, 1:2], None, ALU.mult),
                     r=[("cb", s_), "selm"], w=[("cb", s_)])
                P.do("dve", STT(oo[:, s_, :], ca[:, s_, :], selm[:, 0:1], cb[:, s_, :], ALU.mult, ALU.add),
                     r=[("ca", s_), ("cb", s_), "selm"], w=[("oo", s_)])
                P.dma("pool", xoth[U * 128:(U + 1) * 128, :], oo[:, s_, :], r=[("oo", s_)], w=[("xoth", U)],
                      key=("oo", s_))
            for s_ in range(2):
                for t in list(P._st(("oo", s_))["r"].values()):
                    P._wait("pool", t)
            P.flush()

        def rows_mid(r0, n):
            if r0 < 2048:
                return xmid[r0:r0 + n, :], []
            U0 = (r0 - 2048) // 128
            return xoth[r0 - 2048:r0 - 2048 + n, :], [("xoth", U0 + i) for i in range(n // 128)]

        emit_layer(P, nc, cst, io1, rows_mid, ctxmid, cvec, cos_t, sin_t, x_dst, None, False, uid="b")
    return nc


def _tile_order(hf):
    return list(range(32)) if hf == 0 else list(range(31, -1, -1))


def _rope_tables(hf):
    gl = _tile_order(hf)
    t = np.concatenate([np.arange(g * 128, (g + 1) * 128) for g in gl]).astype(np.int32)
    pos_row = (t // 64).astype(np.float32)
    pos_col = (t % 64).astype(np.float32)
    half = 16
    inv = (1.0 / (np.float32(10000.0) ** (np.arange(half, dtype=np.float32) / np.float32(half)))).astype(np.float32)
    ar = pos_row[:, None] * inv[None, :]
    ac = pos_col[:, None] * inv[None, :]
    cos64 = np.concatenate([np.cos(ar), np.cos(ar), np.cos(ac), np.cos(ac)], axis=1).astype(np.float32)
    sin64 = np.concatenate([np.sin(ar), np.sin(ar), np.sin(ac), np.sin(ac)], axis=1).astype(np.float32)
    cos_t = np.ascontiguousarray(np.tile(cos64.T, (2, 1)))
    sin_t = np.ascontiguousarray(np.tile(sin64.T, (2, 1)))
    return cos_t, sin_t


def _consts():
    ident = np.eye(128, dtype=np.float32)
    bones = np.zeros((128, 128), np.float32)
    bones[:64, :64] = 1.0
    bones[64:, 64:] = 1.0
    R = np.zeros((64, 64), np.float32)
    for base in (0, 32):
        for i in range(16):
            R[base + i, base + i + 16] = -1.0
            R[base + 16 + i, base + i] = 1.0
    R2 = np.zeros((128, 128), np.float32)
    R2[:64, :64] = R
    R2[64:, 64:] = R
    bf = ml_dtypes.bfloat16
    return ident.astype(bf), bones.astype(bf), np.ascontiguousarray(R2.T).astype(bf)


def _sel_const():
    sel = np.zeros((128, 128), np.float32)
    sel[64, 0:64] = 1.0
    sel[0, 64:128] = 1.0
    return sel


def _ebias(rpb_l, hf):
    out = np.full((3, 8, 128, 5, 128), NEG, np.float32)
    kk = np.arange(128)
    a = kk // 64
    kc = kk % 64
    qq = np.arange(128)
    b = qq // 64
    qc = qq % 64
    cs = np.clip(qc - 8, 0, 48)
    colvalid = (kc[:, None] >= cs[None, :]) & (kc[:, None] < cs[None, :] + 16)
    co = kc[:, None] - qc[None, :] + 15
    for c in range(3):
        T = c
        s0 = max(T - 2, 0)
        j = T if hf == 0 else 31 - T
        qr = 2 * j + b
        rs = np.clip(qr - 4, 0, 56)
        for i in range(5):
            slot = s0 + i
            p = slot if hf == 0 else 31 - slot
            kr = 2 * p + a
            rowvalid = (kr[:, None] >= rs[None, :]) & (kr[:, None] < rs[None, :] + 8)
            ro = kr[:, None] - qr[None, :] + 7
            valid = rowvalid & colvalid
            roc = np.clip(ro, 0, 14)
            coc = np.clip(co, 0, 30)
            vals = rpb_l[:, roc, coc]
            out[c, :, :, i, :] = np.where(valid[None], vals, np.float32(NEG))
    return np.ascontiguousarray(out.reshape(3, 8, 128, 640))


def _layer_maps(l, hf, w_ada, b_ada, norm_g, w_in, q_norm_a, k_norm_a, q_norm_b, k_norm_b,
                rpb, w_o_a, w_o_b, w_out, sfx=""):
    f = np.float32
    gains = np.stack([np.tile(q_norm_a[l], 2), np.tile(k_norm_a[l], 2),
                      np.tile(q_norm_b[l], 2), np.tile(k_norm_b[l], 2)], axis=1).astype(f)
    return {
        "w_ada" + sfx: np.ascontiguousarray(w_ada[l]),
        "b_ada_fm" + sfx: np.ascontiguousarray(b_ada[l].reshape(24, 128).T),
        "b_gate" + sfx: np.ascontiguousarray(b_ada[l][None, 2048:3072]),
        "norm_g" + sfx: np.ascontiguousarray(norm_g[l].reshape(8, 128).T),
        "w_in" + sfx: np.ascontiguousarray(w_in[l]),
        "gains" + sfx: np.ascontiguousarray(gains),
        "ebias" + sfx: _ebias(rpb[l], hf),
        "w_o_a" + sfx: np.ascontiguousarray(w_o_a[l]),
        "w_o_b" + sfx: np.ascontiguousarray(w_o_b[l]),
        "w_out" + sfx: np.ascontiguousarray(w_out[l]),
    }


_PROG_CACHE = {}


def _get_prog(mode):
    if mode not in _PROG_CACHE:
        _PROG_CACHE[mode] = build_program(mode)
    return _PROG_CACHE[mode]


def _prog_order(xb, hf):
    t = xb.reshape(32, 128, D)
    if hf == 1:
        t = t[::-1]
    return np.ascontiguousarray(t.reshape(SEQ, D))


def _unprog_own(xo, hf):
    t = xo.reshape(16, 128, D)
    if hf == 1:
        t = t[::-1]
    return t.reshape(2048, D)


def make_in_maps(x, c, ctx, c_ctx, w_ada, b_ada, norm_g, w_in, q_norm_a, k_norm_a,
                 q_norm_b, k_norm_b, rpb, w_o_a, w_o_b, w_out, cores=range(8)):
    f = np.float32
    ident, bones, rotT = _consts()
    ropes = [_rope_tables(0), _rope_tables(1)]
    lm = {}
    for hf in range(2):
        d = {}
        for l in range(2):
            d.update(_layer_maps(l, hf, w_ada, b_ada, norm_g, w_in, q_norm_a, k_norm_a, q_norm_b,
                                 k_norm_b, rpb, w_o_a, w_o_b, w_out, sfx="_%d" % l))
        lm[hf] = d
    in_maps = []
    for core in cores:
        b, hf = core // 2, core % 2
        m = dict(lm[hf])
        cv = np.stack([c[b].reshape(8, 128).T, c_ctx.reshape(8, 128).T], axis=2)
        selm = np.zeros((128, 2), f)
        selm[:, 1 - hf] = 1.0
        m.update({
            "x_prog": _prog_order(x[b], hf),
            "ctx_in": np.ascontiguousarray(ctx[b]),
            "cvec": np.ascontiguousarray(cv.astype(f)),
            "cos_t": ropes[hf][0], "sin_t": ropes[hf][1],
            "c_ident": ident, "c_bones": bones, "c_rotT": rotT, "c_sel": _sel_const(),
            "selm": selm,
        })
        in_maps.append(m)
    return in_maps


def kernel(x, c, ctx, c_ctx, w_ada, b_ada, norm_g, w_in, q_norm_a, k_norm_a,
           q_norm_b, k_norm_b, rpb, w_o_a, w_o_b, w_out):
    f = np.float32
    arrs = [np.asarray(a, dtype=f) for a in (x, c, ctx, c_ctx, w_ada, b_ada, norm_g, w_in, q_norm_a,
                                              k_norm_a, q_norm_b, k_norm_b, rpb, w_o_a, w_o_b, w_out)]
    in_maps = make_in_maps(*arrs)
    nc = _get_prog("fused")
    res = run_bass_kernel_spmd(nc, in_maps, core_ids=list(range(8)))
    out = np.empty((NB, SEQ, D), f)
    for core in range(8):
        b, hf = core // 2, core % 2
        out[b, hf * 2048:(hf + 1) * 2048] = _unprog_own(np.asarray(res.results[core]["xo_out"]), hf)
    return out
```

```python
import numpy as np
import ml_dtypes
from contextlib import ExitStack

import concourse.bass as bass
import concourse.mybir as mybir
from concourse.bass_utils import run_bass_kernel_spmd

F32 = mybir.dt.float32
BF16 = mybir.dt.bfloat16
AF = mybir.ActivationFunctionType
ALU = mybir.AluOpType

D = 1024
SEQ = 4096
CTX = 256
NB = 4
HD = 64
IN_COLS = 5376
EPS = 1e-6
NEG = -30000.0
NT_OWN = 16
NKC = 34

C_AK, C_AV, C_BK, C_BV, C_AQ, C_BQ, C_AZ, C_BZ, C_GA, C_GB = (
    0, 512, 1024, 1152, 1280, 1792, 2304, 2816, 3328, 4352)


def MM(out, lhsT, rhs, start=True, stop=True):
    return lambda e: e.matmul(out, lhsT=lhsT, rhs=rhs, start=start, stop=stop)


def TR(out, in_, ident):
    return lambda e: e.transpose(out, in_, ident)


def ACT(out, in_, func, bias=0.0, scale=1.0, accum_out=None):
    if accum_out is None:
        return lambda e: e.activation(out=out, in_=in_, func=func, bias=bias, scale=scale)
    return lambda e: e.activation(out=out, in_=in_, func=func, bias=bias, scale=scale,
                                  accum_out=accum_out)


def TS(out, in0, s1, s2, op0, op1=None):
    if op1 is None:
        return lambda e: e.tensor_scalar(out=out, in0=in0, scalar1=s1, scalar2=None, op0=op0)
    return lambda e: e.tensor_scalar(out=out, in0=in0, scalar1=s1, scalar2=s2, op0=op0, op1=op1)


def TT(out, in0, in1, op):
    return lambda e: e.tensor_tensor(out=out, in0=in0, in1=in1, op=op)


def STT(out, in0, scalar, in1, op0, op1):
    return lambda e: e.scalar_tensor_tensor(out=out, in0=in0, scalar=scalar, in1=in1,
                                            op0=op0, op1=op1)


def CP(out, in_):
    return lambda e: e.tensor_copy(out=out, in_=in_)


def RCP(out, in_):
    return lambda e: e.reciprocal(out=out, in_=in_)


def MEMSET(ap, v):
    return lambda e: e.memset(ap, v)


class Prog:
    ENG = ("pe", "act", "dve", "pool", "sp")

    def __init__(self, nc, stack):
        self.nc = nc
        self.stack = stack
        self.ops = {e: [] for e in self.ENG}
        self.state = {}
        self.pend = {e: ([], []) for e in self.ENG}
        self.waited = {e: {} for e in self.ENG}
        self.dsem = {}
        self.nsem = 0
        self.esem = {}
        self.ecnt = {}
        self.pe_sems = set()
        self.new_epoch()

    def _sem(self, name):
        self.nsem += 1
        return self.stack.enter_context(self.nc.semaphore(f"{name}_{self.nsem}"))

    def new_epoch(self):
        for e in ("pe", "act", "dve", "pool"):
            assert not self.pend[e][0] and not self.pend[e][1], f"pending on {e}"
            self.esem[e] = self._sem("e" + e)
            self.ecnt[e] = 0
            if e == "pe":
                self.pe_sems.add(id(self.esem[e]))

    def _st(self, k):
        s = self.state.get(k)
        if s is None:
            s = {"w": None, "r": {}}
            self.state[k] = s
        return s

    def _wait(self, eng, tok):
        if tok is None:
            return
        if tok[0] == "PEND":
            assert tok[1] == eng, f"{eng} waiting on a pending (unsignalled) write of {tok[1]}"
            return
        sem, val = tok
        sid = id(sem)
        if eng == "pe" and sid in self.pe_sems:
            return
        if self.waited[eng].get(sid, 0) >= val:
            return
        self.waited[eng][sid] = val
        self.ops[eng].append(lambda e: e.wait_ge(sem, val))

    def _pre(self, eng, r, w):
        for k in r:
            self._wait(eng, self._st(k)["w"])
        for k in w:
            s = self._st(k)
            self._wait(eng, s["w"])
            for t in s["r"].values():
                self._wait(eng, t)

    def _post(self, tok, r, w):
        sem, val = tok
        for k in w:
            s = self._st(k)
            s["w"] = tok
            s["r"] = {}
        for k in r:
            s = self._st(k)
            s["r"][id(sem)] = tok

    def do(self, eng, fn, r=(), w=(), sig=True):
        r = list(r)
        w = list(w)
        self._pre(eng, r, w)
        if not sig:
            self.ops[eng].append(fn)
            self.pend[eng][0].extend(r)
            self.pend[eng][1].extend(w)
            for k in w:
                self._st(k)["w"] = ("PEND", eng)
            return None
        sem = self.esem[eng]
        self.ecnt[eng] += 1
        val = self.ecnt[eng]
        self.ops[eng].append(lambda e: fn(e).then_inc(sem, 1))
        tok = (sem, val)
        pr, pw = self.pend[eng]
        self._post(tok, r + pr, w + pw)
        self.pend[eng] = ([], [])
        return tok

    def dma(self, q, out, in_, r=(), w=(), key=None):
        r = list(r)
        w = list(w)
        self._pre(q, r, w)
        if key is None:
            key = w[0] if w else r[0]
        if key not in self.dsem:
            self.dsem[key] = [self._sem("d"), 0]
        ent = self.dsem[key]
        ent[1] += 16
        sem, val = ent[0], ent[1]
        self.ops[q].append(lambda e: e.dma_start(out=out, in_=in_).then_inc(sem, 16))
        tok = (sem, val)
        self._post(tok, r, w)
        return tok

    def wait_all(self, eng, keys):
        for k in keys:
            self._wait(eng, self._st(k)["w"])

    def flush(self):
        ops = self.ops
        self.ops = {e: [] for e in self.ENG}
        with self.nc.Block() as block:
            @block.tensor
            def _(e):
                for f in ops["pe"]:
                    f(e)

            @block.scalar
            def _(e):
                for f in ops["act"]:
                    f(e)

            @block.vector
            def _(e):
                for f in ops["dve"]:
                    f(e)

            @block.gpsimd
            def _(e):
                for f in ops["pool"]:
                    f(e)

            @block.sync
            def _(e):
                for f in ops["sp"]:
                    f(e)
        self.new_epoch()


class LayerIO:
    pass


_CAST_RR = [0]


def CAST(P, dst, src, r, w):
    eng = ("pool", "dve", "act")[_CAST_RR[0] % 3]
    _CAST_RR[0] += 1
    if eng == "act":
        P.do("act", ACT(dst, src, AF.Copy), r=r, w=w)
    else:
        P.do(eng, CP(dst, src), r=r, w=w)


def declare_layer_inputs(nc, sfx):
    io = LayerIO()
    dt = nc.dram_tensor
    io.w_ada = dt("w_ada" + sfx, [D, 3 * D], F32, kind="ExternalInput").ap()
    io.b_ada_fm = dt("b_ada_fm" + sfx, [128, 24], F32, kind="ExternalInput").ap()
    io.b_gate = dt("b_gate" + sfx, [1, D], F32, kind="ExternalInput").ap()
    io.norm_g = dt("norm_g" + sfx, [128, 8], F32, kind="ExternalInput").ap()
    io.w_in = dt("w_in" + sfx, [D, IN_COLS], F32, kind="ExternalInput").ap()
    io.gains = dt("gains" + sfx, [128, 4], F32, kind="ExternalInput").ap()
    io.ebias = dt("ebias" + sfx, [3, 8, 128, 640], F32, kind="ExternalInput").ap()
    io.w_o_a = dt("w_o_a" + sfx, [512, D], F32, kind="ExternalInput").ap()
    io.w_o_b = dt("w_o_b" + sfx, [512, D], F32, kind="ExternalInput").ap()
    io.w_out = dt("w_out" + sfx, [D, D], F32, kind="ExternalInput").ap()
    return io


def emit_layer(P, nc, cst, io, x_rows, ctx_src, cvec, cos_t, sin_t, x_dst, ctx_dst,
               update_ctx, dbg=None, uid="", after_group=None):
    ident, blockones, rotT, ones64, ones_f, selT = cst
    def sb(name, shape, dtype):
        return nc.sbuf_tensor("s%s_%s" % (uid, name), shape, dtype)

    def ps(name, shape, dtype):
        return nc.psum_tensor("p%s_%s" % (uid, name), shape, dtype)

    own_groups = [("o%d" % g, g * 512, 4, False) for g in range(4)]
    ctx_group = ("c", 2048, 2, True)
    qgroups = own_groups + ([ctx_group] if update_ctx else [])

    with ExitStack() as LA:
        ent = LA.enter_context
        hxT = ent(sb("hxT", [128, 8, 2304], BF16))
        gatedA = ent(sb("gatedA", [128, 4, 2304], BF16))
        gatedB = ent(sb("gatedB", [128, 4, 2304], BF16))
        modv = ent(sb("modv", [128, 16, 2], F32))
        Amod = ent(sb("Amod", [128, 8, 2], F32))
        gate_bc = ent(sb("gate_bc", [128, 2, D], F32))
        gains = ent(sb("gains", [128, 4], F32))
        ng = ent(sb("ng", [128, 8], F32))

        with ExitStack() as S0:
            e0 = S0.enter_context
            cv = e0(sb("cv", [128, 8, 2], F32))
            sc = e0(sb("sc", [128, 8, 2], F32))
            scb = e0(sb("scb", [128, 2, 8, 128], F32))
            wst2 = e0(sb("wst", [128, 2, 8, 512], F32))
            bfm = e0(sb("bfm", [128, 24], F32))
            bgr = e0(sb("bgr", [1, D], F32))
            mps = e0(ps("mps", [128, 512], F32))
            gps = e0(ps("gps", [128, 2, 512], F32))

            P.dma("sp", cv[:], cvec, w=["cv"])
            P.dma("sp", bfm[:], io.b_ada_fm, w=["bfm"])
            P.dma("sp", bgr[:], io.b_gate, w=["bgr"])
            P.dma("sp", ng[:], io.norm_g, w=["ng"])
            P.dma("sp", gains[:], io.gains, w=["gains"])
            P.do("act", ACT(sc[:], cv[:], AF.Silu), r=["cv"], w=["sc"])
            P.do("dve", TS(gains[:, 0:1], gains[:, 0:1], 0.125, None, ALU.mult), r=["gains"], w=["gains"])
            P.do("dve", TS(gains[:, 2:3], gains[:, 2:3], 0.125, None, ALU.mult), r=["gains"], w=["gains"])
            for v in range(2):
                for k in range(8):
                    P.do("dve", TS(scb[:, v, k, :], ones_f[:], sc[:, k, v:v + 1], None, ALU.mult),
                         r=["sc"], w=["scb"])
            for blk in range(6):
                wb = blk % 2
                wst = wst2[:, wb]
                wkey = ("wst", wb)
                P.dma("sp" if wb == 0 else "act", wst, io.w_ada[:, blk * 512:(blk + 1) * 512].rearrange(
                    "(k p) c -> p k c", p=128), w=[wkey])
                if blk < 4:
                    for mm in range(4):
                        m = blk * 4 + mm
                        for k in range(8):
                            P.do("pe", MM(mps[:, 0:2], wst[:, k, mm * 128:(mm + 1) * 128], sc[:, k, :],
                                          start=(k == 0), stop=(k == 7)),
                                 r=[wkey, "sc"], w=["mps"], sig=(k == 7))
                        P.do("dve", TS(modv[:, m, :], mps[:, 0:2], bfm[:, m:m + 1], None, ALU.add),
                             r=["mps", "bfm"], w=["modv"])
                else:
                    hf = blk - 4
                    for v in range(2):
                        for k in range(8):
                            P.do("pe", MM(gps[:, v, :], scb[:, v, k, :], wst[:, k, :],
                                          start=(k == 0), stop=False),
                                 r=[wkey, "scb"], w=[("gps", v)], sig=False)
                        P.do("pe", MM(gps[:, v, :], ones_f[0:1, :], bgr[0:1, hf * 512:(hf + 1) * 512],
                                      start=False, stop=True), r=["bgr"], w=[("gps", v)])
                        P.do("dve", CP(gate_bc[:, v, hf * 512:(hf + 1) * 512], gps[:, v, :]),
                             r=[("gps", v)], w=["gate_bc"])
            for v in range(2):
                P.do("dve", STT(Amod[:, :, v], modv[:, 8:16, v], 1.0, ng[:], ALU.add, ALU.mult),
                     r=["modv", "ng"], w=["Amod"])
            P.flush()

        with ExitStack() as SB:
            eb_ = SB.enter_context
            KbT = eb_(sb("KbT", [128, 2, NKC * 128], BF16))
            Vb = eb_(sb("Vb", [128, NKC, 258], BF16))
            with ExitStack() as SC:
                ec_ = SC.enter_context
                KaT = ec_(sb("KaT", [128, 4, 18 * 128], BF16))
                Va = ec_(sb("Va", [128, 18, 512], BF16))
                KaTc = ec_(sb("KaTc", [128, 4, 256], BF16))
                Vac = ec_(sb("Vac", [128, 2, 512], BF16))

                with ExitStack() as S1:
                    e1 = S1.enter_context
                    Wak = e1(sb("Wak", [128, 8, 512], BF16))
                    Wav = e1(sb("Wav", [128, 8, 512], BF16))
                    Wbk = e1(sb("Wbk", [128, 8, 2, 128], BF16))
                    Wbv = e1(sb("Wbv", [128, 8, 128], BF16))
                    stg = e1(sb("stg", [128, 1, 8, 128], F32))
                    xs = e1(sb("xs", [128, 2, D], F32))
                    def xn_v(p):
                        return gatedB[:, :, p * D:(p + 1) * D]

                    def hxo_v(p):
                        return lambda k: gatedA[:, k // 2, p * D + (k % 2) * 512:p * D + (k % 2) * 512 + 512]
                    ssq = e1(sb("ssq", [128, 4], F32))
                    lnv = e1(sb("lnv", [128, 4], F32))
                    rstd = e1(sb("rstd", [128, 4], F32))
                    cosb = e1(sb("cosb", [128, 512], F32))
                    sinb = e1(sb("sinb", [128, 512], F32))
                    raw = e1(sb("raw", [128, 1, 512], F32))
                    sq = e1(sb("sq", [128, 1, 512], BF16))
                    lnr = e1(sb("lnr", [128, 1, 512], F32))
                    rsr = lnr
                    kn = e1(sb("kn", [128, 512], BF16))
                    P._pre("dve", [], ["gAscr", "gBscr"] + [("gA", i) for i in range(5)] + [("gB", i) for i in range(5)])
                    P._pre("act", [], ["gAscr", "gBscr"] + [("gA", i) for i in range(5)] + [("gB", i) for i in range(5)])
                    tp = e1(ps("tp", [128, 2, 1024], BF16))
                    pj = e1(ps("pj", [128, 4, 512], F32))
                    aux = e1(ps("aux", [128, 2, 512], F32))

                    def load_w(dst_fn, c0, nblk, tag):
                        for bi in range(nblk):
                            s = 0
                            load_w.cnt += 1
                            P.dma("sp", stg[:, s, :, :], io.w_in[:, c0 + bi * 128:c0 + (bi + 1) * 128]
                                  .rearrange("(k p) c -> p k c", p=128), w=[("stg", s)])
                            dst_fn(bi, s)
                    load_w.cnt = 0

                    def cast_to(dst, tag):
                        def f(bi, s):
                            CAST(P, dst[:, :, bi * 128:(bi + 1) * 128], stg[:, s, :, :], [("stg", s)], [tag])
                        return f
                    load_w(cast_to(Wak, "Wak"), C_AK, 4, "Wak")
                    load_w(cast_to(Wav, "Wav"), C_AV, 4, "Wav")

                    def cast_bk(bi, s):
                        for g in range(2):
                            for half in range(2):
                                CAST(P, Wbk[:, :, g, half * 64:(half + 1) * 64],
                                     stg[:, s, :, g * 64:(g + 1) * 64], [("stg", s)], ["Wbk"])
                    load_w(cast_bk, C_BK, 1, "Wbk")
                    load_w(cast_to(Wbv, "Wbv"), C_BV, 1, "Wbv")

                    tp_slot = [0]
                    P.do("pool", MEMSET(Vb[:], 0.0), w=["Vb"])
                    for oc_ in (0, 128, 256):
                        P.do("pool", MEMSET(Vb[:, :, oc_:oc_ + 1], 1.0), w=["Vb"])

                    def qknorm(src_ps, src_key, N, gain_ap, dst, dst_key, slot, rope=None):
                        aslot = slot
                        slot = 0
                        rs_ = raw[:, slot, :N]
                        P.do("dve", CP(rs_, src_ps), r=[src_key], w=[("raw", slot)])
                        P.do("pool", TT(sq[:, slot, :N], rs_, rs_, ALU.mult), r=[("raw", slot)], w=[("sq", slot)])
                        P.do("pe", MM(aux[:, aslot, :N], blockones[:], sq[:, slot, :N]),
                             r=[("sq", slot)], w=[("aux", aslot)])
                        P.do("act", ACT(lnr[:, slot, :N], aux[:, aslot, :N], AF.Ln, bias=EPS, scale=1.0 / HD),
                             r=[("aux", aslot)], w=[("rsr", slot)])
                        P.do("act", ACT(rsr[:, slot, :N], lnr[:, slot, :N], AF.Exp, scale=-0.5),
                             r=[("rsr", slot)], w=[("rsr", slot)])
                        if rope is None:
                            P.do("dve", STT(dst, rs_, gain_ap, rsr[:, slot, :N], ALU.mult, ALU.mult),
                                 r=[("raw", slot), ("rsr", slot), "gains"], w=[dst_key])
                        else:
                            P.do("dve", STT(kn[:, :N], rs_, gain_ap, rsr[:, slot, :N], ALU.mult, ALU.mult),
                                 r=[("raw", slot), ("rsr", slot), "gains"], w=["kn"])
                            P.do("pe", MM(aux[:, aslot, :N], rotT[:], kn[:, :N]), r=["kn"], w=[("aux", aslot)])
                            t1 = lnr[:, slot, :N]
                            t2 = raw[:, slot, :N]
                            P.do("dve", TT(t1, kn[:, :N], cosb[:, :N], ALU.mult), r=["kn", "cosb"], w=[("rsr", slot)])
                            P.do("dve", TT(t2, aux[:, aslot, :N], sinb[:, :N], ALU.mult),
                                 r=[("aux", aslot), "sinb"], w=[("raw", slot)])
                            P.do("dve", TT(dst, t1, t2, ALU.add), r=[("rsr", slot), ("raw", slot)], w=[dst_key])

                    groups = [("ctx", None, 2, 0, None)]
                    for g in range(4):
                        groups.append(("own", g * 512, 4, 2 + g * 4, g * 4))
                    for g in range(4):
                        groups.append(("oth", 2048 + g * 512, 4, 18 + g * 4, 16 if g == 0 else None))

                    pjc = [0]
                    xsc = [0]

                    def pjslot():
                        s = pjc[0] % 4
                        pjc[0] += 1
                        return s

                    def ginfo(gi):
                        (kind, r0, ntile, kc0, slot0) = groups[gi]
                        p = gi % 2
                        if kind == "own":
                            hdst = lambda k, c0=r0: hxT[:, k, c0:c0 + 512]
                            hkey = ("hxT", r0 // 512)
                        elif kind == "ctx":
                            hdst = lambda k: hxT[:, k, 2048:2304]
                            hkey = ("hxT", 4)
                        else:
                            hdst = hxo_v(p)
                            hkey = ("gAscr", p)
                        return kind, r0, ntile, kc0, slot0, p, hdst, hkey

                    def stageA(gi):
                        kind, r0, ntile, kc0, slot0, p, hdst, hkey = ginfo(gi)
                        xn = xn_v(p)
                        N = ntile * 128
                        v = 1 if kind == "ctx" else 0
                        for t in range(ntile):
                            xb = xsc[0] % 2
                            xsc[0] += 1
                            if kind == "ctx":
                                xsrc, xkeys = ctx_src[t * 128:(t + 1) * 128, :], []
                            else:
                                xsrc, xkeys = x_rows(r0 + t * 128, 128)
                            P.dma("sp", xs[:, xb, :], xsrc, r=xkeys, w=[("xs", xb)], key=("xs", xb))
                            P.do("act", ACT(xn[:, t, :], xs[:, xb, :], AF.Square, accum_out=ssq[:, t:t + 1]),
                                 r=[("xs", xb)], w=[("xn", p, t), "gBscr", ("ssq", t)])
                            P.do("act", ACT(lnv[:, t:t + 1], ssq[:, t:t + 1], AF.Ln, bias=EPS, scale=1.0 / D),
                                 r=[("ssq", t)], w=[("lnv", t)])
                            P.do("act", ACT(rstd[:, t:t + 1], lnv[:, t:t + 1], AF.Exp, scale=-0.5),
                                 r=[("lnv", t)], w=[("rstd", t)])
                            P.do("dve", TS(xn[:, t, :], xs[:, xb, :], rstd[:, t:t + 1], None, ALU.mult),
                                 r=[("xs", xb), ("rstd", t)], w=[("xn", p, t), "gBscr"])
                        for k in range(8):
                            sl = tp_slot[0] % 2
                            tp_slot[0] += 1
                            tps = tp[:, sl, 0:N]
                            for t in range(ntile):
                                P.do("pe", TR(tp[:, sl, t * 128:(t + 1) * 128],
                                              xn[:, t, k * 128:(k + 1) * 128], ident[:]),
                                     r=[("xn", p, t), "gBscr"], w=[("tp", sl)], sig=(t == ntile - 1))
                            P.do("dve", TS(hdst(k), tps, Amod[:, k, v:v + 1], modv[:, k, v:v + 1], ALU.mult, ALU.add),
                                 r=[("tp", sl), "Amod", "modv"], w=[hkey, "gAscr"])

                    def stageB(gi):
                        kind, r0, ntile, kc0, slot0, p, hdst, hkey = ginfo(gi)
                        N = ntile * 128
                        if kind != "ctx":
                            P.dma("sp", cosb[:], cos_t[:, r0:r0 + 512], w=["cosb"])
                            P.dma("sp", sinb[:], sin_t[:, r0:r0 + 512], w=["sinb"])

                        def hx(k):
                            return hdst(k)

                        for g in range(2):
                            s = pjslot()
                            for k in range(8):
                                P.do("pe", MM(pj[:, s, :N], Wbk[:, k, g, :], hx(k), start=(k == 0), stop=(k == 7)),
                                     r=["Wbk", hkey], w=[("pj", s)], sig=(k == 7))
                            qknorm(pj[:, s, :N], ("pj", s), N, gains[:, 3:4],
                                   KbT[:, g, kc0 * 128:kc0 * 128 + N], "KbT", g,
                                   rope=None if kind == "ctx" else True)
                        for t in range(ntile):
                            s = pjslot()
                            for k in range(8):
                                P.do("pe", MM(pj[:, s, 0:128], hx(k)[:, t * 128:(t + 1) * 128], Wbv[:, k, :],
                                              start=(k == 0), stop=(k == 7)),
                                     r=["Wbv", hkey], w=[("pj", s)], sig=(k == 7))
                            P.do("act", ACT(Vb[:, kc0 + t, 64:128], pj[:, s, 0:64], AF.Copy), r=[("pj", s)], w=["Vb"])
                            P.do("dve", CP(Vb[:, kc0 + t, 192:256], pj[:, s, 64:128]), r=[("pj", s)], w=["Vb"])
                        if kind == "ctx":
                            na_tiles = 2
                        elif slot0 is not None:
                            na_tiles = 4 if kind == "own" else 2
                        else:
                            na_tiles = 0
                        if na_tiles:
                            Nn = na_tiles * 128
                            for hp in range(4):
                                s = pjslot()
                                for k in range(8):
                                    P.do("pe", MM(pj[:, s, :Nn], Wak[:, k, hp * 128:(hp + 1) * 128], hx(k)[:, :Nn],
                                                  start=(k == 0), stop=(k == 7)),
                                         r=["Wak", hkey], w=[("pj", s)], sig=(k == 7))
                                if kind == "ctx":
                                    dst = KaTc[:, hp, 0:Nn]
                                    dkey = "KaTc"
                                else:
                                    dst = KaT[:, hp, slot0 * 128:slot0 * 128 + Nn]
                                    dkey = "KaT"
                                qknorm(pj[:, s, :Nn], ("pj", s), Nn, gains[:, 1:2], dst, dkey, hp % 2)
                            for t in range(na_tiles):
                                s = pjslot()
                                for k in range(8):
                                    P.do("pe", MM(pj[:, s, :], hx(k)[:, t * 128:(t + 1) * 128], Wav[:, k, :],
                                                  start=(k == 0), stop=(k == 7)),
                                         r=["Wav", hkey], w=[("pj", s)], sig=(k == 7))
                                if kind == "ctx":
                                    P.do("act", ACT(Vac[:, t, :], pj[:, s, :], AF.Copy), r=[("pj", s)], w=["Vac"])
                                else:
                                    P.do("act", ACT(Va[:, slot0 + t, :], pj[:, s, :], AF.Copy), r=[("pj", s)], w=["Va"])
                    stageA(0)
                    for gi in range(len(groups)):
                        if gi + 1 < len(groups):
                            stageA(gi + 1)
                        stageB(gi)
                    if dbg is not None:
                        dbg(P, "KbT", KbT[:], ["KbT"])
                        dbg(P, "Vb", Vb[:], ["Vb"])
                        dbg(P, "KaT", KaT[:], ["KaT"])
                        dbg(P, "Va", Va[:], ["Va"])
                        dbg(P, "hxT", hxT[:], [("hxT", i) for i in range(5)])
                    P.flush()

                with ExitStack() as S2:
                    e2 = S2.enter_context
                    Wq = e2(sb("Waq", [128, 8, 512], BF16))
                    Wz = e2(sb("Waz", [128, 8, 512], BF16))
                    stg = e2(sb("stg2", [128, 1, 8, 128], F32))
                    Eb2 = e2(sb("Eb", [128, 8, 640], BF16))
                    est = e2(sb("est", [128, 1, 640], F32))
                    raw = e2(sb("raw2", [128, 512], F32))
                    sq = e2(sb("sq2", [128, 512], BF16))
                    lnr = e2(sb("lnr2", [128, 512], F32))
                    rsr = lnr
                    Pt = e2(sb("Pt2", [128, 2, 1024], BF16))
                    ebuf = e2(sb("ebuf2", [128, 512], F32))
                    den = ebuf
                    rden = ebuf
                    tz = ebuf
                    P._pre("dve", [], ["gAscr", ("gAscr", 0), ("gAscr", 1)])
                    P._pre("act", [], ["gBscr"])

                    def Eb(cfg, h):
                        if cfg < 2:
                            return gatedB[:, 2 * cfg + h // 4, (h % 4) * 512:(h % 4 + 1) * 512]
                        return Eb2[:, h, :]
                    Sps = e2(ps("Sps", [128, 2, 1024], F32))
                    PV = e2(ps("PV", [128, 512], F32))
                    SM = e2(ps("SM", [128, 512], F32))
                    qps = e2(ps("qps", [128, 512], F32))
                    aps = e2(ps("aps", [128, 512], F32))

                    cnt = [0]

                    def load_cast(dst, c0, nblk, tag):
                        for bi in range(nblk):
                            s = 0
                            cnt[0] += 1
                            P.dma("sp", stg[:, s, :, :], io.w_in[:, c0 + bi * 128:c0 + (bi + 1) * 128]
                                  .rearrange("(k p) c -> p k c", p=128), w=[("stg", s)])
                            CAST(P, dst[:, :, bi * 128:(bi + 1) * 128], stg[:, s, :, :], [("stg", s)], [tag])
                    load_cast(Wq, C_AQ, 4, "Wq")
                    load_cast(Wz, C_AZ, 4, "Wz")
                    ec = 0
                    for c in range(3):
                        for h in range(8):
                            s = 0
                            ne = 512 if c < 2 else 640
                            P.dma("sp", est[:, s, :], io.ebias[c, h, :, :], w=[("est", s)])
                            P.do("act", ACT(Eb(c, h), est[:, s, 0:ne], AF.Exp), r=[("est", s)], w=[("Eb", c), "gBscr"])

                    units = [(grp, hp) for grp in qgroups for hp in range(4)]
                    qn2 = e2(sb("qn2b", [128, 2, 512], BF16))

                    def prep_steps(ui):
                        (gname, c0, ntile, is_ctx), hp = units[ui]
                        N = ntile * 128
                        gi = 4 if is_ctx else c0 // 512
                        hkey = ("hxT", gi)
                        qb = ui % 2

                        def s0():
                            for k in range(8):
                                P.do("pe", MM(qps[:, :N], Wq[:, k, hp * 128:(hp + 1) * 128], hxT[:, k, c0:c0 + N],
                                              start=(k == 0), stop=(k == 7)),
                                     r=["Wq", hkey], w=["qps"], sig=(k == 7))
                            P.do("dve", CP(raw[:, :N], qps[:, :N]), r=["qps"], w=["raw"])
                            P.do("pool", TT(sq[:, :N], raw[:, :N], raw[:, :N], ALU.mult), r=["raw"], w=["sq"])

                        def s1():
                            P.do("pe", MM(qps[:, :N], blockones[:], sq[:, :N]), r=["sq"], w=["qps"])
                            P.do("act", ACT(lnr[:, :N], qps[:, :N], AF.Ln, bias=EPS, scale=1.0 / HD), r=["qps"], w=["rsr"])
                            P.do("act", ACT(rsr[:, :N], lnr[:, :N], AF.Exp, scale=-0.5), r=["rsr"], w=["rsr"])
                            P.do("dve", STT(qn2[:, qb, :N], raw[:, :N], gains[:, 0:1], rsr[:, :N], ALU.mult, ALU.mult),
                                 r=["raw", "rsr", "gains"], w=[("qn", qb)])
                        return [s0, s1]

                    sbuf_i = [0]
                    for st_ in prep_steps(0):
                        st_()
                    for ui, ((gname, c0, ntile, is_ctx), hp) in enumerate(units):
                        N = ntile * 128
                        gi = 4 if is_ctx else c0 // 512
                        hkey = ("hxT", gi)
                        qb = ui % 2
                        nxt = prep_steps(ui + 1) if ui + 1 < len(units) else []
                        def emit_z(N=N, hp=hp, c0=c0, hkey=hkey):
                            for k in range(8):
                                P.do("pe", MM(aps[:, :N], Wz[:, k, hp * 128:(hp + 1) * 128], hxT[:, k, c0:c0 + N],
                                              start=(k == 0), stop=(k == 7)),
                                     r=["Wz", hkey], w=["aps"], sig=(k == 7))
                        items = [(tq, par) for tq in range(ntile) for par in range(2)]

                        def item_info(it):
                            tq, par = it
                            if is_ctx:
                                return [("c", 0), ("c", 1)], None, 0
                            T = c0 // 128 + tq
                            s0_ = max(T - 2, 0)
                            nwin = 4 if T < 2 else 5
                            return ([("w", s0_ + i) for i in range(nwin)] + [("c", 0), ("c", 1)]), min(T, 2), nwin

                        def emit_S(it, bi_):
                            tq, par = it
                            chunks, cfg, nwin = item_info(it)
                            nch = len(chunks)
                            h = hp * 2 + par
                            hb = par * 64
                            for ci, (ck, cs) in enumerate(chunks):
                                lhs = (KaT[hb:hb + 64, hp, cs * 128:(cs + 1) * 128] if ck == "w"
                                       else KaTc[hb:hb + 64, hp, cs * 128:(cs + 1) * 128])
                                P.do("pe", MM(Sps[:, bi_, ci * 128:(ci + 1) * 128], lhs,
                                              qn2[hb:hb + 64, qb, tq * 128:(tq + 1) * 128]),
                                     r=["KaT", "KaTc", ("qn", qb)], w=[("S", bi_)], sig=(ci == nch - 1))
                            P.do("act", ACT(Pt[:, bi_, 0:nch * 128], Sps[:, bi_, 0:nch * 128], AF.Exp),
                                 r=[("S", bi_)], w=[("Pt", bi_)])
                            if not is_ctx:
                                P.do("dve", TT(Pt[:, bi_, 0:nwin * 128], Pt[:, bi_, 0:nwin * 128], Eb(cfg, h), ALU.mult),
                                     r=[("Pt", bi_), ("Eb", cfg), "gBscr"], w=[("Pt", bi_)])

                        def emit_PV(it, bi_):
                            tq, par = it
                            chunks, cfg, nwin = item_info(it)
                            nch = len(chunks)
                            h = hp * 2 + par
                            hb = par * 64
                            for ci, (ck, cs) in enumerate(chunks):
                                vv = (Va[:, cs, h * 64:(h + 1) * 64] if ck == "w" else Vac[:, cs, h * 64:(h + 1) * 64])
                                P.do("pe", MM(PV[hb:hb + 64, tq * 128:(tq + 1) * 128], vv,
                                              Pt[:, bi_, ci * 128:(ci + 1) * 128],
                                              start=(ci == 0), stop=(ci == nch - 1)),
                                     r=["Va", "Vac", ("Pt", bi_)], w=["PV"], sig=(ci == nch - 1))
                            for ci in range(nch):
                                P.do("pe", MM(SM[hb:hb + 64, tq * 128:(tq + 1) * 128], ones64[:],
                                              Pt[:, bi_, ci * 128:(ci + 1) * 128],
                                              start=(ci == 0), stop=(ci == nch - 1)),
                                     r=[("Pt", bi_)], w=["SM"], sig=(ci == nch - 1))

                        base = sbuf_i[0]
                        sbuf_i[0] += len(items)
                        emit_S(items[0], base % 2)
                        nstep = 0
                        for ii, it in enumerate(items):
                            if ii + 1 < len(items):
                                emit_S(items[ii + 1], (base + ii + 1) % 2)
                            emit_PV(it, (base + ii) % 2)
                            if ii == min(len(items) - 1, 3):
                                emit_z()
                            if nstep < len(nxt) and (ii % 3 == 1 or ii == len(items) - 1):
                                nxt[nstep]()
                                nstep += 1
                        while nstep < len(nxt):
                            nxt[nstep]()
                            nstep += 1
                        P.do("act", ACT(ebuf[:, :N], aps[:, :N], AF.Exp, scale=-1.0), r=["aps"], w=["ebuf"])
                        P.do("dve", STT(den[:, :N], ebuf[:, :N], 1.0, SM[:, :N], ALU.add, ALU.mult),
                             r=["ebuf", "SM"], w=["ebuf"])
                        P.do("dve", RCP(rden[:, :N], den[:, :N]), r=["ebuf"], w=["ebuf"])
                        P.do("dve", TT(tz[:, :N], aps[:, :N], rden[:, :N], ALU.mult), r=["aps", "ebuf"], w=["ebuf"])
                        P.do("dve", TT(gatedA[:, hp, c0:c0 + N], PV[:, :N], tz[:, :N], ALU.mult),
                             r=["PV", "ebuf"], w=[("gA", gi)])
                    if dbg is not None:
                        dbg(P, "gatedA", gatedA[:], [("gA", i) for i in range(5)])
                    P.flush()

            with ExitStack() as S3:
                e3 = S3.enter_context
                Wq = e3(sb("Wbq", [128, 8, 512], BF16))
                Wz = e3(sb("Wbz", [128, 8, 512], BF16))
                stg = e3(sb("stg3", [128, 2, 8, 128], F32))
                raw = e3(sb("raw3", [128, 512], F32))
                sq = e3(sb("sq3", [128, 512], BF16))
                lnr = e3(sb("lnr3", [128, 512], F32))
                rsr = lnr
                P._pre("dve", [], ["gBscr"])
                qn = e3(sb("qn3", [128, 512], BF16))
                qr = e3(sb("qr3", [128, 2, 512], BF16))
                cosb = e3(sb("cosb3", [128, 2, 512], F32))
                sinb = e3(sb("sinb3", [128, 2, 512], F32))
                t1 = e3(sb("t13", [128, 512], F32))
                t2 = e3(sb("t23", [128, 512], F32))
                Pt = e3(sb("Pt3", [128, 2, 1024], BF16))
                ebuf = e3(sb("ebuf3", [128, 512], F32))
                den = ebuf
                rden = ebuf
                tz = ebuf
                Sps = e3(ps("Sps3", [128, 2, 1024], F32))
                PVa = e3(ps("PV3", [128, 512], F32))
                PVb = e3(ps("SM3", [128, 512], F32))
                srow = e3(sb("srow3", [128, 512], F32))
                P.do("pool", MEMSET(srow[:], 0.0), w=["srow"])
                qps = e3(ps("qps3", [128, 512], F32))
                aps = e3(ps("aps3", [128, 512], F32))

                cnt = [0]

                def load_cast3(dst, c0, nblk, tag):
                    for bi in range(nblk):
                        s = cnt[0] % 2
                        cnt[0] += 1
                        P.dma("sp" if s == 0 else "act", stg[:, s, :, :], io.w_in[:, c0 + bi * 128:c0 + (bi + 1) * 128]
                              .rearrange("(k p) c -> p k c", p=128), w=[("stg", s)])
                        CAST(P, dst[:, :, bi * 128:(bi + 1) * 128], stg[:, s, :, :], [("stg", s)], [tag])
                load_cast3(Wq, C_BQ, 4, "Wq")
                load_cast3(Wz, C_BZ, 4, "Wz")

                units = [(grp, hp) for grp in qgroups for hp in range(4)]

                def prep_steps(ui):
                    (gname, c0, ntile, is_ctx), hp = units[ui]
                    N = ntile * 128
                    gi = 4 if is_ctx else c0 // 512
                    hkey = ("hxT", gi)
                    qb = ui % 2
                    cb_ = gi % 2
                    qrd = qr[:, qb, :N]

                    def s0():
                        if (not is_ctx) and hp == 0:
                            P.dma("sp", cosb[:, cb_, :], cos_t[:, c0:c0 + 512], w=[("cosb", cb_)])
                            P.dma("sp", sinb[:, cb_, :], sin_t[:, c0:c0 + 512], w=[("sinb", cb_)])
                        for k in range(8):
                            P.do("pe", MM(qps[:, :N], Wq[:, k, hp * 128:(hp + 1) * 128], hxT[:, k, c0:c0 + N],
                                          start=(k == 0), stop=(k == 7)),
                                 r=["Wq", hkey], w=["qps"], sig=(k == 7))
                        P.do("dve", CP(raw[:, :N], qps[:, :N]), r=["qps"], w=["raw"])
                        P.do("pool", TT(sq[:, :N], raw[:, :N], raw[:, :N], ALU.mult), r=["raw"], w=["sq"])

                    def s1():
                        P.do("pe", MM(qps[:, :N], blockones[:], sq[:, :N]), r=["sq"], w=["qps"])
                        P.do("act", ACT(lnr[:, :N], qps[:, :N], AF.Ln, bias=EPS, scale=1.0 / HD), r=["qps"], w=["rsr"])
                        P.do("act", ACT(rsr[:, :N], lnr[:, :N], AF.Exp, scale=-0.5), r=["rsr"], w=["rsr"])
                        if is_ctx:
                            P.do("dve", STT(qrd, raw[:, :N], gains[:, 2:3], rsr[:, :N], ALU.mult, ALU.mult),
                                 r=["raw", "rsr", "gains"], w=[("qr", qb)])
                        else:
                            P.do("dve", STT(qn[:, :N], raw[:, :N], gains[:, 2:3], rsr[:, :N], ALU.mult, ALU.mult),
                                 r=["raw", "rsr", "gains"], w=["qn"])

                    def s2():
                        if not is_ctx:
                            P.do("pe", MM(qps[:, :N], rotT[:], qn[:, :N]), r=["qn"], w=["qps"])
                            P.do("dve", TT(t1[:, :N], qn[:, :N], cosb[:, cb_, :N], ALU.mult),
                                 r=["qn", ("cosb", cb_)], w=["t1"])
                            P.do("dve", TT(t2[:, :N], qps[:, :N], sinb[:, cb_, :N], ALU.mult),
                                 r=["qps", ("sinb", cb_)], w=["t2"])
                            P.do("dve", TT(qrd, t1[:, :N], t2[:, :N], ALU.add), r=["t1", "t2"], w=[("qr", qb)])
                    return [s0, s1, s2]

                sbuf_i = [0]
                for st_ in prep_steps(0):
                    st_()
                for ui, ((gname, c0, ntile, is_ctx), hp) in enumerate(units):
                    N = ntile * 128
                    gi = 4 if is_ctx else c0 // 512
                    hkey = ("hxT", gi)
                    g = hp // 2
                    qb = ui % 2
                    nxt = prep_steps(ui + 1) if ui + 1 < len(units) else []
                    def emit_z(N=N, hp=hp, c0=c0, hkey=hkey):
                        for k in range(8):
                            P.do("pe", MM(aps[:, :N], Wz[:, k, hp * 128:(hp + 1) * 128], hxT[:, k, c0:c0 + N],
                                          start=(k == 0), stop=(k == 7)),
                                 r=["Wz", hkey], w=["aps"], sig=(k == 7))
                    chunks = [0, 1] if is_ctx else list(range(NKC))
                    nblk = len(chunks)
                    items = list(range(nblk))

                    def emit_S(it, bi_):
                        ch = chunks[it]
                        for j in range(2):
                            hb = j * 64
                            P.do("pe", MM(Sps[:, bi_, j * 512:j * 512 + N],
                                          KbT[hb:hb + 64, g, ch * 128:(ch + 1) * 128],
                                          qr[hb:hb + 64, qb, :N]),
                                 r=["KbT", ("qr", qb)], w=[("S", bi_)], sig=(j == 1))
                        if N == 512:
                            P.do("act", ACT(Pt[:, bi_, :], Sps[:, bi_, :], AF.Exp), r=[("S", bi_)], w=[("Pt", bi_)])
                        else:
                            for j in range(2):
                                P.do("act", ACT(Pt[:, bi_, j * 512:j * 512 + N], Sps[:, bi_, j * 512:j * 512 + N], AF.Exp),
                                     r=[("S", bi_)], w=[("Pt", bi_)])

                    def emit_PV(it, bi_):
                        ch = chunks[it]
                        first = (it == 0)
                        last = (it == nblk - 1)
                        for j in range(2):
                            par = j
                            if par == 0:
                                P.do("pe", MM(PVa[0:65, :N], Vb[:, ch, 64 + 128 * g:64 + 128 * g + 65],
                                              Pt[:, bi_, j * 512:j * 512 + N], start=first, stop=last),
                                     r=["Vb", ("Pt", bi_)], w=["PVa"], sig=(j == 1))
                            else:
                                P.do("pe", MM(PVb[:, :N], Vb[:, ch, 128 * g:128 * g + 128],
                                              Pt[:, bi_, j * 512:j * 512 + N], start=first, stop=last),
                                     r=["Vb", ("Pt", bi_)], w=["PVb"], sig=(j == 1))

                    base = sbuf_i[0]
                    sbuf_i[0] += len(items)
                    emit_S(items[0], base % 2)
                    nstep = 0
                    for ii, it in enumerate(items):
                        if ii + 1 < len(items):
                            emit_S(items[ii + 1], (base + ii + 1) % 2)
                        emit_PV(it, (base + ii) % 2)
                        if ii == min(len(items) - 1, 12):
                            emit_z()
                        if nstep < len(nxt) and (ii % 3 == 1 or ii == len(items) - 1):
                            nxt[nstep]()
                            nstep += 1
                    while nstep < len(nxt):
                        nxt[nstep]()
                        nstep += 1
                    P.do("act", ACT(ebuf[:, :N], aps[:, :N], AF.Exp, scale=-1.0), r=["aps"], w=["ebuf"])
                    P.do("dve", CP(srow[64:65, :N], PVa[64:65, :N]), r=["PVa"], w=["srow"])
                    P.do("dve", CP(srow[0:1, :N], PVb[0:1, :N]), r=["PVb"], w=["srow"])
                    P.do("pe", MM(qps[:, :N], selT[:], srow[:, :N]), r=["srow"], w=["qps"])
                    P.do("dve", STT(den[:, :N], ebuf[:, :N], 1.0, qps[:, :N], ALU.add, ALU.mult),
                         r=["ebuf", "qps"], w=["ebuf"])
                    P.do("dve", RCP(rden[:, :N], den[:, :N]), r=["ebuf"], w=["ebuf"])
                    P.do("dve", TT(tz[:, :N], aps[:, :N], rden[:, :N], ALU.mult), r=["aps", "ebuf"], w=["ebuf"])
                    P.do("dve", TT(gatedB[0:64, hp, c0:c0 + N], PVa[0:64, :N], tz[0:64, :N], ALU.mult),
                         r=["PVa", "ebuf"], w=[("gB", gi)])
                    P.do("dve", TT(gatedB[64:128, hp, c0:c0 + N], PVb[64:128, :N], tz[64:128, :N], ALU.mult),
                         r=["PVb", "ebuf"], w=[("gB", gi)])
                P.flush()

        with ExitStack() as S4:
            e4 = S4.enter_context
            Wga = e4(sb("Wga", [128, 8, D], BF16))
            Wgb = e4(sb("Wgb", [128, 8, D], BF16))
            Woa = e4(sb("Woa", [128, 4, D], BF16))
            Wob = e4(sb("Wob", [128, 4, D], BF16))
            Wo = e4(sb("Wo", [128, 8, D], BF16))
            stg = e4(sb("stg4", [128, 2, 8, 128], F32))
            sga = e4(sb("sga", [128, 512], F32))
            sgb = e4(sb("sgb", [128, 512], F32))
            ta = e4(sb("ta", [128, 512], F32))
            tb = e4(sb("tb", [128, 512], F32))
            mg = e4(sb("mg", [128, 8, 512], BF16))
            xres = e4(sb("xres", [128, 2, D], F32))
            xo = e4(sb("xo", [128, 2, D], F32))
            tmo = e4(sb("tmo", [128, D], F32))
            oa = e4(ps("oa", [128, 512], F32))
            ob = e4(ps("ob", [128, 512], F32))
            ga = e4(ps("ga", [128, 512], F32))
            gb = e4(ps("gb", [128, 512], F32))
            ops_ = e4(ps("ops", [128, 2, 2, 512], F32))

            cnt = [0]

            def load_cast4(dst, src, kch, c0, nblk, tag):
                for bi in range(nblk):
                    s = cnt[0] % 2
                    cnt[0] += 1
                    P.dma("sp" if s == 0 else "act", stg[:, s, 0:kch, :], src[:, c0 + bi * 128:c0 + (bi + 1) * 128]
                          .rearrange("(k p) c -> p k c", p=128), w=[("stg", s)])
                    CAST(P, dst[:, :, bi * 128:(bi + 1) * 128], stg[:, s, 0:kch, :], [("stg", s)], [tag])
            load_cast4(Wga, io.w_in, 8, C_GA, 8, "Wga")
            load_cast4(Wgb, io.w_in, 8, C_GB, 8, "Wgb")
            load_cast4(Woa, io.w_o_a, 4, 0, 8, "Woa")
            load_cast4(Wob, io.w_o_b, 4, 0, 8, "Wob")
            load_cast4(Wo, io.w_out, 8, 0, 8, "Wo")

            oc = [0]
            for (gname, c0, ntile, is_ctx) in qgroups:
                N = ntile * 128
                gi = 4 if is_ctx else c0 // 512
                hkey = ("hxT", gi)
                v = 1 if is_ctx else 0
                for c in range(8):
                    for k in range(8):
                        P.do("pe", MM(ga[:, :N], Wga[:, k, c * 128:(c + 1) * 128], hxT[:, k, c0:c0 + N],
                                      start=(k == 0), stop=(k == 7)),
                             r=["Wga", hkey], w=["ga"], sig=(k == 7))
                    for k in range(8):
                        P.do("pe", MM(gb[:, :N], Wgb[:, k, c * 128:(c + 1) * 128], hxT[:, k, c0:c0 + N],
                                      start=(k == 0), stop=(k == 7)),
                             r=["Wgb", hkey], w=["gb"], sig=(k == 7))
                    for hp in range(4):
                        P.do("pe", MM(oa[:, :N], Woa[:, hp, c * 128:(c + 1) * 128], gatedA[:, hp, c0:c0 + N],
                                      start=(hp == 0), stop=(hp == 3)),
                             r=["Woa", ("gA", gi)], w=["oa"], sig=(hp == 3))
                    for hp in range(4):
                        P.do("pe", MM(ob[:, :N], Wob[:, hp, c * 128:(c + 1) * 128], gatedB[:, hp, c0:c0 + N],
                                      start=(hp == 0), stop=(hp == 3)),
                             r=["Wob", ("gB", gi)], w=["ob"], sig=(hp == 3))
                    P.do("act", ACT(sga[:, :N], ga[:, :N], AF.Sigmoid), r=["ga"], w=["sga"])
                    P.do("act", ACT(sgb[:, :N], gb[:, :N], AF.Sigmoid), r=["gb"], w=["sgb"])
                    P.do("dve", TT(ta[:, :N], oa[:, :N], sga[:, :N], ALU.mult), r=["oa", "sga"], w=["ta"])
                    P.do("dve", TT(tb[:, :N], ob[:, :N], sgb[:, :N], ALU.mult), r=["ob", "sgb"], w=["tb"])
                    P.do("dve", TT(mg[:, c, :N], ta[:, :N], tb[:, :N], ALU.add), r=["ta", "tb"], w=["mg"])
                for t in range(ntile):
                    s = oc[0] % 2
                    oc[0] += 1
                    if is_ctx:
                        rsrc = ctx_src[t * 128:(t + 1) * 128, :]
                        dst = ctx_dst[t * 128:(t + 1) * 128, :]
                    else:
                        rsrc = x_rows(c0 + t * 128, 128)[0]
                        dst = x_dst[c0 + t * 128:c0 + (t + 1) * 128, :]
                    P.dma("sp", xres[:, s, :], rsrc, w=[("xres", s)])
                    for hf in range(2):
                        for c in range(8):
                            P.do("pe", MM(ops_[:, s, hf, :], mg[:, c, t * 128:(t + 1) * 128],
                                          Wo[:, c, hf * 512:(hf + 1) * 512], start=(c == 0), stop=(c == 7)),
                                 r=["mg", "Wo"], w=[("ops", s, hf)], sig=(c == 7))
                    for hf in range(2):
                        P.do("dve", TT(tmo[:, hf * 512:(hf + 1) * 512], ops_[:, s, hf, :],
                                       gate_bc[:, v, hf * 512:(hf + 1) * 512], ALU.mult),
                             r=[("ops", s, hf), "gate_bc"], w=["tmo"])
                    P.do("dve", TT(xo[:, s, :], tmo[:], xres[:, s, :], ALU.add), r=["tmo", ("xres", s)], w=[("xo", s)])
                    P.dma("pool", dst, xo[:, s, :], r=[("xo", s)], key=("xo", s))
                if after_group is not None and not is_ctx:
                    for s in range(2):
                        for t in list(P._st(("xo", s))["r"].values()):
                            P._wait("pool", t)
                    after_group(gi)
            for s in range(2):
                st = P._st(("xo", s))
                for t in list(st["r"].values()):
                    P._wait("pool", t)
            P.flush()


def build_program(mode):
    nc = bass.Bass("TRN2", target_bir_lowering=False)
    dt = nc.dram_tensor
    x_src = dt("x_prog", [SEQ, D], F32, kind="ExternalInput").ap()
    ctx_src = dt("ctx_in", [CTX, D], F32, kind="ExternalInput").ap()
    cvec = dt("cvec", [128, 8, 2], F32, kind="ExternalInput").ap()
    cos_t = dt("cos_t", [128, SEQ], F32, kind="ExternalInput").ap()
    sin_t = dt("sin_t", [128, SEQ], F32, kind="ExternalInput").ap()
    c_ident = dt("c_ident", [128, 128], BF16, kind="ExternalInput").ap()
    c_bones = dt("c_bones", [128, 128], BF16, kind="ExternalInput").ap()
    c_rotT = dt("c_rotT", [128, 128], BF16, kind="ExternalInput").ap()
    c_sel = dt("c_sel", [128, 128], F32, kind="ExternalInput").ap()
    fused = (mode == "fused")
    if fused:
        io0 = declare_layer_inputs(nc, "_0")
        io1 = declare_layer_inputs(nc, "_1")
        selm_in = dt("selm", [128, 2], F32, kind="ExternalInput").ap()
        xmid = dt("xmid", [2048, D], F32).ap()
        ctxmid = dt("ctxmid", [CTX, D], F32).ap()
        xgath = [dt("xgath%d" % i, [1024, D], F32).ap() for i in range(4)]
        xoth = dt("xoth", [2048, D], F32).ap()
    else:
        io = declare_layer_inputs(nc, "")
    x_dst = dt("xo_out", [2048, D], F32, kind="ExternalOutput").ap()
    ctx_dst = dt("ctxo_out", [CTX, D], F32, kind="ExternalOutput").ap() if mode == "layer0" else None

    with ExitStack() as stack:
        ent = stack.enter_context
        P = Prog(nc, stack)
        ident = ent(nc.sbuf_tensor("ident", [128, 128], BF16))
        blockones = ent(nc.sbuf_tensor("blockones", [128, 128], BF16))
        rotT = ent(nc.sbuf_tensor("rotT", [128, 128], BF16))
        ones64 = ent(nc.sbuf_tensor("ones64", [128, 64], BF16))
        ones_f = ent(nc.sbuf_tensor("ones_f", [128, 128], F32))
        selT = ent(nc.sbuf_tensor("selT", [128, 128], F32))
        P.dma("sp", selT[:], c_sel, w=["selT"])
        P.dma("sp", ident[:], c_ident, w=["ident"])
        P.dma("sp", blockones[:], c_bones, w=["blockones"])
        P.dma("sp", rotT[:], c_rotT, w=["rotT"])
        P.do("pool", MEMSET(ones64[:], 1.0), w=["ones64"])
        P.do("pool", MEMSET(ones_f[:], 1.0), w=["ones_f"])
        for e in ("pe", "act", "dve", "pool"):
            P._pre(e, ["ident", "blockones", "rotT", "ones64", "ones_f", "selT"], [])
        P.flush()
        cst = (ident, blockones, rotT, ones64, ones_f, selT)

        def rows_in(r0, n):
            return x_src[r0:r0 + n, :], []

        if not fused:
            emit_layer(P, nc, cst, io, rows_in, ctx_src, cvec, cos_t, sin_t, x_dst, ctx_dst,
                       mode == "layer0")
            return nc

        def cc(i):
            return lambda e: e.collective_compute(
                "AllGather", ALU.bypass, replica_groups=[[0, 1], [2, 3], [4, 5], [6, 7]],
                ins=[xmid[i * 512:(i + 1) * 512, :]], outs=[xgath[i]])

        def after_group(gi):
            P.do("pool", cc(gi), w=[("xgath", gi)])

        emit_layer(P, nc, cst, io0, rows_in, ctx_src, cvec, cos_t, sin_t, xmid, ctxmid, True, uid="a",
                   after_group=after_group)

        with ExitStack() as SX:
            ex = SX.enter_context
            ca = ex(nc.sbuf_tensor("x_ca", [128, 2, D], F32))
            cb = ex(nc.sbuf_tensor("x_cb", [128, 2, D], F32))
            oo = ex(nc.sbuf_tensor("x_oo", [128, 2, D], F32))
            selm = ex(nc.sbuf_tensor("x_selm", [128, 2], F32))
            P.dma("sp", selm[:], selm_in, w=["selm"])
            for U in range(16):
                s_ = U % 2
                pt = 15 - U
                ci, cj = pt // 4, pt % 4
                P.dma("sp", ca[:, s_, :], xgath[ci][cj * 128:(cj + 1) * 128, :], r=[("xgath", ci)],
                      w=[("ca", s_)], key=("ca", s_))
                P.dma("sp", cb[:, s_, :], xgath[ci][512 + cj * 128:512 + (cj + 1) * 128, :], r=[("xgath", ci)],
                      w=[("cb", s_)], key=("cb", s_))
                P.do("dve", TS(cb[:, s_, :], cb[:, s_, :], selm[:, 1:2], None, ALU.mult),
                     r=[("cb", s_), "selm"], w=[("cb", s_)])
                P.do("dve", STT(oo[:, s_, :], ca[:, s_, :], selm[:, 0:1], cb[:, s_, :], ALU.mult, ALU.add),
                     r=[("ca", s_), ("cb", s_), "selm"], w=[("oo", s_)])
                P.dma("pool", xoth[U * 128:(U + 1) * 128, :], oo[:, s_, :], r=[("oo", s_)], w=[("xoth", U)],
                      key=("oo", s_))
            for s_ in range(2):
                for t in list(P._st(("oo", s_))["r"].values()):
                    P._wait("pool", t)
            P.flush()

        def rows_mid(r0, n):
            if r0 < 2048:
                return xmid[r0:r0 + n, :], []
            U0 = (r0 - 2048) // 128
            return xoth[r0 - 2048:r0 - 2048 + n, :], [("xoth", U0 + i) for i in range(n // 128)]

        emit_layer(P, nc, cst, io1, rows_mid, ctxmid, cvec, cos_t, sin_t, x_dst, None, False, uid="b")
    return nc


def _tile_order(hf):
    return list(range(32)) if hf == 0 else list(range(31, -1, -1))


def _rope_tables(hf):
    gl = _tile_order(hf)
    t = np.concatenate([np.arange(g * 128, (g + 1) * 128) for g in gl]).astype(np.int32)
    pos_row = (t // 64).astype(np.float32)
    pos_col = (t % 64).astype(np.float32)
    half = 16
    inv = (1.0 / (np.float32(10000.0) ** (np.arange(half, dtype=np.float32) / np.float32(half)))).astype(np.float32)
    ar = pos_row[:, None] * inv[None, :]
    ac = pos_col[:, None] * inv[None, :]
    cos64 = np.concatenate([np.cos(ar), np.cos(ar), np.cos(ac), np.cos(ac)], axis=1).astype(np.float32)
    sin64 = np.concatenate([np.sin(ar), np.sin(ar), np.sin(ac), np.sin(ac)], axis=1).astype(np.float32)
    cos_t = np.ascontiguousarray(np.tile(cos64.T, (2, 1)))
    sin_t = np.ascontiguousarray(np.tile(sin64.T, (2, 1)))
    return cos_t, sin_t


def _consts():
    ident = np.eye(128, dtype=np.float32)
    bones = np.zeros((128, 128), np.float32)
    bones[:64, :64] = 1.0
    bones[64:, 64:] = 1.0
    R = np.zeros((64, 64), np.float32)
    for base in (0, 32):
        for i in range(16):
            R[base + i, base + i + 16] = -1.0
            R[base + 16 + i, base + i] = 1.0
    R2 = np.zeros((128, 128), np.float32)
    R2[:64, :64] = R
    R2[64:, 64:] = R
    bf = ml_dtypes.bfloat16
    return ident.astype(bf), bones.astype(bf), np.ascontiguousarray(R2.T).astype(bf)


def _sel_const():
    sel = np.zeros((128, 128), np.float32)
    sel[64, 0:64] = 1.0
    sel[0, 64:128] = 1.0
    return sel


def _ebias(rpb_l, hf):
    out = np.full((3, 8, 128, 5, 128), NEG, np.float32)
    kk = np.arange(128)
    a = kk // 64
    kc = kk % 64
    qq = np.arange(128)
    b = qq // 64
    qc = qq % 64
    cs = np.clip(qc - 8, 0, 48)
    colvalid = (kc[:, None] >= cs[None, :]) & (kc[:, None] < cs[None, :] + 16)
    co = kc[:, None] - qc[None, :] + 15
    for c in range(3):
        T = c
        s0 = max(T - 2, 0)
        j = T if hf == 0 else 31 - T
        qr = 2 * j + b
        rs = np.clip(qr - 4, 0, 56)
        for i in range(5):
            slot = s0 + i
            p = slot if hf == 0 else 31 - slot
            kr = 2 * p + a
            rowvalid = (kr[:, None] >= rs[None, :]) & (kr[:, None] < rs[None, :] + 8)
            ro = kr[:, None] - qr[None, :] + 7
            valid = rowvalid & colvalid
            roc = np.clip(ro, 0, 14)
            coc = np.clip(co, 0, 30)
            vals = rpb_l[:, roc, coc]
            out[c, :, :, i, :] = np.where(valid[None], vals, np.float32(NEG))
    return np.ascontiguousarray(out.reshape(3, 8, 128, 640))


def _layer_maps(l, hf, w_ada, b_ada, norm_g, w_in, q_norm_a, k_norm_a, q_norm_b, k_norm_b,
                rpb, w_o_a, w_o_b, w_out, sfx=""):
    f = np.float32
    gains = np.stack([np.tile(q_norm_a[l], 2), np.tile(k_norm_a[l], 2),
                      np.tile(q_norm_b[l], 2), np.tile(k_norm_b[l], 2)], axis=1).astype(f)
    return {
        "w_ada" + sfx: np.ascontiguousarray(w_ada[l]),
        "b_ada_fm" + sfx: np.ascontiguousarray(b_ada[l].reshape(24, 128).T),
        "b_gate" + sfx: np.ascontiguousarray(b_ada[l][None, 2048:3072]),
        "norm_g" + sfx: np.ascontiguousarray(norm_g[l].reshape(8, 128).T),
        "w_in" + sfx: np.ascontiguousarray(w_in[l]),
        "gains" + sfx: np.ascontiguousarray(gains),
        "ebias" + sfx: _ebias(rpb[l], hf),
        "w_o_a" + sfx: np.ascontiguousarray(w_o_a[l]),
        "w_o_b" + sfx: np.ascontiguousarray(w_o_b[l]),
        "w_out" + sfx: np.ascontiguousarray(w_out[l]),
    }


_PROG_CACHE = {}


def _get_prog(mode):
    if mode not in _PROG_CACHE:
        _PROG_CACHE[mode] = build_program(mode)
    return _PROG_CACHE[mode]


def _prog_order(xb, hf):
    t = xb.reshape(32, 128, D)
    if hf == 1:
        t = t[::-1]
    return np.ascontiguousarray(t.reshape(SEQ, D))


def _unprog_own(xo, hf):
    t = xo.reshape(16, 128, D)
    if hf == 1:
        t = t[::-1]
    return t.reshape(2048, D)


def make_in_maps(x, c, ctx, c_ctx, w_ada, b_ada, norm_g, w_in, q_norm_a, k_norm_a,
                 q_norm_b, k_norm_b, rpb, w_o_a, w_o_b, w_out, cores=range(8)):
    f = np.float32
    ident, bones, rotT = _consts()
    ropes = [_rope_tables(0), _rope_tables(1)]
    lm = {}
    for hf in range(2):
        d = {}
        for l in range(2):
            d.update(_layer_maps(l, hf, w_ada, b_ada, norm_g, w_in, q_norm_a, k_norm_a, q_norm_b,
                                 k_norm_b, rpb, w_o_a, w_o_b, w_out, sfx="_%d" % l))
        lm[hf] = d
    in_maps = []
    for core in cores:
        b, hf = core // 2, core % 2
        m = dict(lm[hf])
        cv = np.stack([c[b].reshape(8, 128).T, c_ctx.reshape(8, 128).T], axis=2)
        selm = np.zeros((128, 2), f)
        selm[:, 1 - hf] = 1.0
        m.update({
            "x_prog": _prog_order(x[b], hf),
            "ctx_in": np.ascontiguousarray(ctx[b]),
            "cvec": np.ascontiguousarray(cv.astype(f)),
            "cos_t": ropes[hf][0], "sin_t": ropes[hf][1],
            "c_ident": ident, "c_bones": bones, "c_rotT": rotT, "c_sel": _sel_const(),
            "selm": selm,
        })
        in_maps.append(m)
    return in_maps


def kernel(x, c, ctx, c_ctx, w_ada, b_ada, norm_g, w_in, q_norm_a, k_norm_a,
           q_norm_b, k_norm_b, rpb, w_o_a, w_o_b, w_out):
    f = np.float32
    arrs = [np.asarray(a, dtype=f) for a in (x, c, ctx, c_ctx, w_ada, b_ada, norm_g, w_in, q_norm_a,
                                              k_norm_a, q_norm_b, k_norm_b, rpb, w_o_a, w_o_b, w_out)]
    in_maps = make_in_maps(*arrs)
    nc = _get_prog("fused")
    res = run_bass_kernel_spmd(nc, in_maps, core_ids=list(range(8)))
    out = np.empty((NB, SEQ, D), f)
    for core in range(8):
        b, hf = core // 2, core % 2
        out[b, hf * 2048:(hf + 1) * 2048] = _unprog_own(np.asarray(res.results[core]["xo_out"]), hf)
    return out
```
